# Optimizing a Trainium2 kernel written in Bass

```python
import math
import jax, jax.numpy as jnp
from jax import lax
import numpy as np

D_MODEL = 1024
BATCH = 32
SEQ = 2048
DEPTH = 4
DEC_BATCH = 16
DEC_SEQ = 64
PAST_LEN = 2048

CHUNK = 64
N_MIXERS = 2
N_HGRN = (DEPTH + 1) // 2
N_ATTN = DEPTH // 2

HG_EXPAND = 128
HG_HEADS = D_MODEL // HG_EXPAND
HG_DK = HG_EXPAND
HG_DV = D_MODEL // HG_HEADS
GLA_BLOCK = 16

DA_HEAD_DIM = 64
DA_HEADS = D_MODEL // (2 * DA_HEAD_DIM)
Q_BLOCK = 128

EPS = 1e-6
NEG_INF = -1e30
F32 = jnp.float32

kernel_name = "hgrn2_diffattn_streaming_step"


def rms_norm(x, w):
    x32 = x.astype(F32)
    y = x32 * lax.rsqrt(jnp.mean(x32 * x32, axis=-1, keepdims=True) + EPS)
    return y * w.astype(F32)


def hgrn_lower_bounds(logits):
    p = jax.nn.softmax(logits.astype(F32), axis=0)
    return jnp.cumsum(p, axis=0) - p[0]


def _gla_block(S, blk):
    q, k, v, g = blk
    L = q.shape[1]
    b = jnp.cumsum(g, axis=1)
    causal = jnp.tril(jnp.ones((L, L), dtype=bool))[None, :, :, None, None]
    diff = b[:, :, None] - b[:, None, :]
    decay = jnp.exp(jnp.where(causal, diff, -jnp.inf))
    att = jnp.einsum('bthk,bshk,btshk->bhts', q, k, decay)
    o = (jnp.einsum('bthk,bhkv->bthv', q * jnp.exp(b), S)
         + jnp.einsum('bhts,bshv->bthv', att, v))
    b_last = b[:, -1]
    S_new = (jnp.exp(b_last)[..., None] * S
             + jnp.einsum('bshk,bshv->bhkv', k * jnp.exp(b_last[:, None] - b), v))
    return S_new, o


def gla(q, k, v, g, S0, block_len):
    B, T = q.shape[:2]
    nb = T // block_len

    def to_blocks(a):
        return jnp.moveaxis(a.reshape(B, nb, block_len, *a.shape[2:]), 1, 0)

    S, o = lax.scan(_gla_block, S0, (to_blocks(q), to_blocks(k), to_blocks(v), to_blocks(g)))
    o = jnp.moveaxis(o, 0, 1).reshape(B, T, HG_HEADS, HG_DV)
    return o, S


def hgrn_layer(x, S0, norm_w, w_in, lb, onorm_w, w_out, block_len):
    B, T, _ = x.shape
    h = rms_norm(x, norm_w).astype(x.dtype)
    q, fx, i, z = jnp.split(h @ w_in, 4, axis=-1)
    f = lb + (1.0 - lb) * jax.nn.sigmoid(fx.astype(F32))
    g = jnp.log(f)
    k = 1.0 - f
    shp = (B, T, HG_HEADS, HG_DK)
    q32 = q.astype(F32).reshape(shp) * (HG_DK ** -0.5)
    v32 = i.astype(F32).reshape(B, T, HG_HEADS, HG_DV)
    o, S = gla(q32, k.reshape(shp), v32, g.reshape(shp), S0.astype(F32), block_len)
    o = rms_norm(o.reshape(B, T, D_MODEL), onorm_w).astype(x.dtype) * jax.nn.silu(z)
    return x + o @ w_out, S.astype(x.dtype)


def diff_attend(q, k, v, q_pos, k_pos, slopes, lam):
    s = jnp.einsum('bqmhd,bkmhd->bmhqk', q.astype(F32), k.astype(F32)) * (DA_HEAD_DIM ** -0.5)
    dist = jnp.abs(q_pos[:, None] - k_pos[None, :]).astype(F32)
    visible = (k_pos[None, :] // CHUNK) <= (q_pos[:, None] // CHUNK)
    bias = jnp.where(visible[None], -slopes[:, None, None] * dist[None], NEG_INF)
    p = jax.nn.softmax(s + bias, axis=-1)
    a = p[:, 0] - lam * p[:, 1]
    return jnp.einsum('bhqk,bkhe->bqhe', a, v.astype(F32))


def diff_attn_layer(x, k_past, v_past, q_pos, k_pos, norm_w, w_in, qn_w, kn_w, lam_vec, subln_w,
                    w_out, layer_idx, q_block):
    B, T, _ = x.shape
    h = rms_norm(x, norm_w).astype(x.dtype)
    q, k, v, z = jnp.split(h @ w_in, 4, axis=-1)
    q = rms_norm(q.reshape(B, T, 2, DA_HEADS, DA_HEAD_DIM), qn_w).astype(x.dtype)
    k = rms_norm(k.reshape(B, T, 2, DA_HEADS, DA_HEAD_DIM), kn_w).astype(x.dtype)
    v = v.reshape(B, T, DA_HEADS, 2 * DA_HEAD_DIM)
    if k_past is None:
        k_all, v_all = k, v
    else:
        k_all = jnp.concatenate([k_past.astype(k.dtype), k], axis=1)
        v_all = jnp.concatenate([v_past.astype(v.dtype), v], axis=1)
    lam_init = 0.8 - 0.6 * math.exp(-0.3 * layer_idx)
    lv = lam_vec.astype(F32)
    lam = jnp.exp(jnp.sum(lv[0] * lv[1])) - jnp.exp(jnp.sum(lv[2] * lv[3])) + lam_init
    slopes = jnp.exp2(-8.0 * jnp.arange(1, DA_HEADS + 1, dtype=F32) / DA_HEADS)
    nb = T // q_block
    qb = jnp.moveaxis(q.reshape(B, nb, q_block, 2, DA_HEADS, DA_HEAD_DIM), 1, 0)
    pb = q_pos.reshape(nb, q_block)
    o = lax.map(lambda a: diff_attend(a[0], k_all, v_all, a[1], k_pos, slopes, lam), (qb, pb))
    o = jnp.moveaxis(o, 0, 1).reshape(B, T, DA_HEADS, 2 * DA_HEAD_DIM)
    o = rms_norm(o, subln_w) * (1.0 - lam_init)
    o = o.reshape(B, T, D_MODEL).astype(x.dtype) * jax.nn.silu(z)
    return x + o @ w_out, k, v


def setup_inputs(seed: int = 0) -> dict:
    key = jax.random.key(seed)
    ks = jax.random.split(key, 20)
    D = D_MODEL
    sc = D ** -0.5
    nrm = jax.random.normal
    return {
        "x_prompt": nrm(ks[0], (BATCH, SEQ, D), F32),
        "x_sample": nrm(ks[1], (DEC_BATCH, DEC_SEQ, D), F32),
        "cache_k": nrm(ks[2], (N_ATTN, DEC_BATCH, PAST_LEN, 2, DA_HEADS, DA_HEAD_DIM), F32),
        "cache_v": nrm(ks[3], (N_ATTN, DEC_BATCH, PAST_LEN, DA_HEADS, 2 * DA_HEAD_DIM), F32),
        "state_hgrn": 0.5 * nrm(ks[4], (N_HGRN, DEC_BATCH, HG_HEADS, HG_DK, HG_DV), F32),
        "norm_w": 1.0 + 0.02 * nrm(ks[5], (DEPTH, D), F32),
        "hgrn_w_in": sc * nrm(ks[6], (N_HGRN, D, 4 * D), F32),
        "hgrn_lb_logits": 0.1 * nrm(ks[7], (N_HGRN, D), F32),
        "hgrn_onorm_w": 1.0 + 0.02 * nrm(ks[8], (N_HGRN, D), F32),
        "hgrn_w_out": 0.5 * sc * nrm(ks[9], (N_HGRN, D, D), F32),
        "attn_w_in": sc * nrm(ks[10], (N_ATTN, D, 4 * D), F32),
        "attn_q_norm": 1.0 + 0.02 * nrm(ks[11], (N_ATTN, DA_HEAD_DIM), F32),
        "attn_k_norm": 1.0 + 0.02 * nrm(ks[12], (N_ATTN, DA_HEAD_DIM), F32),
        "attn_lambda": 0.1 * nrm(ks[13], (N_ATTN, 4, DA_HEAD_DIM), F32),
        "attn_subln": 1.0 + 0.02 * nrm(ks[14], (N_ATTN, 2 * DA_HEAD_DIM), F32),
        "attn_w_out": 0.5 * sc * nrm(ks[15], (N_ATTN, D, D), F32),
    }


def reference(x_prompt, x_sample, cache_k, cache_v, state_hgrn, norm_w, hgrn_w_in, hgrn_lb_logits,
              hgrn_onorm_w, hgrn_w_out, attn_w_in, attn_q_norm, attn_k_norm, attn_lambda, attn_subln,
              attn_w_out):
    yp, ys = x_prompt, x_sample
    Bp, Tp = yp.shape[0], yp.shape[1]
    Ts = ys.shape[1]
    P = cache_k.shape[2]
    pos_p = jnp.arange(Tp, dtype=jnp.int32)
    pos_sq = P + jnp.arange(Ts, dtype=jnp.int32)
    pos_sk = jnp.arange(P + Ts, dtype=jnp.int32)
    lbs = hgrn_lower_bounds(hgrn_lb_logits)
    kp, vp, ks_, vs_, sp, ss = [], [], [], [], [], []
    for l in range(DEPTH):
        j = l // N_MIXERS
        if l % N_MIXERS == 0:
            S0 = jnp.zeros((Bp, HG_HEADS, HG_DK, HG_DV), yp.dtype)
            yp, s_p = hgrn_layer(yp, S0, norm_w[l], hgrn_w_in[j], lbs[j], hgrn_onorm_w[j],
                                 hgrn_w_out[j], GLA_BLOCK)
            ys, s_s = hgrn_layer(ys, state_hgrn[j], norm_w[l], hgrn_w_in[j], lbs[j], hgrn_onorm_w[j],
                                 hgrn_w_out[j], Ts)
            sp.append(s_p)
            ss.append(s_s)
        else:
            yp, k_p, v_p = diff_attn_layer(yp, None, None, pos_p, pos_p, norm_w[l], attn_w_in[j],
                                           attn_q_norm[j], attn_k_norm[j], attn_lambda[j],
                                           attn_subln[j], attn_w_out[j], l, min(Q_BLOCK, Tp))
            ys, k_s, v_s = diff_attn_layer(ys, cache_k[j], cache_v[j], pos_sq, pos_sk, norm_w[l],
                                           attn_w_in[j], attn_q_norm[j], attn_k_norm[j],
                                           attn_lambda[j], attn_subln[j], attn_w_out[j], l, Ts)
            kp.append(k_p)
            vp.append(v_p)
            ks_.append(k_s)
            vs_.append(v_s)
    return (yp, ys, jnp.stack(kp), jnp.stack(vp), jnp.stack(ks_), jnp.stack(vs_), jnp.stack(sp), jnp.stack(ss))
```

```python
import contextlib
import math
import numpy as np
import concourse.bass as bass
import concourse.mybir as mybir
from concourse.bass_utils import run_bass_kernel_spmd

F32 = mybir.dt.float32
BF16 = mybir.dt.bfloat16
AF = mybir.ActivationFunctionType
ALU = mybir.AluOpType
AX = mybir.AxisListType

NCORES = 8
D = 1024
SEQ = 2048
DEC_SEQ = 64
PAST = 2048
EPS = 1e-6
K_DMA = 8
STOP = 0


class Sched:
    LAT_X = 250.0
    LAT_S = 80.0

    def __init__(self, reorder=True):
        self.ins = []
        self.last_w = {}
        self.readers = {}
        self.cur_bar = None
        self.reorder = reorder
        self.rkeys = set()
        self.wkeys = set()

    def op(self, eng, fn, reads=(), writes=(), dma=False, cost=100.0, bar=False):
        idx = len(self.ins)
        deps = set()
        self.rkeys.update(reads)
        self.wkeys.update(writes)
        for k in reads:
            w = self.last_w.get(k)
            if w is not None:
                deps.add(w)
            if isinstance(k, tuple) and k[0] in ("pb", "pt"):
                for r in self.readers.get(k, ()):
                    if self.ins[r]["eng"] != eng:
                        deps.add(r)
        for k in writes:
            w = self.last_w.get(k)
            if w is not None:
                deps.add(w)
            for r in self.readers.get(k, ()):
                deps.add(r)
        for k in reads:
            self.readers.setdefault(k, []).append(idx)
        for k in writes:
            self.last_w[k] = idx
            self.readers[k] = []
        if self.cur_bar is not None and not bar:
            deps.add(self.cur_bar.get(eng, self.cur_bar["dve"]))
        deps.discard(idx)
        self.ins.append(dict(eng=eng, fn=fn, deps=deps, dma=dma, cost=cost, bar=bar))
        return idx

    def barrier(self, dummies):
        nb = {}
        for e, fn in dummies.items():
            nb[e] = self.op(e, fn, (), [("bar", e)], bar=True)
        self.cur_bar = nb

    def _sched_segment(self, seg):
        import heapq
        ins = self.ins
        if not self.reorder or len(seg) < 3:
            return list(seg)
        segset = set(seg)
        indeg = {}
        succ = {}
        avail = {}
        for i in seg:
            ds = [d for d in ins[i]["deps"] if d in segset]
            indeg[i] = len(ds)
            avail[i] = 0.0
            for d in ds:
                succ.setdefault(d, []).append(i)
        engs = sorted(set(ins[i]["eng"] for i in seg))
        waiting = {e: [] for e in engs}
        ready = {e: [] for e in engs}
        busy = {e: 0.0 for e in engs}
        for i in seg:
            if indeg[i] == 0:
                heapq.heappush(ready[ins[i]["eng"]], i)
        order = []
        t = 0.0
        done = 0
        n = len(seg)
        finish = {}
        while done < n:
            progressed = False
            for e in engs:
                if busy[e] > t:
                    continue
                w = waiting[e]
                while w and w[0][0] <= t:
                    heapq.heappush(ready[e], heapq.heappop(w)[1])
                if not ready[e]:
                    continue
                i = heapq.heappop(ready[e])
                I = ins[i]
                c = I["cost"]
                if I["dma"]:
                    issue = 60.0 if e == "sp" else 700.0
                    busy[e] = t + issue
                    fin = t + c
                else:
                    busy[e] = t + c
                    fin = t + c
                finish[i] = fin
                order.append(i)
                done += 1
                progressed = True
                for sidx in succ.get(i, ()):
                    lat = self.LAT_S if ins[sidx]["eng"] == e and not I["dma"] else self.LAT_X
                    if e == "pe" and ins[sidx]["eng"] == "pe":
                        lat = 0.0
                    a = fin + lat
                    if a > avail[sidx]:
                        avail[sidx] = a
                    indeg[sidx] -= 1
                    if indeg[sidx] == 0:
                        heapq.heappush(waiting[ins[sidx]["eng"]], (avail[sidx], sidx))
            if not progressed:
                cand = []
                for e in engs:
                    if busy[e] > t:
                        cand.append(busy[e])
                    elif waiting[e]:
                        cand.append(max(waiting[e][0][0], busy[e]))
                assert cand, "scheduler stuck"
                t = min(cand)
        return order

    def schedule(self):
        ins = self.ins
        n = len(ins)
        order = []
        seg = []
        last_on = {}
        seg_dmas = []

        def flush():
            nonlocal seg
            o = self._sched_segment(seg)
            for i in o:
                last_on[ins[i]["eng"]] = i
                if ins[i]["dma"]:
                    seg_dmas.append(i)
            order.extend(o)
            seg = []

        i = 0
        while i < n:
            if ins[i]["bar"]:
                flush()
                prev = set(last_on.values()) | set(seg_dmas)
                seg_dmas.clear()
                while i < n and ins[i]["bar"]:
                    ins[i]["deps"] = set(prev)
                    order.append(i)
                    i += 1
                for k in order[-8:]:
                    if ins[k]["bar"]:
                        last_on[ins[k]["eng"]] = k
            else:
                seg.append(i)
                i += 1
        flush()
        assert len(order) == n and len(set(order)) == n
        return order

    def emit(self, nc, engines=("pe", "act", "dve", "pool", "sp")):
        print("keys read but never written:", sorted(map(str, self.rkeys - self.wkeys)))
        order = self.schedule()
        ins = [self.ins[i] for i in order]
        remap = {old: new for new, old in enumerate(order)}
        for I in ins:
            I["deps"] = set(remap[d] for d in I["deps"])
        n = len(ins)
        for i, I in enumerate(ins):
            for d in I["deps"]:
                assert d < i, "schedule violates a dependency"
        has_cons = [False] * n
        for I in ins:
            nd = set()
            for d in I["deps"]:
                Pp = ins[d]
                if (not Pp["dma"]) and Pp["eng"] == I["eng"] and I["eng"] == "pe":
                    continue
                nd.add(d)
            I["deps"] = nd
            for d in nd:
                has_cons[d] = True
        cnt = {e: 0 for e in engines}
        dcnt = {e: 0 for e in engines}
        for i, I in enumerate(ins):
            e = I["eng"]
            if I["dma"]:
                k = dcnt[e]
                dcnt[e] += 1
                I["sig"] = (("d", e, k % K_DMA), 16 * (k // K_DMA + 1))
                I["dma_k"] = k
            elif has_cons[i]:
                cnt[e] += 1
                I["sig"] = (("c", e), cnt[e])
            else:
                I["sig"] = None
        with contextlib.ExitStack() as st:
            sems = {}
            for e in engines:
                sems[("c", e)] = st.enter_context(nc.semaphore("c_" + e))
                if dcnt[e] > 0:
                    for j in range(K_DMA):
                        sems[("d", e, j)] = st.enter_context(nc.semaphore("d_%s_%d" % (e, j)))
            block = st.enter_context(nc.Block())

            def make(e):
                def body(eng):
                    seen = {}
                    for I in ins:
                        if I["eng"] != e:
                            continue
                        waits = {}
                        for d in I["deps"]:
                            s, v = ins[d]["sig"]
                            if waits.get(s, 0) < v:
                                waits[s] = v
                        if I["dma"] and I["dma_k"] >= K_DMA:
                            s, v = I["sig"]
                            if waits.get(s, 0) < v - 16:
                                waits[s] = v - 16
                        for s, v in waits.items():
                            if seen.get(s, 0) >= v:
                                continue
                            seen[s] = v
                            eng.wait_ge(sems[s], v)
                        bi = I["fn"](eng)
                        if I["sig"] is not None:
                            s, v = I["sig"]
                            bi.then_inc(sems[s], 16 if I["dma"] else 1)
                    if dcnt[e] > 0:
                        for j in range(K_DMA):
                            tot = (dcnt[e] - j + K_DMA - 1) // K_DMA
                            if tot > 0 and seen.get(("d", e, j), 0) < 16 * tot:
                                eng.wait_ge(sems[("d", e, j)], 16 * tot)
                return body

            reg = {"pe": block.tensor, "act": block.scalar, "dve": block.vector,
                   "pool": block.gpsimd, "sp": block.sync}
            for e in engines:
                if any(I["eng"] == e for I in ins):
                    reg[e](make(e))


def _slopes():
    return [2.0 ** (-8.0 * (h + 1) / 8.0) for h in range(8)]


def _const_tables():
    sl = np.array(_slopes(), np.float64)
    kk = np.arange(128)[:, None, None]
    dd = np.arange(16)[None, None, :]
    btab_p = -(sl[None, :, None]) * (128.0 * dd + 127.0 - kk)
    k = np.arange(128)[:, None, None]
    q = np.arange(128)[None, None, :]
    vis = (k // 64) <= (q // 64)
    bm = -(sl[None, :, None]) * np.abs(q - k) - sl[None, :, None] * (127.0 - q)
    bmat_p = np.where(vis, bm, -30000.0)
    ii = np.arange(16)[None, None, :]
    btab_s = -(sl[None, :, None]) * (2111.0 - 128.0 * ii - kk)
    k6 = np.arange(64)[:, None, None]
    q6 = np.arange(64)[None, None, :]
    bmat_s = -(sl[None, :, None]) * np.abs(q6 - k6) - sl[None, :, None] * (63.0 - q6)
    bmat_s_full = np.zeros((128, 8, 64))
    bmat_s_full[:64] = bmat_s
    tri = (np.arange(128)[:, None] <= np.arange(128)[None, :]).astype(np.float32)
    scanmask = np.ones((128, 512), np.float32)
    scanmask[:, ::128] = 0.0
    return dict(
        btab_p=btab_p.astype(np.float32), bmat_p=bmat_p.astype(np.float32),
        btab_s=btab_s.astype(np.float32), bmat_s=bmat_s_full.astype(np.float32),
        tri=tri, scanmask=scanmask, ident=np.eye(128, dtype=np.float32),
    )


def build_program(NP, NS, NL=4):
    nc = bass.Bass("TRN2", target_bir_lowering=False)

    def din(name, shape):
        return nc.dram_tensor(name, list(shape), F32, kind="ExternalInput").ap()

    def dout(name, shape):
        return nc.dram_tensor(name, list(shape), F32, kind="ExternalOutput").ap()

    xp_d = din("xp", [max(NP, 1), SEQ, D])
    xs_d = din("xs", [max(NS, 1), DEC_SEQ, D])
    ck_d = din("ck", [2, max(NS, 1), PAST, D])
    cv_d = din("cv", [2, max(NS, 1), PAST, D])
    st_d = din("st", [2, max(NS, 1), 8, 128, 128])
    normwT_d = din("normwT", [128, 4, 8])
    hwin_d = din("hwin", [2, D, 4 * D])
    lbT_d = din("lbT", [128, 2, 8])
    onwT_d = din("onwT", [128, 2, 8])
    hwo_d = din("hwo", [2, D, D])
    awin_d = din("awin", [2, D, 4 * D])
    qkn_d = din("qkn", [128, 2, 2, 64])
    lam_d = din("lam", [128, 2, 4, 64])
    sub_d = din("sub", [128, 2, 128])
    awo_d = din("awo", [2, D, D])
    btab_p_d = din("btab_p", [128, 8, 16])
    bmat_p_d = din("bmat_p", [128, 8, 128])
    btab_s_d = din("btab_s", [128, 8, 16])
    bmat_s_d = din("bmat_s", [128, 8, 64])
    tri_d = din("tri", [128, 128])
    scanmask_d = din("scanmask", [128, 512])
    ident_d = din("ident", [128, 128])

    yp_d = dout("yp", [max(NP, 1), SEQ, D])
    ys_d = dout("ys", [max(NS, 1), DEC_SEQ, D])
    nkp_d = dout("nkp", [2, max(NP, 1), SEQ, D])
    nvp_d = dout("nvp", [2, max(NP, 1), SEQ, D])
    nks_d = dout("nks", [2, max(NS, 1), DEC_SEQ, D])
    nvs_d = dout("nvs", [2, max(NS, 1), DEC_SEQ, D])
    nsp_d = dout("nsp", [2, max(NP, 1), 8, 128, 128])
    nss_d = dout("nss", [2, max(NS, 1), 8, 128, 128])

    S = Sched()
    uid = [0]

    with contextlib.ExitStack() as stk:
        def sb(name, shape, dt):
            return stk.enter_context(nc.sbuf_tensor("s_" + name, list(shape), dt))

        def psb(name, shape, dt):
            return stk.enter_context(nc.psum_tensor("p_" + name, list(shape), dt))

        xres = sb("xres", [128, 16, D], F32)
        hT = sb("hT", [128, 8, SEQ], BF16)
        uT = sb("uT", [128, 8, SEQ], BF16)
        wbuf_t = sb("wbuf", [128, 8, 1024], BF16)
        wbuf = [wbuf_t[:, :, 0:512], wbuf_t[:, :, 512:1024]]
        hb = [sb("hb0", [128, D], BF16)] * 2
        ssx = sb("ssx", [128, 16], F32)
        rstdx = sb("rstdx", [128, 16], F32)
        ssacc = sb("ssacc", [128, 16], F32)
        rstdo = sb("rstdo", [128, 16], F32)
        bar_d = sb("bar_d", [128, 4], F32)
        identb = sb("identb", [128, 128], BF16)
        tri = sb("tri", [128, 128], F32)
        scanmask = sb("scanmask", [128, 512], F32)
        ones1 = sb("ones1", [128, 1], F32)
        normwT = sb("normwT", [128, 4, 8], F32)
        lbraw = sb("lbraw", [128, 2, 8], F32)
        lbT = sb("lbT", [128, 2, 8], F32)
        omlT = sb("omlT", [128, 2, 8], F32)
        onwT = sb("onwT", [128, 2, 8], F32)
        qkn = sb("qkn", [128, 2, 2, 64], F32)
        lams = sb("lams", [128, 2, 2], F32)
        lame = sb("lame", [128, 2, 2], F32)
        neglam = sb("neglam", [128, 2], F32)
        wsub = sb("wsub", [128, 2, 128], F32)
        btab_p = sb("btab_p", [128, 8, 16], F32)
        bmat_p = sb("bmat_p", [128, 8, 128], F32)
        btab_s = sb("btab_s", [128, 8, 16], F32)
        bmat_s = sb("bmat_s", [128, 8, 64], F32)
        Sall = sb("Sall", [128, 17, 128], BF16)
        S32 = [sb("S32a", [128, 128], F32), sb("S32b", [128, 128], F32)]
        SCR_BYTES = 43008
        scr = sb("scr", [128, SCR_BYTES // 2], BF16)

        class Carver:
            def __init__(self):
                self.off = 0

            def __call__(self, shape, dt):
                n = 1
                for x in shape[1:]:
                    n *= x
                nb = n * (4 if dt == F32 else 2)
                nb_al = (nb + 31) // 32 * 32
                assert self.off + nb_al <= SCR_BYTES, (self.off, nb_al)
                ap = scr[:, self.off // 2:(self.off + nb) // 2]
                self.off += nb_al
                if dt == F32:
                    ap = ap.bitcast(F32)
                if len(shape) == 3:
                    ap = ap.rearrange("p (a b) -> p a b", a=shape[1])
                elif len(shape) == 4:
                    ap = ap.rearrange("p (a b c) -> p a b c", a=shape[1], b=shape[2])
                assert tuple(ap.shape) == tuple(shape), (ap.shape, shape)
                return ap

        cv_ = Carver()
        junk = cv_([128, D], BF16)
        base_off = cv_.off
        lamv = cv_([128, 2, 4, 64], F32)
        lamp = cv_([128, 2, 2, 64], F32)
        cv_.off = base_off
        qf = cv_([128, 512], F32)
        ff = cv_([128, 512], F32)
        gg = cv_([128, 512], F32)
        kf = cv_([128, 512], F32)
        bbuf = cv_([128, 512], F32)
        ee = [cv_([128, 512], F32), cv_([128, 512], F32)]
        ex = [cv_([128, 512], F32), cv_([128, 512], F32)]
        osq = cv_([128, 512], F32)
        qt = cv_([128, 512], BF16)
        qh = cv_([128, 512], BF16)
        khT = cv_([128, 512], BF16)
        szT = cv_([128, 512], BF16)
        Kb = [cv_([128, 4, 128], BF16) for i in range(4)]
        khtm = cv_([128, 4, 128], BF16)
        vtm = cv_([128, 4, 128], BF16)
        attm = cv_([128, 4, 128], BF16)
        dec = cv_([128, 8], F32)
        hg_end = cv_.off
        cv_.off = base_off
        kT = cv_([128, 2, SEQ + 64], BF16)
        vaug = cv_([128, 17, 2, 129], BF16)
        kctm = cv_([128, 16, 2, 128], BF16)
        sq = cv_([128, 256], F32)
        ssq4 = cv_([128, 8], F32)
        rs4 = cv_([128, 8], F32)
        kn = cv_([128, 256], F32)
        kout = [cv_([128, 256], F32), cv_([128, 256], F32)]
        vout = [cv_([128, 256], F32), cv_([128, 256], F32)]
        qkb = cv_([128, 256], BF16)
        qTt = [cv_([128, 2, 128], BF16), cv_([128, 2, 128], BF16)]
        szt = [cv_([128, 256], BF16), cv_([128, 256], BF16)]
        Pt = [cv_([128, 2, 128], BF16) for i in range(4)]
        dtmp = cv_([128, 2, 128], F32)
        rr = cv_([128, 8], F32)
        rl = cv_([128, 8], F32)
        t1 = cv_([128, 128], F32)
        oo = cv_([128, 128], F32)
        ssn = cv_([128, 8], F32)
        rsn = cv_([128, 8], F32)
        ug = cv_([128, 128], F32)
        ub = cv_([128, 128], BF16)
        at_end = cv_.off
        print("scratch bytes: hgrn", hg_end, "attn", at_end)
        pb = [psb("pb%d" % i, [128, 512], F32) for i in range(6)]
        pt = [psb("pt%d" % i, [128, 1024], BF16) for i in range(2)]

        def fsz(ap):
            n = 1
            for x in tuple(ap.shape)[1:]:
                n *= x
            return float(n)

        def mm(out, lhsT, rhs, start, stop, r, w):
            nn = fsz(out)
            c = max(nn, 64.0) / 2.2 + 12.0
            if lhsT.dtype == F32:
                c *= 4.0
            S.op("pe", lambda e: e.matmul(out, lhsT=lhsT, rhs=rhs, start=start, stop=stop), r, w, cost=c)

        def tr(out, in_, ident, r, w):
            S.op("pe", lambda e: e.transpose(out=out, in_=in_, identity=ident), r, w, cost=max(fsz(out), 64.0) / 2.2 + 40.0)

        def act(out, in_, func, r, w, bias=None, scale=None, accum=None):
            kw = {}
            if bias is not None:
                kw["bias"] = bias
            if scale is not None:
                kw["scale"] = scale
            if accum is not None:
                kw["accum_out"] = accum
            S.op("act", lambda e: e.activation(out=out, in_=in_, func=func, **kw), r, w, cost=(224.0 + fsz(out)) / 1.4)

        def vcost(eng, out):
            if eng == "pool":
                return (120.0 + fsz(out)) / 1.0
            return (70.0 + fsz(out)) / 0.96

        def tt(eng, out, in0, in1, op, r, w):
            S.op(eng, lambda e: e.tensor_tensor(out=out, in0=in0, in1=in1, op=op), r, w, cost=vcost(eng, out))

        def tsc(eng, out, in0, s1, s2, op0, op1, r, w):
            if op1 is None:
                S.op(eng, lambda e: e.tensor_scalar(out=out, in0=in0, scalar1=s1, scalar2=None, op0=op0), r, w,
                     cost=vcost(eng, out))
            else:
                S.op(eng, lambda e: e.tensor_scalar(out=out, in0=in0, scalar1=s1, scalar2=s2, op0=op0, op1=op1), r, w,
                     cost=vcost(eng, out))

        def stt(eng, out, in0, scalar, in1, op0, op1, r, w):
            S.op(eng, lambda e: e.scalar_tensor_tensor(out=out, in0=in0, scalar=scalar, in1=in1, op0=op0, op1=op1), r, w,
                 cost=vcost(eng, out))

        def cp(eng, out, in_, r, w):
            if eng == "act":
                S.op("act", lambda e: e.copy(out=out, in_=in_), r, w, cost=(224.0 + fsz(out)) / 1.4)
            else:
                S.op(eng, lambda e: e.tensor_copy(out=out, in_=in_), r, w, cost=vcost(eng, out))

        def mset(eng, ap, val, w):
            S.op(eng, lambda e: e.memset(ap, val), (), w, cost=vcost(eng, ap) * 0.5)

        def dma(eng, out, in_, r, w):
            nb = fsz(out) * float(out.shape[0]) * 4.0
            S.op(eng, lambda e: e.dma_start(out=out, in_=in_), r, w, dma=True, cost=2000.0 + nb / 120.0)

        def rstd_ops(eng, out, in_, n, r, w):
            tsc(eng, out, in_, 1.0 / n, EPS, ALU.mult, ALU.add, r, w)
            act(out, out, AF.Sqrt, w, w)
            S.op("dve", lambda e: e.reciprocal(out=out, in_=out), w, w)

        dma("sp", tri[:], tri_d, (), ["tri"])
        dma("sp", scanmask[:], scanmask_d, (), ["scanmask"])
        dma("pool", identb[:], ident_d, (), ["identb"])
        dma("sp", normwT[:], normwT_d, (), ["normwT"])
        dma("sp", lbraw[:], lbT_d, (), ["lbraw"])
        dma("sp", onwT[:], onwT_d, (), ["onwT"])
        dma("sp", qkn[:], qkn_d, (), ["qkn"])
        dma("sp", lamv[:], lam_d, (), ["lamv"])
        dma("sp", wsub[:], sub_d, (), ["wsub"])
        dma("sp", btab_p[:], btab_p_d, (), ["btab_p"])
        dma("sp", bmat_p[:], bmat_p_d, (), ["bmat_p"])
        dma("sp", btab_s[:], btab_s_d, (), ["btab_s"])
        dma("sp", bmat_s[:], bmat_s_d, (), ["bmat_s"])
        mset("dve", ones1[:], 1.0, ["ones1"])
        mset("dve", lbT[:], 0.0, ["lbT"])
        tt("dve", lbT[:, 1, :], lbraw[:, 1, :], lbraw[:, 0, :], ALU.subtract, ["lbraw"], ["lbT"])
        act(lbT[:, 1, :], lbT[:, 1, :], AF.Sigmoid, ["lbT"], ["lbT"])
        tsc("dve", omlT[:], lbT[:], -1.0, 1.0, ALU.mult, ALU.add, ["lbT"], ["omlT"])
        tt("dve", lamp[:, :, 0, :], lamv[:, :, 0, :], lamv[:, :, 1, :], ALU.mult, ["lamv"], ["lamp"])
        tt("dve", lamp[:, :, 1, :], lamv[:, :, 2, :], lamv[:, :, 3, :], ALU.mult, ["lamv"], ["lamp"])
        S.op("dve", lambda e: e.tensor_reduce(out=lams[:], in_=lamp[:], axis=AX.X, op=ALU.add), ["lamp"], ["lams"])
        act(lame[:], lams[:], AF.Exp, ["lams"], ["lame"])
        for j in range(2):
            lam_init = 0.8 - 0.6 * math.exp(-0.3 * (2 * j + 1))
            tt("dve", neglam[:, j:j + 1], lame[:, j, 1:2], lame[:, j, 0:1], ALU.subtract, ["lame"], ["neglam"])
            tsc("dve", neglam[:, j:j + 1], neglam[:, j:j + 1], -lam_init, None, ALU.add, None, ["neglam"], ["neglam"])
            tsc("dve", wsub[:, j, :], wsub[:, j, :], 1.0 - lam_init, None, ALU.mult, None, ["wsub"], ["wsub"])

        def fence():
            S.barrier({
                "act": lambda e: e.copy(out=bar_d[:, 0:1], in_=ones1[:, 0:1]),
                "dve": lambda e: e.memset(bar_d[:, 1:2], 0.0),
                "pool": lambda e: e.memset(bar_d[:, 2:3], 0.0),
            })

        def phase_norm(l, T):
            TP = min(T, 128)
            NT = T // TP
            mset("dve", ssx[:], 0.0, ["ssx"])
            for t in range(NT):
                act(junk[:TP, :], xres[:TP, t, :], AF.Square, [("x", t)], ["junk", "ssx"], accum=ssx[:TP, t:t + 1])
            rstd_ops("dve", rstdx[:TP, :NT], ssx[:TP, :NT], float(D), ["ssx"], ["rstdx"])
            for t in range(NT):
                hbt = hb[t % 2]
                act(hbt[:TP, :], xres[:TP, t, :], AF.Identity, [("x", t), "rstdx"], [("hb", 0)],
                    scale=rstdx[:TP, t:t + 1])
                ptb = pt[t % 2]
                ptv = ptb[:, :].rearrange("p (a b) -> p a b", a=8)
                for kc in range(8):
                    tr(ptv[:, kc, :TP], hbt[:TP, kc * 128:(kc + 1) * 128], identb[:TP, :TP],
                       [("hb", 0), "identb"], [("pt", t % 2)])
                tt("dve", hT[:, :, t * TP:(t + 1) * TP], ptv[:, :, :TP],
                   normwT[:, l, :].unsqueeze(2).broadcast_to([128, 8, TP]), ALU.mult,
                   [("pt", t % 2), "normwT"], [("hT", t)])

        def phase_outproj(w_d, j, T, use_rstd):
            TP = min(T, 128)
            NT = T // TP
            wv = w_d[j].rearrange("(kc p) n -> p kc n", p=128)
            for half in range(2):
                dma("pool", wbuf[half][:, :, 0:512], wv[:, :, half * 512:(half + 1) * 512], (), [("wbuf", half)])
            for t in range(NT):
                for half in range(2):
                    bank = pb[(2 * t + half) % 2]
                    bk = ("pb", (2 * t + half) % 2)
                    for kc in range(8):
                        mm(bank[:TP, :], uT[:, kc, t * TP:(t + 1) * TP], wbuf[half][:, kc, 0:512],
                           kc == 0, kc == 7, ["uT", ("wbuf", half)], [bk])
                    xo = xres[:TP, t, half * 512:(half + 1) * 512]
                    if use_rstd:
                        stt("dve", xo, bank[:TP, :], rstdo[:TP, t:t + 1], xo, ALU.mult, ALU.add,
                            [bk, "rstdo", ("x", t)], [("x", t)])
                    else:
                        tt("dve", xo, bank[:TP, :], xo, ALU.add, [bk, ("x", t)], [("x", t)])

        def hgrn_layer(l, T, kind, si):
            j = l // 2
            CH = min(T, 128)
            NCHT = T // CH
            SEGT = min(T, 512)
            NSEG = T // SEGT
            NCH = SEGT // CH
            NSB = CH // 32
            fence()
            for i in range(4):
                mset("pool", Kb[i], 0.0, [("Kb", i)])
            phase_norm(l, T)
            if STOP == 1:
                return
            wv = hwin_d[j].rearrange("(kc p) n -> p kc n", p=128)
            for h in range(8):
                slot = h % 2
                wb = wbuf[slot]
                wk = ("wbuf", slot)
                for qi in range(4):
                    dma("pool", wb[:, :, qi * 128:(qi + 1) * 128],
                        wv[:, :, qi * 1024 + h * 128: qi * 1024 + (h + 1) * 128], (), [wk])
                s32 = S32[h % 2]
                sk = ("S32", h % 2)
                if kind == "p":
                    mset("pool", s32[:], 0.0, [sk])
                    mset("pool", Sall[:, 0, :], 0.0, ["Sall"])
                else:
                    dma("sp", s32[:], st_d[j, si, h], (), [sk])
                    cp("pool", Sall[:, 0, :], s32[:], [sk], ["Sall"])
                for seg in range(NSEG):
                    t0 = seg * SEGT
                    for qi, bi in ((0, 0), (1, 1), (3, 2)):
                        for kc in range(8):
                            mm(pb[bi][:, :SEGT], wb[:, kc, qi * 128:(qi + 1) * 128], hT[:, kc, t0:t0 + SEGT],
                               kc == 0, kc == 7, [wk] + [("hT", tt_) for tt_ in range(t0 // CH, (t0 + SEGT) // CH)], [("pb", bi)])
                    act(qf[:, :SEGT], pb[0][:, :SEGT], AF.Identity, [("pb", 0)], ["qf"], scale=128.0 ** -0.5)
                    act(ff[:, :SEGT], pb[1][:, :SEGT], AF.Sigmoid, [("pb", 1)], ["ff"])
                    act(szT[:, :SEGT], pb[2][:, :SEGT], AF.Silu, [("pb", 2)], ["szT"])
                    tsc("dve", ff[:, :SEGT], ff[:, :SEGT], omlT[:, j, h:h + 1], lbT[:, j, h:h + 1], ALU.mult, ALU.add,
                        ["ff", "omlT", "lbT"], ["ff"])
                    act(gg[:, :SEGT], ff[:, :SEGT], AF.Ln, ["ff"], ["gg"])
                    tsc("pool", kf[:, :SEGT], ff[:, :SEGT], -1.0, 1.0, ALU.mult, ALU.add, ["ff"], ["kf"])
                    pv = pb[3][:, :].rearrange("p (c v) -> p c v", c=4)
                    for c in range(NCH):
                        for kc in range(8):
                            mm(pv[:CH, c, :], hT[:, kc, t0 + c * CH:t0 + (c + 1) * CH], wb[:, kc, 256:384],
                               kc == 0, kc == 7, [wk, ("hT", t0 // CH + c)], [("pb", 3)])
                    cp("act", vtm[:CH, :NCH, :], pv[:CH, :NCH, :], [("pb", 3)], ["vtm"])
                    if STOP == 2:
                        return
                    S.op("dve", lambda e, _o=bbuf[:, :SEGT], _m=scanmask[:, :SEGT], _g=gg[:, :SEGT]:
                         e.tensor_tensor_scan(out=_o, data0=_m, data1=_g, initial=0.0, op0=ALU.mult, op1=ALU.add),
                         ["scanmask", "gg"], ["bb"], cost=(70.0 + SEGT) / 0.96)
                    b3 = bbuf[:, :SEGT].rearrange("p (c t) -> p c t", c=NCH)
                    b4 = bbuf[:, :SEGT].rearrange("p (c i t) -> p c i t", c=NCH, i=NSB)
                    e0 = ee[0]
                    e04 = e0[:, :SEGT].rearrange("p (c i t) -> p c i t", c=NCH, i=NSB)
                    if NSB > 1:
                        tt("dve", e04[:, :, 1:NSB, :], b4[:, :, 1:NSB, :],
                           b4[:, :, 0:NSB - 1, 31:32].broadcast_to([128, NCH, NSB - 1, 32]), ALU.subtract,
                           ["bb"], [("ee", 0)])
                    cp("pool", e04[:, :, 0, :], b4[:, :, 0, :], ["bb"], [("ee", 0)])
                    act(ex[0][:, :SEGT], e0[:, :SEGT], AF.Exp, [("ee", 0)], [("ex", 0)])
                    tt("dve", qt[:, :SEGT], qf[:, :SEGT], ex[0][:, :SEGT], ALU.mult, ["qf", ("ex", 0)], ["qt"])
                    act(ex[1][:, :SEGT], bbuf[:, :SEGT], AF.Exp, ["bb"], [("ex", 1)])
                    tt("pool", qh[:, :SEGT], qf[:, :SEGT], ex[1][:, :SEGT], ALU.mult, ["qf", ("ex", 1)], ["qh"])
                    if STOP == 3:
                        return
                    kf3 = kf[:, :SEGT].rearrange("p (c t) -> p c t", c=NCH)
                    for i in range(NSB):
                        wi = 32 * (i + 1)
                        b_ = i % 2
                        ex3 = ex[b_][:, :SEGT].rearrange("p (c t) -> p c t", c=NCH)
                        if i == 0:
                            act(ex3[:, :, 0:wi], b3[:, :, 0:wi], AF.Exp, ["bb"], [("ex", b_)], scale=-1.0)
                        else:
                            ee3 = ee[b_][:, :SEGT].rearrange("p (c t) -> p c t", c=NCH)
                            tt("dve", ee3[:, :, 0:wi], b3[:, :, 32 * i - 1:32 * i].broadcast_to([128, NCH, wi]),
                               b3[:, :, 0:wi], ALU.subtract, ["bb"], [("ee", b_)])
                            act(ex3[:, :, 0:wi], ee3[:, :, 0:wi], AF.Exp, [("ee", b_)], [("ex", b_)])
                        tt("dve" if i % 2 == 0 else "pool", Kb[i][:, :NCH, 0:wi], kf3[:, :, 0:wi], ex3[:, :, 0:wi], ALU.mult,
                           ["kf", ("ex", b_)], [("Kb", i)])
                    ee3 = ee[0][:, :SEGT].rearrange("p (c t) -> p c t", c=NCH)
                    tt("dve", ee3[:, :, :], b3[:, :, CH - 1:CH].broadcast_to([128, NCH, CH]), b3[:, :, :], ALU.subtract,
                       ["bb"], [("ee", 0)])
                    act(ex[0][:, :SEGT], ee[0][:, :SEGT], AF.Exp, [("ee", 0)], [("ex", 0)])
                    tt("pool", khT[:, :SEGT], kf[:, :SEGT], ex[0][:, :SEGT], ALU.mult, ["kf", ("ex", 0)], ["khT"])
                    act(dec[:, :NCH], b3[:, :, CH - 1], AF.Exp, ["bb"], ["dec"])
                    if STOP == 4:
                        return
                    ptv = pt[0][:, 0:512].rearrange("p (c k) -> p c k", c=4)
                    for c in range(NCH):
                        tr(ptv[:CH, c, :], khT[:, c * CH:(c + 1) * CH], identb[:, :], ["khT", "identb"], [("pt", 0)])
                    cp("act", khtm[:CH, :NCH, :], ptv[:CH, :NCH, :], [("pt", 0)], ["khtm"])
                    if STOP == 5:
                        return
                    for c in range(NCH):
                        cg = seg * NCH + c
                        ub_ = 4 if c % 2 == 0 else 2
                        pu = pb[ub_][:, 0:128]
                        mm(pu, khtm[:CH, c, :], vtm[:CH, c, :], True, True, ["khtm", "vtm"], [("pb", ub_)])
                        stt("dve", s32[:], s32[:], dec[:, c:c + 1], pu, ALU.mult, ALU.add,
                            [sk, "dec", ("pb", ub_)], [sk])
                        cp("pool", Sall[:, cg + 1, :], s32[:], [sk], ["Sall"])
                    if STOP == 6:
                        return
                    pa = pb[5][:, :].rearrange("p (c t) -> p c t", c=4)
                    for c in range(NCH):
                        for i in range(NSB):
                            mm(pa[:CH, c, 32 * i:32 * (i + 1)], Kb[i][:, c, 0:CH], qt[:, c * CH + 32 * i:c * CH + 32 * (i + 1)],
                               True, True, [("Kb", i), "qt"], [("pb", 5)])
                    tt("dve", attm[:CH, :NCH, :CH], pa[:CH, :NCH, :CH],
                       tri[:CH, :CH].unsqueeze(1).broadcast_to([CH, NCH, CH]), ALU.mult,
                       [("pb", 5), "tri"], ["attm"])
                    if STOP == 7:
                        return
                    po = pb[0]
                    for c in range(NCH):
                        cg = seg * NCH + c
                        mm(po[:, c * CH:(c + 1) * CH], vtm[:CH, c, :], attm[:CH, c, :CH], True, False,
                           ["vtm", "attm"], [("pb", 0)])
                        mm(po[:, c * CH:(c + 1) * CH], Sall[:, cg, :], qh[:, c * CH:(c + 1) * CH], False, True,
                           ["Sall", "qh"], [("pb", 0)])
                    if STOP == 8:
                        return
                    act(osq[:, :SEGT], po[:, :SEGT], AF.Identity, [("pb", 0)], ["osq", "po_rd"])
                    tt("dve", osq[:, :SEGT], osq[:, :SEGT], osq[:, :SEGT], ALU.mult, ["osq"], ["osq"])
                    if STOP == 12:
                        return
                    stt("dve", uT[:, h, t0:t0 + SEGT], po[:, :SEGT], onwT[:, j, h:h + 1], szT[:, :SEGT], ALU.mult, ALU.mult,
                        [("pb", 0), "onwT", "szT", "po_rd"], ["uT"])
                    if STOP == 9:
                        return
                    pss = pb[1]
                    for c in range(NCH):
                        mm(pss[:CH, c:c + 1], osq[:, c * CH:(c + 1) * CH], ones1[:, 0:1], True, True,
                           ["osq", "ones1"], [("pb", 1)])
                    if STOP == 10:
                        return
                    cg0 = seg * NCH
                    if h == 0:
                        cp("dve", ssacc[:CH, cg0:cg0 + NCH], pss[:CH, 0:NCH], [("pb", 1)], ["ssacc"])
                    else:
                        tt("dve", ssacc[:CH, cg0:cg0 + NCH], pss[:CH, 0:NCH], ssacc[:CH, cg0:cg0 + NCH], ALU.add,
                           [("pb", 1), "ssacc"], ["ssacc"])
                if STOP == 11:
                    return
                od = nsp_d if kind == "p" else nss_d
                dma("sp", od[j, si, h], s32[:], [sk], ())
            rstd_ops("dve", rstdo[:CH, :NCHT], ssacc[:CH, :NCHT], float(D), ["ssacc"], ["rstdo"])
            phase_outproj(hwo_d, j, T, True)

        def attn_layer(l, T, kind, si):
            j = l // 2
            TP = min(T, 128)
            NT = T // TP
            fence()
            mset("pool", vaug, 1.0, ["vaug"])
            phase_norm(l, T)
            wv = awin_d[j].rearrange("(kc p) n -> p kc n", p=128)
            nk_d = nkp_d if kind == "p" else nks_d
            nv_d = nvp_d if kind == "p" else nvs_d
            ktile0 = 16 if kind == "s" else 0
            for hp in range(4):
                wb = wbuf_t
                wk = ("wbuf", 0)
                wk1 = ("wbuf", 1)
                cols = [(0, hp * 128, 128), (128, 512 + hp * 128, 128), (256, 1024 + hp * 128, 128),
                        (384, 1536 + hp * 128, 128), (512, 2048 + hp * 256, 256), (768, 3072 + hp * 256, 256)]
                for (o, c0, n) in cols:
                    dma("pool", wb[:, :, o:o + n], wv[:, :, c0:c0 + n], (), [wk, wk1])
                if kind == "s":
                    ckv = ck_d[j, si].rearrange("(t p) n -> p t n", p=128)
                    cvv = cv_d[j, si].rearrange("(t p) n -> p t n", p=128)
                    for m in range(2):
                        dma("pool", kctm[:, :, m, :], ckv[:, :, m * 512 + hp * 128:m * 512 + (hp + 1) * 128], (), ["kctm"])
                    for hh in range(2):
                        dma("pool", vaug[:, 0:16, hh, 0:128], cvv[:, :, hp * 256 + hh * 128:hp * 256 + (hh + 1) * 128],
                            (), ["vaug"])
                    for t in range(16):
                        ptb = pt[t % 2]
                        ptv = ptb[:, 0:256].rearrange("p (m k) -> p m k", m=2)
                        for m in range(2):
                            tr(ptv[:, m, :], kctm[:, t, m, :], identb[:, :], ["kctm", "identb"], [("pt", t % 2)])
                        cp("act" if t % 2 == 0 else "dve", kT[:, :, t * 128:(t + 1) * 128], ptv[:, :, :],
                           [("pt", t % 2)], ["kT"])
                for t in range(NT):
                    bank = pb[t % 2]
                    bk = ("pb", t % 2)
                    for kc in range(8):
                        mm(bank[:TP, :], hT[:, kc, t * TP:(t + 1) * TP], wb[:, kc, 256:768], kc == 0, kc == 7,
                           [wk, wk1, ("hT", t)], [bk])
                    qk_norm(bank, bk, TP, j, 1)
                    ko = kout[t % 2]
                    tt("pool", ko[:TP, :].rearrange("p (g d) -> p g d", g=4), kn[:TP, :].rearrange("p (g d) -> p g d", g=4),
                       qkn[:TP, j, 1, :].unsqueeze(1).broadcast_to([TP, 4, 64]), ALU.mult, ["kn", "qkn"], [("kout", t % 2)])
                    for m in range(2):
                        dma("sp", nk_d[j, si, t * TP:(t + 1) * TP, m * 512 + hp * 128:m * 512 + (hp + 1) * 128],
                            ko[:TP, m * 128:(m + 1) * 128], [("kout", t % 2)], ())
                    cp("act", qkb[:TP, :], ko[:TP, :], [("kout", t % 2)], ["qkb"])
                    ptv = pt[t % 2][:, 0:256].rearrange("p (m k) -> p m k", m=2)
                    for m in range(2):
                        tr(ptv[:, m, :TP], qkb[:TP, m * 128:(m + 1) * 128], identb[:TP, :TP], ["qkb", "identb"],
                           [("pt", t % 2)])
                    kt0 = ktile0 * 128 + t * TP
                    cp("dve", kT[:, :, kt0:kt0 + TP], ptv[:, :, :TP], [("pt", t % 2)], ["kT"])
                    vo = vout[t % 2]
                    cp("act", vo[:TP, :], bank[:TP, 256:512], [bk], [("vout", t % 2)])
                    dma("sp", nv_d[j, si, t * TP:(t + 1) * TP, hp * 256:(hp + 1) * 256], vo[:TP, :], [("vout", t % 2)], ())
                    cp("pool", vaug[:TP, ktile0 + t, :, 0:128], vo[:TP, :].rearrange("p (h e) -> p h e", h=2),
                       [("vout", t % 2)], ["vaug"])
                for jt in range(NT):
                    bank = pb[jt % 2]
                    bk = ("pb", jt % 2)
                    for kc in range(8):
                        mm(bank[:TP, 0:256], hT[:, kc, jt * TP:(jt + 1) * TP], wb[:, kc, 0:256], kc == 0, kc == 7,
                           [wk, wk1, ("hT", jt)], [bk])
                    for kc in range(8):
                        mm(bank[:TP, 256:512], hT[:, kc, jt * TP:(jt + 1) * TP], wb[:, kc, 768:1024], kc == 0, kc == 7,
                           [wk, wk1, ("hT", jt)], [bk])
                    qk_norm(bank, bk, TP, j, 0)
                    tt("pool", qkb[:TP, :].rearrange("p (g d) -> p g d", g=4), kn[:TP, :].rearrange("p (g d) -> p g d", g=4),
                       qkn[:TP, j, 0, :].unsqueeze(1).broadcast_to([TP, 4, 64]), ALU.mult, ["kn", "qkn"], ["qkb"])
                    ptv = pt[jt % 2][:, 0:256].rearrange("p (m k) -> p m k", m=2)
                    for m in range(2):
                        tr(ptv[:, m, :TP], qkb[:TP, m * 128:(m + 1) * 128], identb[:TP, :TP], ["qkb", "identb"],
                           [("pt", jt % 2)])
                    qT_ = qTt[jt % 2]
                    cp("dve", qT_[:, :, :TP], ptv[:, :, :TP], [("pt", jt % 2)], [("qTt", jt % 2)])
                    sz_ = szt[jt % 2]
                    act(sz_[:TP, :], bank[:TP, 256:512], AF.Silu, [bk], [("szt", jt % 2)])
                    if kind == "p":
                        ktiles = [(i, 128, "off" if i < jt else "diag") for i in range(jt + 1)]
                    else:
                        ktiles = [(i, 128, "off") for i in range(16)] + [(16, 64, "diag")]
                    for hh in range(2):
                        h = 2 * hp + hh
                        r0 = 64 * hh
                        pos = [pb[4 + m][:, hh * 129:(hh + 1) * 129] for m in range(2)]
                        poks = [("pb", 4), ("pb", 5)]
                        for idx, (i, nk, typ) in enumerate(ktiles):
                            sslot = uid[0] % 4
                            uid[0] += 1
                            psb_ = pb[2 + sslot % 2]
                            ps = psb_[:, 0:256].rearrange("p (m q) -> p m q", m=2)
                            psk = ("pb", 2 + sslot % 2)
                            for m in range(2):
                                mm(ps[:nk, m, :TP], kT[r0:r0 + 64, m, i * 128:i * 128 + nk], qT_[r0:r0 + 64, m, :TP],
                                   True, True, ["kT", ("qTt", jt % 2)], [psk])
                            P_ = Pt[sslot]
                            pk = ("Pt", sslot)
                            if typ == "off":
                                bias = btab_p[:nk, h, jt - i:jt - i + 1] if kind == "p" else btab_s[:nk, h, i:i + 1]
                                act(P_[:nk, :, :TP], ps[:nk, :, :TP], AF.Exp, [psk, "btab_p", "btab_s"], [pk],
                                    bias=bias, scale=0.125)
                            else:
                                bm = bmat_p[:nk, h, :TP] if kind == "p" else bmat_s[:nk, h, :TP]
                                stt("dve", dtmp[:nk, :, :TP], ps[:nk, :, :TP], 0.125,
                                    bm.unsqueeze(1).broadcast_to([nk, 2, TP]), ALU.mult, ALU.add,
                                    [psk, "bmat_p", "bmat_s"], ["dtmp"])
                                act(P_[:nk, :, :TP], dtmp[:nk, :, :TP], AF.Exp, ["dtmp"], [pk])
                            for m in range(2):
                                mm(pos[m][:TP, :], P_[:nk, m, :TP], vaug[:nk, i, hh, :], idx == 0, idx == len(ktiles) - 1,
                                   [pk, "vaug"], [poks[m]])
                        for m in range(2):
                            S.op("dve", lambda e, _o=rr[:TP, m:m + 1], _i=pos[m][:TP, 128:129]: e.reciprocal(out=_o, in_=_i),
                                 [poks[m]], ["rr"])
                        tt("dve", rl[:TP, 0:1], rr[:TP, 1:2], neglam[:TP, j:j + 1], ALU.mult, ["rr", "neglam"], ["rl"])
                        tsc("dve", t1[:TP, :], pos[1][:TP, 0:128], rl[:TP, 0:1], None, ALU.mult, None, [poks[1], "rl"], ["t1"])
                        stt("dve", oo[:TP, :], pos[0][:TP, 0:128], rr[:TP, 0:1], t1[:TP, :], ALU.mult, ALU.add,
                            [poks[0], "rr", "t1"], ["oo"])
                        mset("pool", ssn[:, 0:1], 0.0, ["ssn"])
                        act(junk[:TP, 0:128], oo[:TP, :], AF.Square, ["oo", "ssn"], ["junk", "ssn"], accum=ssn[:TP, 0:1])
                        rstd_ops("dve", rsn[:TP, 0:1], ssn[:TP, 0:1], 128.0, ["ssn"], ["rsn"])
                        stt("dve", ug[:TP, :], oo[:TP, :], rsn[:TP, 0:1], wsub[:TP, j, :], ALU.mult, ALU.mult,
                            ["oo", "rsn", "wsub"], ["ug"])
                        tt("pool", ub[:TP, :], ug[:TP, :], sz_[:TP, hh * 128:(hh + 1) * 128], ALU.mult,
                           ["ug", ("szt", jt % 2)], ["ub"])
                        us = uid[0] % 2
                        ptu = pt[us][:, 512:640]
                        tr(ptu[:, :TP], ub[:TP, :], identb[:TP, :TP], ["ub", "identb"], [("pt", us)])
                        cp("act", uT[:, h, jt * TP:(jt + 1) * TP], ptu[:, :TP], [("pt", us)], ["uT"])
            phase_outproj(awo_d, j, T, False)

        def qk_norm(bank, bk, TP, j, which):
            act(sq[:TP, :], bank[:TP, 0:256], AF.Identity, [bk], ["sq"])
            tt("dve", sq[:TP, :], sq[:TP, :], sq[:TP, :], ALU.mult, ["sq"], ["sq"])
            S.op("dve", lambda e, _o=ssq4[:TP, 0:4], _i=sq[:TP, :].rearrange("p (g d) -> p g d", g=4):
                 e.tensor_reduce(out=_o, in_=_i, axis=AX.X, op=ALU.add), ["sq"], ["ssq4"])
            rstd_ops("dve", rs4[:TP, 0:4], ssq4[:TP, 0:4], 64.0, ["ssq4"], ["rs4"])
            tt("dve", kn[:TP, :].rearrange("p (g d) -> p g d", g=4), bank[:TP, 0:256].rearrange("p (g d) -> p g d", g=4),
               rs4[:TP, 0:4].unsqueeze(2).broadcast_to([TP, 4, 64]), ALU.mult, [bk, "rs4"], ["kn"])

        seqs = [("p", i) for i in range(NP)] + [("s", i) for i in range(NS)]
        for kind, si in seqs:
            T = SEQ if kind == "p" else DEC_SEQ
            TP = min(T, 128)
            NT = T // TP
            xd = (xp_d if kind == "p" else xs_d)[si]
            yd = (yp_d if kind == "p" else ys_d)[si]
            for t in range(NT):
                dma("sp", xres[:TP, t, :], xd[t * TP:(t + 1) * TP, :], (), [("x", t)])
            for l in range(NL):
                if l % 2 == 0:
                    hgrn_layer(l, T, kind, si)
                else:
                    attn_layer(l, T, kind, si)
            for t in range(NT):
                dma("sp", yd[t * TP:(t + 1) * TP, :], xres[:TP, t, :], [("x", t)], ())
        S.emit(nc)
    return nc, len(S.ins)


_PROG_CACHE = {}


def _get_prog(NP, NS, NL=4):
    key = (NP, NS, NL)
    if key not in _PROG_CACHE:
        _PROG_CACHE[key] = build_program(NP, NS, NL)
    return _PROG_CACHE[key]


def make_in_maps(inputs, NP, NS, ncores):
    f = lambda a: np.ascontiguousarray(np.asarray(a, dtype=np.float32))
    c = _const_tables()
    shared = dict(
        normwT=f(np.asarray(inputs["norm_w"]).reshape(4, 8, 128).transpose(2, 0, 1)),
        hwin=f(inputs["hgrn_w_in"]),
        lbT=f(np.asarray(inputs["hgrn_lb_logits"]).reshape(2, 8, 128).transpose(2, 0, 1)),
        onwT=f(np.asarray(inputs["hgrn_onorm_w"]).reshape(2, 8, 128).transpose(2, 0, 1)),
        hwo=f(inputs["hgrn_w_out"]),
        awin=f(inputs["attn_w_in"]),
        qkn=f(np.broadcast_to(np.stack([np.asarray(inputs["attn_q_norm"]), np.asarray(inputs["attn_k_norm"])], axis=1)[None],
                              (128, 2, 2, 64))),
        lam=f(np.broadcast_to(np.asarray(inputs["attn_lambda"])[None], (128, 2, 4, 64))),
        sub=f(np.broadcast_to(np.asarray(inputs["attn_subln"])[None], (128, 2, 128))),
        awo=f(inputs["attn_w_out"]),
        btab_p=c["btab_p"], bmat_p=c["bmat_p"], btab_s=c["btab_s"], bmat_s=c["bmat_s"],
        tri=c["tri"], scanmask=c["scanmask"], ident=c["ident"],
    )
    xp = np.asarray(inputs["x_prompt"])
    xs = np.asarray(inputs["x_sample"])
    ck = np.asarray(inputs["cache_k"]).reshape(2, -1, PAST, D)
    cv = np.asarray(inputs["cache_v"]).reshape(2, -1, PAST, D)
    st = np.asarray(inputs["state_hgrn"])
    maps = []
    for c_ in range(ncores):
        m = dict(shared)
        m["xp"] = f(xp[c_ * NP:(c_ + 1) * NP]) if NP > 0 else np.zeros((1, SEQ, D), np.float32)
        if NS > 0:
            m["xs"] = f(xs[c_ * NS:(c_ + 1) * NS])
            m["ck"] = f(ck[:, c_ * NS:(c_ + 1) * NS])
            m["cv"] = f(cv[:, c_ * NS:(c_ + 1) * NS])
            m["st"] = f(st[:, c_ * NS:(c_ + 1) * NS])
        else:
            m["xs"] = np.zeros((1, DEC_SEQ, D), np.float32)
            m["ck"] = np.zeros((2, 1, PAST, D), np.float32)
            m["cv"] = np.zeros((2, 1, PAST, D), np.float32)
            m["st"] = np.zeros((2, 1, 8, 128, 128), np.float32)
        maps.append(m)
    return maps


def kernel(x_prompt, x_sample, cache_k, cache_v, state_hgrn, norm_w, hgrn_w_in, hgrn_lb_logits,
           hgrn_onorm_w, hgrn_w_out, attn_w_in, attn_q_norm, attn_k_norm, attn_lambda, attn_subln,
           attn_w_out):
    inputs = dict(x_prompt=x_prompt, x_sample=x_sample, cache_k=cache_k, cache_v=cache_v,
                  state_hgrn=state_hgrn, norm_w=norm_w, hgrn_w_in=hgrn_w_in,
                  hgrn_lb_logits=hgrn_lb_logits, hgrn_onorm_w=hgrn_onorm_w, hgrn_w_out=hgrn_w_out,
                  attn_w_in=attn_w_in, attn_q_norm=attn_q_norm, attn_k_norm=attn_k_norm,
                  attn_lambda=attn_lambda, attn_subln=attn_subln, attn_w_out=attn_w_out)
    B = np.asarray(x_prompt).shape[0]
    Bs = np.asarray(x_sample).shape[0]
    NP = B // NCORES
    NS = Bs // NCORES
    nc, _ = _get_prog(NP, NS)
    maps = make_in_maps(inputs, NP, NS, NCORES)
    res = run_bass_kernel_spmd(nc, maps, core_ids=list(range(NCORES)))
    R = res.results
    cat = lambda name, ax: np.concatenate([np.asarray(r[name]) for r in R], axis=ax)
    yp = cat("yp", 0)
    ys = cat("ys", 0)
    nkp = cat("nkp", 1).reshape(2, B, SEQ, 2, 8, 64)
    nvp = cat("nvp", 1).reshape(2, B, SEQ, 8, 128)
    nks = cat("nks", 1).reshape(2, Bs, DEC_SEQ, 2, 8, 64)
    nvs = cat("nvs", 1).reshape(2, Bs, DEC_SEQ, 8, 128)
    nsp = cat("nsp", 1)
    nss = cat("nss", 1)
    return (yp, ys, nkp, nvp, nks, nvs, nsp, nss)
```

```python
import contextlib
import math
import numpy as np
import concourse.bass as bass
import concourse.mybir as mybir
from concourse.bass_utils import run_bass_kernel_spmd

F32 = mybir.dt.float32
BF16 = mybir.dt.bfloat16
AF = mybir.ActivationFunctionType
ALU = mybir.AluOpType
AX = mybir.AxisListType

NCORES = 8
D = 1024
SEQ = 2048
DEC_SEQ = 64
PAST = 2048
EPS = 1e-6
K_DMA = 8
STOP = 0
SKIP_T = 180.0


class Sched:
    LAT_X = 250.0
    LAT_S = 80.0

    def __init__(self, reorder=True):
        self.ins = []
        self.last_w = {}
        self.readers = {}
        self.cur_bar = None
        self.reorder = reorder
        self.rkeys = set()
        self.wkeys = set()

    def op(self, eng, fn, reads=(), writes=(), dma=False, cost=100.0, bar=False):
        idx = len(self.ins)
        deps = set()
        self.rkeys.update(reads)
        self.wkeys.update(writes)
        for k in reads:
            w = self.last_w.get(k)
            if w is not None:
                deps.add(w)
            if isinstance(k, tuple) and k[0] in ("pb", "pt"):
                for r in self.readers.get(k, ()):
                    if self.ins[r]["eng"] != eng:
                        deps.add(r)
        for k in writes:
            w = self.last_w.get(k)
            if w is not None:
                deps.add(w)
            for r in self.readers.get(k, ()):
                deps.add(r)
        for k in reads:
            self.readers.setdefault(k, []).append(idx)
        for k in writes:
            self.last_w[k] = idx
            self.readers[k] = []
        if self.cur_bar is not None and not bar:
            deps.add(self.cur_bar.get(eng, self.cur_bar["dve"]))
        deps.discard(idx)
        self.ins.append(dict(eng=eng, fn=fn, deps=deps, dma=dma, cost=cost, bar=bar))
        return idx

    def barrier(self, dummies):
        nb = {}
        for e, fn in dummies.items():
            nb[e] = self.op(e, fn, (), [("bar", e)], bar=True)
        self.cur_bar = nb

    def _sched_segment(self, seg):
        import heapq
        ins = self.ins
        if not self.reorder or len(seg) < 3:
            return list(seg)
        segset = set(seg)
        indeg = {}
        succ = {}
        avail = {}
        for i in seg:
            ds = [d for d in ins[i]["deps"] if d in segset]
            indeg[i] = len(ds)
            avail[i] = 0.0
            for d in ds:
                succ.setdefault(d, []).append(i)
        engs = sorted(set(ins[i]["eng"] for i in seg))
        waiting = {e: [] for e in engs}
        ready = {e: [] for e in engs}
        busy = {e: 0.0 for e in engs}
        for i in seg:
            if indeg[i] == 0:
                heapq.heappush(ready[ins[i]["eng"]], i)
        order = []
        t = 0.0
        done = 0
        n = len(seg)
        finish = {}
        while done < n:
            progressed = False
            for e in engs:
                if busy[e] > t:
                    continue
                w = waiting[e]
                while w and w[0][0] <= t:
                    heapq.heappush(ready[e], heapq.heappop(w)[1])
                if not ready[e]:
                    continue
                i = heapq.heappop(ready[e])
                I = ins[i]
                c = I["cost"]
                if I["dma"]:
                    issue = 60.0 if e == "sp" else 700.0
                    busy[e] = t + issue
                    fin = t + c
                else:
                    busy[e] = t + c
                    fin = t + c
                finish[i] = fin
                order.append(i)
                done += 1
                progressed = True
                for sidx in succ.get(i, ()):
                    lat = self.LAT_S if ins[sidx]["eng"] == e and not I["dma"] else self.LAT_X
                    if e == "pe" and ins[sidx]["eng"] == "pe":
                        lat = 0.0
                    a = fin + lat
                    if a > avail[sidx]:
                        avail[sidx] = a
                    indeg[sidx] -= 1
                    if indeg[sidx] == 0:
                        heapq.heappush(waiting[ins[sidx]["eng"]], (avail[sidx], sidx))
            if not progressed:
                cand = []
                for e in engs:
                    if busy[e] > t:
                        cand.append(busy[e])
                    elif waiting[e]:
                        cand.append(max(waiting[e][0][0], busy[e]))
                assert cand, "scheduler stuck"
                t = min(cand)
        return order

    def schedule(self):
        ins = self.ins
        n = len(ins)
        order = []
        seg = []
        last_on = {}
        seg_dmas = []

        def flush():
            nonlocal seg
            o = self._sched_segment(seg)
            for i in o:
                last_on[ins[i]["eng"]] = i
                if ins[i]["dma"]:
                    seg_dmas.append(i)
            order.extend(o)
            seg = []

        i = 0
        while i < n:
            if ins[i]["bar"]:
                flush()
                prev = set(last_on.values()) | set(seg_dmas)
                seg_dmas.clear()
                while i < n and ins[i]["bar"]:
                    ins[i]["deps"] = set(prev)
                    order.append(i)
                    i += 1
                for k in order[-8:]:
                    if ins[k]["bar"]:
                        last_on[ins[k]["eng"]] = k
            else:
                seg.append(i)
                i += 1
        flush()
        assert len(order) == n and len(set(order)) == n
        return order

    def emit(self, nc, engines=("pe", "act", "dve", "pool", "sp")):
        print("keys read but never written:", sorted(map(str, self.rkeys - self.wkeys)))
        order = self.schedule()
        ins = [self.ins[i] for i in order]
        remap = {old: new for new, old in enumerate(order)}
        for I in ins:
            I["deps"] = set(remap[d] for d in I["deps"])
        n = len(ins)
        for i, I in enumerate(ins):
            for d in I["deps"]:
                assert d < i, "schedule violates a dependency"
        has_cons = [False] * n
        for I in ins:
            nd = set()
            for d in I["deps"]:
                Pp = ins[d]
                if (not Pp["dma"]) and Pp["eng"] == I["eng"] and I["eng"] == "pe":
                    continue
                nd.add(d)
            I["deps"] = nd
            for d in nd:
                has_cons[d] = True
        cnt = {e: 0 for e in engines}
        dcnt = {e: 0 for e in engines}
        for i, I in enumerate(ins):
            e = I["eng"]
            if I["dma"]:
                k = dcnt[e]
                dcnt[e] += 1
                I["sig"] = (("d", e, k % K_DMA), 16 * (k // K_DMA + 1))
                I["dma_k"] = k
            elif has_cons[i]:
                cnt[e] += 1
                I["sig"] = (("c", e), cnt[e])
            else:
                I["sig"] = None
        with contextlib.ExitStack() as st:
            sems = {}
            for e in engines:
                sems[("c", e)] = st.enter_context(nc.semaphore("c_" + e))
                if dcnt[e] > 0:
                    for j in range(K_DMA):
                        sems[("d", e, j)] = st.enter_context(nc.semaphore("d_%s_%d" % (e, j)))
            block = st.enter_context(nc.Block())

            def make(e):
                def body(eng):
                    seen = {}
                    for I in ins:
                        if I["eng"] != e:
                            continue
                        waits = {}
                        for d in I["deps"]:
                            s, v = ins[d]["sig"]
                            if waits.get(s, 0) < v:
                                waits[s] = v
                        if I["dma"] and I["dma_k"] >= K_DMA:
                            s, v = I["sig"]
                            if waits.get(s, 0) < v - 16:
                                waits[s] = v - 16
                        for s, v in waits.items():
                            if seen.get(s, 0) >= v:
                                continue
                            seen[s] = v
                            eng.wait_ge(sems[s], v)
                        bi = I["fn"](eng)
                        if I["sig"] is not None:
                            s, v = I["sig"]
                            bi.then_inc(sems[s], 16 if I["dma"] else 1)
                    if dcnt[e] > 0:
                        for j in range(K_DMA):
                            tot = (dcnt[e] - j + K_DMA - 1) // K_DMA
                            if tot > 0 and seen.get(("d", e, j), 0) < 16 * tot:
                                eng.wait_ge(sems[("d", e, j)], 16 * tot)
                return body

            reg = {"pe": block.tensor, "act": block.scalar, "dve": block.vector,
                   "pool": block.gpsimd, "sp": block.sync}
            for e in engines:
                if any(I["eng"] == e for I in ins):
                    reg[e](make(e))


def _slopes():
    return [2.0 ** (-8.0 * (h + 1) / 8.0) for h in range(8)]


def _const_tables():
    sl = np.array(_slopes(), np.float64)
    kk = np.arange(128)[:, None, None]
    dd = np.arange(16)[None, None, :]
    btab_p = -(sl[None, :, None]) * (128.0 * dd + 127.0 - kk)
    k = np.arange(128)[:, None, None]
    q = np.arange(128)[None, None, :]
    vis = (k // 64) <= (q // 64)
    bm = -(sl[None, :, None]) * np.abs(q - k) - sl[None, :, None] * (127.0 - q)
    bmat_p = np.where(vis, bm, -30000.0)
    ii = np.arange(16)[None, None, :]
    btab_s = -(sl[None, :, None]) * (2111.0 - 128.0 * ii - kk)
    k6 = np.arange(64)[:, None, None]
    q6 = np.arange(64)[None, None, :]
    bmat_s = -(sl[None, :, None]) * np.abs(q6 - k6) - sl[None, :, None] * (63.0 - q6)
    bmat_s_full = np.zeros((128, 8, 64))
    bmat_s_full[:64] = bmat_s
    tri = (np.arange(128)[:, None] <= np.arange(128)[None, :]).astype(np.float32)
    scanmask = np.ones((128, 512), np.float32)
    scanmask[:, ::128] = 0.0
    return dict(
        btab_p=btab_p.astype(np.float32), bmat_p=bmat_p.astype(np.float32),
        btab_s=btab_s.astype(np.float32), bmat_s=bmat_s_full.astype(np.float32),
        tri=tri, scanmask=scanmask, ident=np.eye(128, dtype=np.float32),
    )


def build_program(NP, NS, NL=4):
    nc = bass.Bass("TRN2", target_bir_lowering=False)

    def din(name, shape):
        return nc.dram_tensor(name, list(shape), F32, kind="ExternalInput").ap()

    def dout(name, shape):
        return nc.dram_tensor(name, list(shape), F32, kind="ExternalOutput").ap()

    xp_d = din("xp", [max(NP, 1), SEQ, D])
    xs_d = din("xs", [max(NS, 1), DEC_SEQ, D])
    ck_d = din("ck", [2, max(NS, 1), PAST, D])
    cv_d = din("cv", [2, max(NS, 1), PAST, D])
    st_d = din("st", [2, max(NS, 1), 8, 128, 128])
    normwT_d = din("normwT", [128, 4, 8])
    hwin_d = din("hwin", [2, D, 4 * D])
    lbT_d = din("lbT", [128, 2, 8])
    onwT_d = din("onwT", [128, 2, 8])
    hwo_d = din("hwo", [2, D, D])
    awin_d = din("awin", [2, D, 4 * D])
    qkn_d = din("qkn", [128, 2, 2, 64])
    lam_d = din("lam", [128, 2, 4, 64])
    sub_d = din("sub", [128, 2, 128])
    awo_d = din("awo", [2, D, D])
    btab_p_d = din("btab_p", [128, 8, 16])
    bmat_p_d = din("bmat_p", [128, 8, 128])
    btab_s_d = din("btab_s", [128, 8, 16])
    bmat_s_d = din("bmat_s", [128, 8, 64])
    tri_d = din("tri", [128, 128])
    scanmask_d = din("scanmask", [128, 512])
    ident_d = din("ident", [128, 128])

    yp_d = dout("yp", [max(NP, 1), SEQ, D])
    ys_d = dout("ys", [max(NS, 1), DEC_SEQ, D])
    nkp_d = dout("nkp", [2, max(NP, 1), SEQ, D])
    nvp_d = dout("nvp", [2, max(NP, 1), SEQ, D])
    nks_d = dout("nks", [2, max(NS, 1), DEC_SEQ, D])
    nvs_d = dout("nvs", [2, max(NS, 1), DEC_SEQ, D])
    nsp_d = dout("nsp", [2, max(NP, 1), 8, 128, 128])
    nss_d = dout("nss", [2, max(NS, 1), 8, 128, 128])

    S = Sched()
    uid = [0]

    with contextlib.ExitStack() as stk:
        def sb(name, shape, dt):
            return stk.enter_context(nc.sbuf_tensor("s_" + name, list(shape), dt))

        def psb(name, shape, dt):
            return stk.enter_context(nc.psum_tensor("p_" + name, list(shape), dt))

        xres = sb("xres", [128, 16, D], F32)
        hT = sb("hT", [128, 8, SEQ], BF16)
        uT = sb("uT", [128, 8, SEQ], BF16)
        wbuf_t = sb("wbuf", [128, 8, 1024], BF16)
        wbuf = [wbuf_t[:, :, 0:512], wbuf_t[:, :, 512:1024]]
        hb = [sb("hb0", [128, D], BF16)] * 2
        ssx = sb("ssx", [128, 16], F32)
        rstdx = sb("rstdx", [128, 16], F32)
        ssacc = sb("ssacc", [128, 16], F32)
        rstdo = sb("rstdo", [128, 16], F32)
        bar_d = sb("bar_d", [128, 4], F32)
        identb = sb("identb", [128, 128], BF16)
        tri = sb("tri", [128, 128], F32)
        scanmask = sb("scanmask", [128, 512], F32)
        ones1 = sb("ones1", [128, 1], F32)
        normwT = sb("normwT", [128, 4, 8], F32)
        lbraw = sb("lbraw", [128, 2, 8], F32)
        lbT = sb("lbT", [128, 2, 8], F32)
        omlT = sb("omlT", [128, 2, 8], F32)
        onwT = sb("onwT", [128, 2, 8], F32)
        qkn = sb("qkn", [128, 2, 2, 64], F32)
        lams = sb("lams", [128, 2, 2], F32)
        lame = sb("lame", [128, 2, 2], F32)
        neglam = sb("neglam", [128, 2], F32)
        wsub = sb("wsub", [128, 2, 128], F32)
        btab_p = sb("btab_p", [128, 8, 16], F32)
        bmat_p = sb("bmat_p", [128, 8, 128], F32)
        btab_s = sb("btab_s", [128, 8, 16], F32)
        bmat_s = sb("bmat_s", [128, 8, 64], F32)
        Sall = sb("Sall", [128, 17, 128], BF16)
        S32 = [sb("S32a", [128, 128], F32), sb("S32b", [128, 128], F32)]
        SCR_BYTES = 43008
        scr = sb("scr", [128, SCR_BYTES // 2], BF16)

        class Carver:
            def __init__(self):
                self.off = 0

            def __call__(self, shape, dt):
                n = 1
                for x in shape[1:]:
                    n *= x
                nb = n * (4 if dt == F32 else 2)
                nb_al = (nb + 31) // 32 * 32
                assert self.off + nb_al <= SCR_BYTES, (self.off, nb_al)
                ap = scr[:, self.off // 2:(self.off + nb) // 2]
                self.off += nb_al
                if dt == F32:
                    ap = ap.bitcast(F32)
                if len(shape) == 3:
                    ap = ap.rearrange("p (a b) -> p a b", a=shape[1])
                elif len(shape) == 4:
                    ap = ap.rearrange("p (a b c) -> p a b c", a=shape[1], b=shape[2])
                assert tuple(ap.shape) == tuple(shape), (ap.shape, shape)
                return ap

        cv_ = Carver()
        junk = cv_([128, D], BF16)
        base_off = cv_.off
        lamv = cv_([128, 2, 4, 64], F32)
        lamp = cv_([128, 2, 2, 64], F32)
        cv_.off = base_off
        qf = cv_([128, 512], F32)
        ff = cv_([128, 512], F32)
        gg = cv_([128, 512], F32)
        kf = cv_([128, 512], F32)
        bbuf = cv_([128, 512], F32)
        ee = [cv_([128, 512], F32), cv_([128, 512], F32)]
        ex = [cv_([128, 512], F32), cv_([128, 512], F32)]
        osq = cv_([128, 512], F32)
        qt = cv_([128, 512], BF16)
        qh = cv_([128, 512], BF16)
        khT = cv_([128, 512], BF16)
        szT = cv_([128, 512], BF16)
        Kb = [cv_([128, 4, 128], BF16) for i in range(4)]
        khtm = cv_([128, 4, 128], BF16)
        vtm = cv_([128, 4, 128], BF16)
        attm = cv_([128, 4, 128], BF16)
        dec = cv_([128, 8], F32)
        hg_end = cv_.off
        cv_.off = base_off
        kT = cv_([128, 2, SEQ + 64], BF16)
        vaug = cv_([128, 17, 2, 129], BF16)
        kctm = cv_([128, 16, 2, 128], BF16)
        sq = cv_([128, 256], F32)
        ssq4 = cv_([128, 8], F32)
        rs4 = cv_([128, 8], F32)
        kn = cv_([128, 256], F32)
        kout = [cv_([128, 256], F32), cv_([128, 256], F32)]
        vout = [cv_([128, 256], F32), cv_([128, 256], F32)]
        qkb = cv_([128, 256], BF16)
        qTt = [cv_([128, 2, 128], BF16), cv_([128, 2, 128], BF16)]
        szt = [cv_([128, 256], BF16), cv_([128, 256], BF16)]
        Pt = [cv_([128, 2, 128], BF16) for i in range(4)]
        dtmp = cv_([128, 2, 128], F32)
        rr = cv_([128, 8], F32)
        rl = cv_([128, 8], F32)
        t1 = cv_([128, 128], F32)
        oo = cv_([128, 128], F32)
        ssn = cv_([128, 8], F32)
        rsn = cv_([128, 8], F32)
        ug = cv_([128, 128], F32)
        ub = cv_([128, 128], BF16)
        sgt = cv_([128, 256], F32)
        at_end = cv_.off
        print("scratch bytes: hgrn", hg_end, "attn", at_end)
        pb = [psb("pb%d" % i, [128, 512], F32) for i in range(6)]
        pt = [psb("pt%d" % i, [128, 1024], BF16) for i in range(2)]

        def fsz(ap):
            n = 1
            for x in tuple(ap.shape)[1:]:
                n *= x
            return float(n)

        def mm(out, lhsT, rhs, start, stop, r, w):
            nn = fsz(out)
            c = max(nn, 64.0) / 2.2 + 12.0
            if lhsT.dtype == F32:
                c *= 4.0
            S.op("pe", lambda e: e.matmul(out, lhsT=lhsT, rhs=rhs, start=start, stop=stop), r, w, cost=c)

        def tr(out, in_, ident, r, w):
            S.op("pe", lambda e: e.transpose(out=out, in_=in_, identity=ident), r, w, cost=max(fsz(out), 64.0) / 2.2 + 40.0)

        def act(out, in_, func, r, w, bias=None, scale=None, accum=None):
            kw = {}
            if bias is not None:
                kw["bias"] = bias
            if scale is not None:
                kw["scale"] = scale
            if accum is not None:
                kw["accum_out"] = accum
            S.op("act", lambda e: e.activation(out=out, in_=in_, func=func, **kw), r, w, cost=(224.0 + fsz(out)) / 1.4)

        def vcost(eng, out):
            if eng == "pool":
                return 120.0 + 2.1 * fsz(out)
            return (70.0 + fsz(out)) / 0.96

        def tt(eng, out, in0, in1, op, r, w):
            S.op(eng, lambda e: e.tensor_tensor(out=out, in0=in0, in1=in1, op=op), r, w, cost=vcost(eng, out))

        def tsc(eng, out, in0, s1, s2, op0, op1, r, w):
            if op1 is None:
                S.op(eng, lambda e: e.tensor_scalar(out=out, in0=in0, scalar1=s1, scalar2=None, op0=op0), r, w,
                     cost=vcost(eng, out))
            else:
                S.op(eng, lambda e: e.tensor_scalar(out=out, in0=in0, scalar1=s1, scalar2=s2, op0=op0, op1=op1), r, w,
                     cost=vcost(eng, out))

        def stt(eng, out, in0, scalar, in1, op0, op1, r, w):
            S.op(eng, lambda e: e.scalar_tensor_tensor(out=out, in0=in0, scalar=scalar, in1=in1, op0=op0, op1=op1), r, w,
                 cost=vcost(eng, out))

        def cp(eng, out, in_, r, w):
            if eng == "act":
                S.op("act", lambda e: e.copy(out=out, in_=in_), r, w, cost=(224.0 + fsz(out)) / 1.4)
            else:
                S.op(eng, lambda e: e.tensor_copy(out=out, in_=in_), r, w, cost=vcost(eng, out))

        def mset(eng, ap, val, w):
            S.op(eng, lambda e: e.memset(ap, val), (), w, cost=vcost(eng, ap) * 0.5)

        def dma(eng, out, in_, r, w):
            nb = fsz(out) * float(out.shape[0]) * 4.0
            S.op(eng, lambda e: e.dma_start(out=out, in_=in_), r, w, dma=True, cost=2000.0 + nb / 120.0)

        def rstd_ops(eng, out, in_, n, r, w):
            tsc(eng, out, in_, 1.0 / n, EPS, ALU.mult, ALU.add, r, w)
            act(out, out, AF.Ln, w, w)
            act(out, out, AF.Exp, w, w, scale=-0.5)

        dma("sp", tri[:], tri_d, (), ["tri"])
        dma("sp", scanmask[:], scanmask_d, (), ["scanmask"])
        dma("pool", identb[:], ident_d, (), ["identb"])
        dma("sp", normwT[:], normwT_d, (), ["normwT"])
        dma("sp", lbraw[:], lbT_d, (), ["lbraw"])
        dma("sp", onwT[:], onwT_d, (), ["onwT"])
        dma("sp", qkn[:], qkn_d, (), ["qkn"])
        dma("sp", lamv[:], lam_d, (), ["lamv"])
        dma("sp", wsub[:], sub_d, (), ["wsub"])
        dma("sp", btab_p[:], btab_p_d, (), ["btab_p"])
        dma("sp", bmat_p[:], bmat_p_d, (), ["bmat_p"])
        dma("sp", btab_s[:], btab_s_d, (), ["btab_s"])
        dma("sp", bmat_s[:], bmat_s_d, (), ["bmat_s"])
        mset("dve", ones1[:], 1.0, ["ones1"])
        mset("dve", lbT[:], 0.0, ["lbT"])
        tt("dve", lbT[:, 1, :], lbraw[:, 1, :], lbraw[:, 0, :], ALU.subtract, ["lbraw"], ["lbT"])
        act(lbT[:, 1, :], lbT[:, 1, :], AF.Exp, ["lbT"], ["lbT"], scale=-1.0)
        tsc("dve", lbT[:, 1, :], lbT[:, 1, :], 1.0, None, ALU.add, None, ["lbT"], ["lbT"])
        S.op("dve", lambda e: e.reciprocal(out=lbT[:, 1, :], in_=lbT[:, 1, :]), ["lbT"], ["lbT"])
        tsc("dve", omlT[:], lbT[:], -1.0, 1.0, ALU.mult, ALU.add, ["lbT"], ["omlT"])
        tt("dve", lamp[:, :, 0, :], lamv[:, :, 0, :], lamv[:, :, 1, :], ALU.mult, ["lamv"], ["lamp"])
        tt("dve", lamp[:, :, 1, :], lamv[:, :, 2, :], lamv[:, :, 3, :], ALU.mult, ["lamv"], ["lamp"])
        S.op("dve", lambda e: e.tensor_reduce(out=lams[:], in_=lamp[:], axis=AX.X, op=ALU.add), ["lamp"], ["lams"])
        act(lame[:], lams[:], AF.Exp, ["lams"], ["lame"])
        for j in range(2):
            lam_init = 0.8 - 0.6 * math.exp(-0.3 * (2 * j + 1))
            tt("dve", neglam[:, j:j + 1], lame[:, j, 1:2], lame[:, j, 0:1], ALU.subtract, ["lame"], ["neglam"])
            tsc("dve", neglam[:, j:j + 1], neglam[:, j:j + 1], -lam_init, None, ALU.add, None, ["neglam"], ["neglam"])
            tsc("dve", wsub[:, j, :], wsub[:, j, :], 1.0 - lam_init, None, ALU.mult, None, ["wsub"], ["wsub"])

        def fence():
            S.barrier({
                "act": lambda e: e.copy(out=bar_d[:, 0:1], in_=ones1[:, 0:1]),
                "dve": lambda e: e.memset(bar_d[:, 1:2], 0.0),
                "pool": lambda e: e.memset(bar_d[:, 2:3], 0.0),
            })

        def phase_norm(l, T):
            TP = min(T, 128)
            NT = T // TP
            mset("dve", ssx[:], 0.0, ["ssx"])
            for t in range(NT):
                act(junk[:TP, :], xres[:TP, t, :], AF.Square, [("x", t)], ["junk", "ssx"], accum=ssx[:TP, t:t + 1])
            rstd_ops("dve", rstdx[:TP, :NT], ssx[:TP, :NT], float(D), ["ssx"], ["rstdx"])
            for t in range(NT):
                hbt = hb[t % 2]
                act(hbt[:TP, :], xres[:TP, t, :], AF.Identity, [("x", t), "rstdx"], [("hb", 0)],
                    scale=rstdx[:TP, t:t + 1])
                ptb = pt[t % 2]
                ptv = ptb[:, :].rearrange("p (a b) -> p a b", a=8)
                for kc in range(8):
                    tr(ptv[:, kc, :TP], hbt[:TP, kc * 128:(kc + 1) * 128], identb[:TP, :TP],
                       [("hb", 0), "identb"], [("pt", t % 2)])
                tt("dve", hT[:, :, t * TP:(t + 1) * TP], ptv[:, :, :TP],
                   normwT[:, l, :].unsqueeze(2).broadcast_to([128, 8, TP]), ALU.mult,
                   [("pt", t % 2), "normwT"], [("hT", t)])

        def phase_outproj(w_d, j, T, use_rstd):
            TP = min(T, 128)
            NT = T // TP
            wv = w_d[j].rearrange("(kc p) n -> p kc n", p=128)
            for half in range(2):
                dma("pool", wbuf[half][:, :, 0:512], wv[:, :, half * 512:(half + 1) * 512], (), [("wbuf", half)])
            for t in range(NT):
                for half in range(2):
                    bank = pb[(2 * t + half) % 2]
                    bk = ("pb", (2 * t + half) % 2)
                    for kc in range(8):
                        mm(bank[:TP, :], uT[:, kc, t * TP:(t + 1) * TP], wbuf[half][:, kc, 0:512],
                           kc == 0, kc == 7, ["uT", ("wbuf", half)], [bk])
                    xo = xres[:TP, t, half * 512:(half + 1) * 512]
                    if use_rstd:
                        stt("dve", xo, bank[:TP, :], rstdo[:TP, t:t + 1], xo, ALU.mult, ALU.add,
                            [bk, "rstdo", ("x", t)], [("x", t)])
                    else:
                        tt("dve", xo, bank[:TP, :], xo, ALU.add, [bk, ("x", t)], [("x", t)])

        def hgrn_layer(l, T, kind, si):
            j = l // 2
            CH = min(T, 128)
            NCHT = T // CH
            SEGT = min(T, 512)
            NSEG = T // SEGT
            NCH = SEGT // CH
            NSB = CH // 32
            fence()
            for i in range(4):
                mset("pool", Kb[i], 0.0, [("Kb", i)])
            phase_norm(l, T)
            if STOP == 1:
                return
            wv = hwin_d[j].rearrange("(kc p) n -> p kc n", p=128)
            for h in range(8):
                slot = h % 2
                wb = wbuf[slot]
                wk = ("wbuf", slot)
                for qi in range(4):
                    dma("pool", wb[:, :, qi * 128:(qi + 1) * 128],
                        wv[:, :, qi * 1024 + h * 128: qi * 1024 + (h + 1) * 128], (), [wk])
                s32 = S32[h % 2]
                sk = ("S32", h % 2)
                if kind == "p":
                    mset("pool", s32[:], 0.0, [sk])
                    mset("pool", Sall[:, 0, :], 0.0, ["Sall"])
                else:
                    dma("sp", s32[:], st_d[j, si, h], (), [sk])
                    cp("pool", Sall[:, 0, :], s32[:], [sk], ["Sall"])
                for seg in range(NSEG):
                    t0 = seg * SEGT
                    for qi, bi in ((0, 0), (1, 1), (3, 2)):
                        for kc in range(8):
                            mm(pb[bi][:, :SEGT], wb[:, kc, qi * 128:(qi + 1) * 128], hT[:, kc, t0:t0 + SEGT],
                               kc == 0, kc == 7, [wk] + [("hT", tt_) for tt_ in range(t0 // CH, (t0 + SEGT) // CH)], [("pb", bi)])
                    act(qf[:, :SEGT], pb[0][:, :SEGT], AF.Identity, [("pb", 0)], ["qf"], scale=128.0 ** -0.5)
                    act(gg[:, :SEGT], pb[1][:, :SEGT], AF.Exp, [("pb", 1)], ["gg"], scale=-1.0)
                    act(gg[:, :SEGT], gg[:, :SEGT], AF.Ln, ["gg"], ["gg"], bias=1.0)
                    act(ff[:, :SEGT], gg[:, :SEGT], AF.Exp, ["gg"], ["ff"], scale=-1.0)
                    act(ex[1][:, :SEGT], pb[2][:, :SEGT], AF.Exp, [("pb", 2)], [("ex", 1)], scale=-1.0)
                    tsc("dve", ex[1][:, :SEGT], ex[1][:, :SEGT], 1.0, None, ALU.add, None, [("ex", 1)], [("ex", 1)])
                    S.op("dve", lambda e, _a=ex[1][:, :SEGT]: e.reciprocal(out=_a, in_=_a), [("ex", 1)], [("ex", 1)],
                         cost=(70.0 + SEGT) / 0.96)
                    tt("dve", szT[:, :SEGT], ex[1][:, :SEGT], pb[2][:, :SEGT], ALU.mult, [("ex", 1), ("pb", 2)], ["szT"])
                    tsc("dve", ff[:, :SEGT], ff[:, :SEGT], omlT[:, j, h:h + 1], lbT[:, j, h:h + 1], ALU.mult, ALU.add,
                        ["ff", "omlT", "lbT"], ["ff"])
                    act(gg[:, :SEGT], ff[:, :SEGT], AF.Ln, ["ff"], ["gg"])
                    tsc("pool", kf[:, :SEGT], ff[:, :SEGT], -1.0, 1.0, ALU.mult, ALU.add, ["ff"], ["kf"])
                    pv = pb[3][:, :].rearrange("p (c v) -> p c v", c=4)
                    for c in range(NCH):
                        for kc in range(8):
                            mm(pv[:CH, c, :], hT[:, kc, t0 + c * CH:t0 + (c + 1) * CH], wb[:, kc, 256:384],
                               kc == 0, kc == 7, [wk, ("hT", t0 // CH + c)], [("pb", 3)])
                    cp("act", vtm[:CH, :NCH, :], pv[:CH, :NCH, :], [("pb", 3)], ["vtm"])
                    if STOP == 2:
                        return
                    S.op("dve", lambda e, _o=bbuf[:, :SEGT], _m=scanmask[:, :SEGT], _g=gg[:, :SEGT]:
                         e.tensor_tensor_scan(out=_o, data0=_m, data1=_g, initial=0.0, op0=ALU.mult, op1=ALU.add),
                         ["scanmask", "gg"], ["bb"], cost=(70.0 + SEGT) / 0.96)
                    b3 = bbuf[:, :SEGT].rearrange("p (c t) -> p c t", c=NCH)
                    b4 = bbuf[:, :SEGT].rearrange("p (c i t) -> p c i t", c=NCH, i=NSB)
                    e0 = ee[0]
                    e04 = e0[:, :SEGT].rearrange("p (c i t) -> p c i t", c=NCH, i=NSB)
                    if NSB > 1:
                        tt("dve", e04[:, :, 1:NSB, :], b4[:, :, 1:NSB, :],
                           b4[:, :, 0:NSB - 1, 31:32].broadcast_to([128, NCH, NSB - 1, 32]), ALU.subtract,
                           ["bb"], [("ee", 0)])
                    cp("pool", e04[:, :, 0, :], b4[:, :, 0, :], ["bb"], [("ee", 0)])
                    act(ex[0][:, :SEGT], e0[:, :SEGT], AF.Exp, [("ee", 0)], [("ex", 0)])
                    tt("dve", qt[:, :SEGT], qf[:, :SEGT], ex[0][:, :SEGT], ALU.mult, ["qf", ("ex", 0)], ["qt"])
                    act(ex[1][:, :SEGT], bbuf[:, :SEGT], AF.Exp, ["bb"], [("ex", 1)])
                    tt("pool", qh[:, :SEGT], qf[:, :SEGT], ex[1][:, :SEGT], ALU.mult, ["qf", ("ex", 1)], ["qh"])
                    if STOP == 3:
                        return
                    kf3 = kf[:, :SEGT].rearrange("p (c t) -> p c t", c=NCH)
                    for i in range(NSB):
                        wi = 32 * (i + 1)
                        b_ = i % 2
                        ex3 = ex[b_][:, :SEGT].rearrange("p (c t) -> p c t", c=NCH)
                        if i == 0:
                            act(ex3[:, :, 0:wi], b3[:, :, 0:wi], AF.Exp, ["bb"], [("ex", b_)], scale=-1.0)
                        else:
                            ee3 = ee[b_][:, :SEGT].rearrange("p (c t) -> p c t", c=NCH)
                            tt("dve", ee3[:, :, 0:wi], b3[:, :, 32 * i - 1:32 * i].broadcast_to([128, NCH, wi]),
                               b3[:, :, 0:wi], ALU.subtract, ["bb"], [("ee", b_)])
                            act(ex3[:, :, 0:wi], ee3[:, :, 0:wi], AF.Exp, [("ee", b_)], [("ex", b_)])
                        tt("dve" if i % 2 == 0 else "pool", Kb[i][:, :NCH, 0:wi], kf3[:, :, 0:wi], ex3[:, :, 0:wi], ALU.mult,
                           ["kf", ("ex", b_)], [("Kb", i)])
                    ee3 = ee[0][:, :SEGT].rearrange("p (c t) -> p c t", c=NCH)
                    tt("dve", ee3[:, :, :], b3[:, :, CH - 1:CH].broadcast_to([128, NCH, CH]), b3[:, :, :], ALU.subtract,
                       ["bb"], [("ee", 0)])
                    act(ex[0][:, :SEGT], ee[0][:, :SEGT], AF.Exp, [("ee", 0)], [("ex", 0)])
                    tt("pool", khT[:, :SEGT], kf[:, :SEGT], ex[0][:, :SEGT], ALU.mult, ["kf", ("ex", 0)], ["khT"])
                    act(dec[:, :NCH], b3[:, :, CH - 1], AF.Exp, ["bb"], ["dec"])
                    if STOP == 4:
                        return
                    ptv = pt[0][:, 0:512].rearrange("p (c k) -> p c k", c=4)
                    for c in range(NCH):
                        tr(ptv[:CH, c, :], khT[:, c * CH:(c + 1) * CH], identb[:, :], ["khT", "identb"], [("pt", 0)])
                    cp("act", khtm[:CH, :NCH, :], ptv[:CH, :NCH, :], [("pt", 0)], ["khtm"])
                    if STOP == 5:
                        return
                    for c in range(NCH):
                        cg = seg * NCH + c
                        ub_ = 4 if c % 2 == 0 else 2
                        pu = pb[ub_][:, 0:128]
                        mm(pu, khtm[:CH, c, :], vtm[:CH, c, :], True, True, ["khtm", "vtm"], [("pb", ub_)])
                        stt("dve", s32[:], s32[:], dec[:, c:c + 1], pu, ALU.mult, ALU.add,
                            [sk, "dec", ("pb", ub_)], [sk])
                        cp("pool", Sall[:, cg + 1, :], s32[:], [sk], ["Sall"])
                    if STOP == 6:
                        return
                    pa = pb[5][:, :].rearrange("p (c t) -> p c t", c=4)
                    for c in range(NCH):
                        for i in range(NSB):
                            mm(pa[:CH, c, 32 * i:32 * (i + 1)], Kb[i][:, c, 0:CH], qt[:, c * CH + 32 * i:c * CH + 32 * (i + 1)],
                               True, True, [("Kb", i), "qt"], [("pb", 5)])
                    tt("dve", attm[:CH, :NCH, :CH], pa[:CH, :NCH, :CH],
                       tri[:CH, :CH].unsqueeze(1).broadcast_to([CH, NCH, CH]), ALU.mult,
                       [("pb", 5), "tri"], ["attm"])
                    if STOP == 7:
                        return
                    po = pb[0]
                    for c in range(NCH):
                        cg = seg * NCH + c
                        mm(po[:, c * CH:(c + 1) * CH], vtm[:CH, c, :], attm[:CH, c, :CH], True, False,
                           ["vtm", "attm"], [("pb", 0)])
                        mm(po[:, c * CH:(c + 1) * CH], Sall[:, cg, :], qh[:, c * CH:(c + 1) * CH], False, True,
                           ["Sall", "qh"], [("pb", 0)])
                    if STOP == 8:
                        return
                    act(osq[:, :SEGT], po[:, :SEGT], AF.Identity, [("pb", 0)], ["osq", "po_rd"])
                    tt("dve", osq[:, :SEGT], osq[:, :SEGT], osq[:, :SEGT], ALU.mult, ["osq"], ["osq"])
                    if STOP == 12:
                        return
                    stt("dve", uT[:, h, t0:t0 + SEGT], po[:, :SEGT], onwT[:, j, h:h + 1], szT[:, :SEGT], ALU.mult, ALU.mult,
                        [("pb", 0), "onwT", "szT", "po_rd"], ["uT"])
                    if STOP == 9:
                        return
                    pss = pb[1]
                    for c in range(NCH):
                        mm(pss[:CH, c:c + 1], osq[:, c * CH:(c + 1) * CH], ones1[:, 0:1], True, True,
                           ["osq", "ones1"], [("pb", 1)])
                    if STOP == 10:
                        return
                    cg0 = seg * NCH
                    if h == 0:
                        cp("dve", ssacc[:CH, cg0:cg0 + NCH], pss[:CH, 0:NCH], [("pb", 1)], ["ssacc"])
                    else:
                        tt("dve", ssacc[:CH, cg0:cg0 + NCH], pss[:CH, 0:NCH], ssacc[:CH, cg0:cg0 + NCH], ALU.add,
                           [("pb", 1), "ssacc"], ["ssacc"])
                if STOP == 11:
                    return
                od = nsp_d if kind == "p" else nss_d
                dma("sp", od[j, si, h], s32[:], [sk], ())
            rstd_ops("dve", rstdo[:CH, :NCHT], ssacc[:CH, :NCHT], float(D), ["ssacc"], ["rstdo"])
            phase_outproj(hwo_d, j, T, True)

        def attn_layer(l, T, kind, si):
            j = l // 2
            TP = min(T, 128)
            NT = T // TP
            fence()
            mset("pool", vaug, 1.0, ["vaug"])
            phase_norm(l, T)
            wv = awin_d[j].rearrange("(kc p) n -> p kc n", p=128)
            nk_d = nkp_d if kind == "p" else nks_d
            nv_d = nvp_d if kind == "p" else nvs_d
            ktile0 = 16 if kind == "s" else 0
            for hp in range(4):
                wb = wbuf_t
                wk = ("wbuf", 0)
                wk1 = ("wbuf", 1)
                cols = [(0, hp * 128, 128), (128, 512 + hp * 128, 128), (256, 1024 + hp * 128, 128),
                        (384, 1536 + hp * 128, 128), (512, 2048 + hp * 256, 256), (768, 3072 + hp * 256, 256)]
                for (o, c0, n) in cols:
                    dma("pool", wb[:, :, o:o + n], wv[:, :, c0:c0 + n], (), [wk, wk1])
                if kind == "s":
                    ckv = ck_d[j, si].rearrange("(t p) n -> p t n", p=128)
                    cvv = cv_d[j, si].rearrange("(t p) n -> p t n", p=128)
                    for m in range(2):
                        dma("pool", kctm[:, :, m, :], ckv[:, :, m * 512 + hp * 128:m * 512 + (hp + 1) * 128], (), ["kctm"])
                    for hh in range(2):
                        dma("pool", vaug[:, 0:16, hh, 0:128], cvv[:, :, hp * 256 + hh * 128:hp * 256 + (hh + 1) * 128],
                            (), ["vaug"])
                    for t in range(16):
                        ptb = pt[t % 2]
                        ptv = ptb[:, 0:256].rearrange("p (m k) -> p m k", m=2)
                        for m in range(2):
                            tr(ptv[:, m, :], kctm[:, t, m, :], identb[:, :], ["kctm", "identb"], [("pt", t % 2)])
                        cp("act" if t % 2 == 0 else "dve", kT[:, :, t * 128:(t + 1) * 128], ptv[:, :, :],
                           [("pt", t % 2)], ["kT"])
                for t in range(NT):
                    bank = pb[t % 2]
                    bk = ("pb", t % 2)
                    for kc in range(8):
                        mm(bank[:TP, :], hT[:, kc, t * TP:(t + 1) * TP], wb[:, kc, 256:768], kc == 0, kc == 7,
                           [wk, wk1, ("hT", t)], [bk])
                    qk_norm(bank, bk, TP, j, 1)
                    ko = kout[t % 2]
                    tt("pool", ko[:TP, :].rearrange("p (g d) -> p g d", g=4), kn[:TP, :].rearrange("p (g d) -> p g d", g=4),
                       qkn[:TP, j, 1, :].unsqueeze(1).broadcast_to([TP, 4, 64]), ALU.mult, ["kn", "qkn"], [("kout", t % 2)])
                    for m in range(2):
                        dma("sp", nk_d[j, si, t * TP:(t + 1) * TP, m * 512 + hp * 128:m * 512 + (hp + 1) * 128],
                            ko[:TP, m * 128:(m + 1) * 128], [("kout", t % 2)], ())
                    cp("act", qkb[:TP, :], ko[:TP, :], [("kout", t % 2)], ["qkb"])
                    ptv = pt[t % 2][:, 0:256].rearrange("p (m k) -> p m k", m=2)
                    for m in range(2):
                        tr(ptv[:, m, :TP], qkb[:TP, m * 128:(m + 1) * 128], identb[:TP, :TP], ["qkb", "identb"],
                           [("pt", t % 2)])
                    kt0 = ktile0 * 128 + t * TP
                    cp("dve", kT[:, :, kt0:kt0 + TP], ptv[:, :, :TP], [("pt", t % 2)], ["kT"])
                    vo = vout[t % 2]
                    cp("act", vo[:TP, :], bank[:TP, 256:512], [bk], [("vout", t % 2)])
                    dma("sp", nv_d[j, si, t * TP:(t + 1) * TP, hp * 256:(hp + 1) * 256], vo[:TP, :], [("vout", t % 2)], ())
                    cp("pool", vaug[:TP, ktile0 + t, :, 0:128], vo[:TP, :].rearrange("p (h e) -> p h e", h=2),
                       [("vout", t % 2)], ["vaug"])
                for jt in range(NT):
                    bank = pb[jt % 2]
                    bk = ("pb", jt % 2)
                    for kc in range(8):
                        mm(bank[:TP, 0:256], hT[:, kc, jt * TP:(jt + 1) * TP], wb[:, kc, 0:256], kc == 0, kc == 7,
                           [wk, wk1, ("hT", jt)], [bk])
                    for kc in range(8):
                        mm(bank[:TP, 256:512], hT[:, kc, jt * TP:(jt + 1) * TP], wb[:, kc, 768:1024], kc == 0, kc == 7,
                           [wk, wk1, ("hT", jt)], [bk])
                    qk_norm(bank, bk, TP, j, 0)
                    tt("pool", qkb[:TP, :].rearrange("p (g d) -> p g d", g=4), kn[:TP, :].rearrange("p (g d) -> p g d", g=4),
                       qkn[:TP, j, 0, :].unsqueeze(1).broadcast_to([TP, 4, 64]), ALU.mult, ["kn", "qkn"], ["qkb"])
                    ptv = pt[jt % 2][:, 0:256].rearrange("p (m k) -> p m k", m=2)
                    for m in range(2):
                        tr(ptv[:, m, :TP], qkb[:TP, m * 128:(m + 1) * 128], identb[:TP, :TP], ["qkb", "identb"],
                           [("pt", jt % 2)])
                    qT_ = qTt[jt % 2]
                    cp("dve", qT_[:, :, :TP], ptv[:, :, :TP], [("pt", jt % 2)], [("qTt", jt % 2)])
                    sz_ = szt[jt % 2]
                    act(sgt[:TP, :], bank[:TP, 256:512], AF.Exp, [bk], ["sgt"], scale=-1.0)
                    tsc("dve", sgt[:TP, :], sgt[:TP, :], 1.0, None, ALU.add, None, ["sgt"], ["sgt"])
                    S.op("dve", lambda e, _a=sgt[:TP, :]: e.reciprocal(out=_a, in_=_a), ["sgt"], ["sgt"], cost=350.0)
                    tt("dve", sz_[:TP, :], sgt[:TP, :], bank[:TP, 256:512], ALU.mult, ["sgt", bk], [("szt", jt % 2)])
                    for hh in range(2):
                        h = 2 * hp + hh
                        slope_h = _slopes()[h]
                        if kind == "p":
                            ktiles = [(i, 128, "off" if i < jt else "diag") for i in range(jt + 1)
                                      if i == jt or slope_h * ((jt - i - 1) * 128 + 1) <= SKIP_T]
                        else:
                            ktiles = [(i, 128, "off") for i in range(16)
                                      if slope_h * (PAST - (128 * i + 127)) <= SKIP_T] + [(16, 64, "diag")]
                        r0 = 64 * hh
                        pos = [pb[4 + m][:, hh * 129:(hh + 1) * 129] for m in range(2)]
                        poks = [("pb", 4), ("pb", 5)]
                        for idx, (i, nk, typ) in enumerate(ktiles):
                            sslot = uid[0] % 4
                            uid[0] += 1
                            psb_ = pb[2 + sslot % 2]
                            ps = psb_[:, 0:256].rearrange("p (m q) -> p m q", m=2)
                            psk = ("pb", 2 + sslot % 2)
                            for m in range(2):
                                mm(ps[:nk, m, :TP], kT[r0:r0 + 64, m, i * 128:i * 128 + nk], qT_[r0:r0 + 64, m, :TP],
                                   True, True, ["kT", ("qTt", jt % 2)], [psk])
                            P_ = Pt[sslot]
                            pk = ("Pt", sslot)
                            if typ == "off":
                                bias = btab_p[:nk, h, jt - i:jt - i + 1] if kind == "p" else btab_s[:nk, h, i:i + 1]
                                act(P_[:nk, :, :TP], ps[:nk, :, :TP], AF.Exp, [psk, "btab_p", "btab_s"], [pk],
                                    bias=bias, scale=0.125)
                            else:
                                bm = bmat_p[:nk, h, :TP] if kind == "p" else bmat_s[:nk, h, :TP]
                                stt("dve", dtmp[:nk, :, :TP], ps[:nk, :, :TP], 0.125,
                                    bm.unsqueeze(1).broadcast_to([nk, 2, TP]), ALU.mult, ALU.add,
                                    [psk, "bmat_p", "bmat_s"], ["dtmp"])
                                act(P_[:nk, :, :TP], dtmp[:nk, :, :TP], AF.Exp, ["dtmp"], [pk])
                            for m in range(2):
                                mm(pos[m][:TP, :], P_[:nk, m, :TP], vaug[:nk, i, hh, :], idx == 0, idx == len(ktiles) - 1,
                                   [pk, "vaug"], [poks[m]])
                        for m in range(2):
                            S.op("dve", lambda e, _o=rr[:TP, m:m + 1], _i=pos[m][:TP, 128:129]: e.reciprocal(out=_o, in_=_i),
                                 [poks[m]], ["rr"])
                        tt("dve", rl[:TP, 0:1], rr[:TP, 1:2], neglam[:TP, j:j + 1], ALU.mult, ["rr", "neglam"], ["rl"])
                        tsc("dve", t1[:TP, :], pos[1][:TP, 0:128], rl[:TP, 0:1], None, ALU.mult, None, [poks[1], "rl"], ["t1"])
                        stt("dve", oo[:TP, :], pos[0][:TP, 0:128], rr[:TP, 0:1], t1[:TP, :], ALU.mult, ALU.add,
                            [poks[0], "rr", "t1"], ["oo"])
                        mset("pool", ssn[:, 0:1], 0.0, ["ssn"])
                        act(junk[:TP, 0:128], oo[:TP, :], AF.Square, ["oo", "ssn"], ["junk", "ssn"], accum=ssn[:TP, 0:1])
                        rstd_ops("dve", rsn[:TP, 0:1], ssn[:TP, 0:1], 128.0, ["ssn"], ["rsn"])
                        stt("dve", ug[:TP, :], oo[:TP, :], rsn[:TP, 0:1], wsub[:TP, j, :], ALU.mult, ALU.mult,
                            ["oo", "rsn", "wsub"], ["ug"])
                        tt("pool", ub[:TP, :], ug[:TP, :], sz_[:TP, hh * 128:(hh + 1) * 128], ALU.mult,
                           ["ug", ("szt", jt % 2)], ["ub"])
                        us = uid[0] % 2
                        ptu = pt[us][:, 512:640]
                        tr(ptu[:, :TP], ub[:TP, :], identb[:TP, :TP], ["ub", "identb"], [("pt", us)])
                        cp("act", uT[:, h, jt * TP:(jt + 1) * TP], ptu[:, :TP], [("pt", us)], ["uT"])
            phase_outproj(awo_d, j, T, False)

        def qk_norm(bank, bk, TP, j, which):
            act(sq[:TP, :], bank[:TP, 0:256], AF.Identity, [bk], ["sq"])
            tt("dve", sq[:TP, :], sq[:TP, :], sq[:TP, :], ALU.mult, ["sq"], ["sq"])
            S.op("dve", lambda e, _o=ssq4[:TP, 0:4], _i=sq[:TP, :].rearrange("p (g d) -> p g d", g=4):
                 e.tensor_reduce(out=_o, in_=_i, axis=AX.X, op=ALU.add), ["sq"], ["ssq4"])
            rstd_ops("dve", rs4[:TP, 0:4], ssq4[:TP, 0:4], 64.0, ["ssq4"], ["rs4"])
            tt("dve", kn[:TP, :].rearrange("p (g d) -> p g d", g=4), bank[:TP, 0:256].rearrange("p (g d) -> p g d", g=4),
               rs4[:TP, 0:4].unsqueeze(2).broadcast_to([TP, 4, 64]), ALU.mult, [bk, "rs4"], ["kn"])

        seqs = [("p", i) for i in range(NP)] + [("s", i) for i in range(NS)]
        for kind, si in seqs:
            T = SEQ if kind == "p" else DEC_SEQ
            TP = min(T, 128)
            NT = T // TP
            xd = (xp_d if kind == "p" else xs_d)[si]
            yd = (yp_d if kind == "p" else ys_d)[si]
            for t in range(NT):
                dma("sp", xres[:TP, t, :], xd[t * TP:(t + 1) * TP, :], (), [("x", t)])
            for l in range(NL):
                if l % 2 == 0:
                    hgrn_layer(l, T, kind, si)
                else:
                    attn_layer(l, T, kind, si)
            for t in range(NT):
                dma("sp", yd[t * TP:(t + 1) * TP, :], xres[:TP, t, :], [("x", t)], ())
        S.emit(nc)
    return nc, len(S.ins)


_PROG_CACHE = {}


def _get_prog(NP, NS, NL=4):
    key = (NP, NS, NL)
    if key not in _PROG_CACHE:
        _PROG_CACHE[key] = build_program(NP, NS, NL)
    return _PROG_CACHE[key]


def make_in_maps(inputs, NP, NS, ncores):
    f = lambda a: np.ascontiguousarray(np.asarray(a, dtype=np.float32))
    c = _const_tables()
    shared = dict(
        normwT=f(np.asarray(inputs["norm_w"]).reshape(4, 8, 128).transpose(2, 0, 1)),
        hwin=f(inputs["hgrn_w_in"]),
        lbT=f(np.asarray(inputs["hgrn_lb_logits"]).reshape(2, 8, 128).transpose(2, 0, 1)),
        onwT=f(np.asarray(inputs["hgrn_onorm_w"]).reshape(2, 8, 128).transpose(2, 0, 1)),
        hwo=f(inputs["hgrn_w_out"]),
        awin=f(inputs["attn_w_in"]),
        qkn=f(np.broadcast_to(np.stack([np.asarray(inputs["attn_q_norm"]), np.asarray(inputs["attn_k_norm"])], axis=1)[None],
                              (128, 2, 2, 64))),
        lam=f(np.broadcast_to(np.asarray(inputs["attn_lambda"])[None], (128, 2, 4, 64))),
        sub=f(np.broadcast_to(np.asarray(inputs["attn_subln"])[None], (128, 2, 128))),
        awo=f(inputs["attn_w_out"]),
        btab_p=c["btab_p"], bmat_p=c["bmat_p"], btab_s=c["btab_s"], bmat_s=c["bmat_s"],
        tri=c["tri"], scanmask=c["scanmask"], ident=c["ident"],
    )
    xp = np.asarray(inputs["x_prompt"])
    xs = np.asarray(inputs["x_sample"])
    ck = np.asarray(inputs["cache_k"]).reshape(2, -1, PAST, D)
    cv = np.asarray(inputs["cache_v"]).reshape(2, -1, PAST, D)
    st = np.asarray(inputs["state_hgrn"])
    maps = []
    for c_ in range(ncores):
        m = dict(shared)
        m["xp"] = f(xp[c_ * NP:(c_ + 1) * NP]) if NP > 0 else np.zeros((1, SEQ, D), np.float32)
        if NS > 0:
            m["xs"] = f(xs[c_ * NS:(c_ + 1) * NS])
            m["ck"] = f(ck[:, c_ * NS:(c_ + 1) * NS])
            m["cv"] = f(cv[:, c_ * NS:(c_ + 1) * NS])
            m["st"] = f(st[:, c_ * NS:(c_ + 1) * NS])
        else:
            m["xs"] = np.zeros((1, DEC_SEQ, D), np.float32)
            m["ck"] = np.zeros((2, 1, PAST, D), np.float32)
            m["cv"] = np.zeros((2, 1, PAST, D), np.float32)
            m["st"] = np.zeros((2, 1, 8, 128, 128), np.float32)
        maps.append(m)
    return maps


def kernel(x_prompt, x_sample, cache_k, cache_v, state_hgrn, norm_w, hgrn_w_in, hgrn_lb_logits,
           hgrn_onorm_w, hgrn_w_out, attn_w_in, attn_q_norm, attn_k_norm, attn_lambda, attn_subln,
           attn_w_out):
    inputs = dict(x_prompt=x_prompt, x_sample=x_sample, cache_k=cache_k, cache_v=cache_v,
                  state_hgrn=state_hgrn, norm_w=norm_w, hgrn_w_in=hgrn_w_in,
                  hgrn_lb_logits=hgrn_lb_logits, hgrn_onorm_w=hgrn_onorm_w, hgrn_w_out=hgrn_w_out,
                  attn_w_in=attn_w_in, attn_q_norm=attn_q_norm, attn_k_norm=attn_k_norm,
                  attn_lambda=attn_lambda, attn_subln=attn_subln, attn_w_out=attn_w_out)
    B = np.asarray(x_prompt).shape[0]
    Bs = np.asarray(x_sample).shape[0]
    NP = B // NCORES
    NS = Bs // NCORES
    nc, _ = _get_prog(NP, NS)
    maps = make_in_maps(inputs, NP, NS, NCORES)
    res = run_bass_kernel_spmd(nc, maps, core_ids=list(range(NCORES)))
    R = res.results
    cat = lambda name, ax: np.concatenate([np.asarray(r[name]) for r in R], axis=ax)
    yp = cat("yp", 0)
    ys = cat("ys", 0)
    nkp = cat("nkp", 1).reshape(2, B, SEQ, 2, 8, 64)
    nvp = cat("nvp", 1).reshape(2, B, SEQ, 8, 128)
    nks = cat("nks", 1).reshape(2, Bs, DEC_SEQ, 2, 8, 64)
    nvs = cat("nvs", 1).reshape(2, Bs, DEC_SEQ, 8, 128)
    nsp = cat("nsp", 1)
    nss = cat("nss", 1)
    return (yp, ys, nkp, nvp, nks, nvs, nsp, nss)
```

```python
import contextlib
import math
import numpy as np
import concourse.bass as bass
import concourse.mybir as mybir
from concourse.bass_utils import run_bass_kernel_spmd

F32 = mybir.dt.float32
BF16 = mybir.dt.bfloat16
AF = mybir.ActivationFunctionType
ALU = mybir.AluOpType
AX = mybir.AxisListType

NCORES = 8
D = 1024
SEQ = 2048
DEC_SEQ = 64
PAST = 2048
EPS = 1e-6
K_DMA = 8
STOP = 0
SKIP_T = 180.0


class Sched:
    LAT_X = 250.0
    LAT_S = 80.0

    def __init__(self, reorder=True):
        self.ins = []
        self.last_w = {}
        self.readers = {}
        self.cur_bar = None
        self.reorder = reorder
        self.rkeys = set()
        self.wkeys = set()

    def op(self, eng, fn, reads=(), writes=(), dma=False, cost=100.0, bar=False):
        idx = len(self.ins)
        deps = set()
        self.rkeys.update(reads)
        self.wkeys.update(writes)
        for k in reads:
            w = self.last_w.get(k)
            if w is not None:
                deps.add(w)
            if isinstance(k, tuple) and k[0] in ("pb", "pt"):
                for r in self.readers.get(k, ()):
                    if self.ins[r]["eng"] != eng:
                        deps.add(r)
        for k in writes:
            w = self.last_w.get(k)
            if w is not None:
                deps.add(w)
            for r in self.readers.get(k, ()):
                deps.add(r)
        for k in reads:
            self.readers.setdefault(k, []).append(idx)
        for k in writes:
            self.last_w[k] = idx
            self.readers[k] = []
        if self.cur_bar is not None and not bar:
            deps.add(self.cur_bar.get(eng, self.cur_bar["dve"]))
        deps.discard(idx)
        self.ins.append(dict(eng=eng, fn=fn, deps=deps, dma=dma, cost=cost, bar=bar))
        return idx

    def barrier(self, dummies):
        nb = {}
        for e, fn in dummies.items():
            nb[e] = self.op(e, fn, (), [("bar", e)], bar=True)
        self.cur_bar = nb

    def _sched_segment(self, seg):
        import heapq
        ins = self.ins
        if not self.reorder or len(seg) < 3:
            return list(seg)
        segset = set(seg)
        indeg = {}
        succ = {}
        avail = {}
        for i in seg:
            ds = [d for d in ins[i]["deps"] if d in segset]
            indeg[i] = len(ds)
            avail[i] = 0.0
            for d in ds:
                succ.setdefault(d, []).append(i)
        engs = sorted(set(ins[i]["eng"] for i in seg))
        waiting = {e: [] for e in engs}
        ready = {e: [] for e in engs}
        busy = {e: 0.0 for e in engs}
        for i in seg:
            if indeg[i] == 0:
                heapq.heappush(ready[ins[i]["eng"]], i)
        order = []
        t = 0.0
        done = 0
        n = len(seg)
        finish = {}
        while done < n:
            progressed = False
            for e in engs:
                if busy[e] > t:
                    continue
                w = waiting[e]
                while w and w[0][0] <= t:
                    heapq.heappush(ready[e], heapq.heappop(w)[1])
                if not ready[e]:
                    continue
                i = heapq.heappop(ready[e])
                I = ins[i]
                c = I["cost"]
                if I["dma"]:
                    issue = 60.0 if e == "sp" else 700.0
                    busy[e] = t + issue
                    fin = t + c
                else:
                    busy[e] = t + c
                    fin = t + c
                finish[i] = fin
                order.append(i)
                done += 1
                progressed = True
                for sidx in succ.get(i, ()):
                    lat = self.LAT_S if ins[sidx]["eng"] == e and not I["dma"] else self.LAT_X
                    if e == "pe" and ins[sidx]["eng"] == "pe":
                        lat = 0.0
                    a = fin + lat
                    if a > avail[sidx]:
                        avail[sidx] = a
                    indeg[sidx] -= 1
                    if indeg[sidx] == 0:
                        heapq.heappush(waiting[ins[sidx]["eng"]], (avail[sidx], sidx))
            if not progressed:
                cand = []
                for e in engs:
                    if busy[e] > t:
                        cand.append(busy[e])
                    elif waiting[e]:
                        cand.append(max(waiting[e][0][0], busy[e]))
                assert cand, "scheduler stuck"
                t = min(cand)
        return order

    def schedule(self):
        ins = self.ins
        n = len(ins)
        order = []
        seg = []
        last_on = {}
        seg_dmas = []

        def flush():
            nonlocal seg
            o = self._sched_segment(seg)
            for i in o:
                last_on[ins[i]["eng"]] = i
                if ins[i]["dma"]:
                    seg_dmas.append(i)
            order.extend(o)
            seg = []

        i = 0
        while i < n:
            if ins[i]["bar"]:
                flush()
                prev = set(last_on.values()) | set(seg_dmas)
                seg_dmas.clear()
                while i < n and ins[i]["bar"]:
                    ins[i]["deps"] = set(prev)
                    order.append(i)
                    i += 1
                for k in order[-8:]:
                    if ins[k]["bar"]:
                        last_on[ins[k]["eng"]] = k
            else:
                seg.append(i)
                i += 1
        flush()
        assert len(order) == n and len(set(order)) == n
        return order

    def emit(self, nc, engines=("pe", "act", "dve", "pool", "sp")):
        print("keys read but never written:", sorted(map(str, self.rkeys - self.wkeys)))
        order = self.schedule()
        ins = [self.ins[i] for i in order]
        remap = {old: new for new, old in enumerate(order)}
        for I in ins:
            I["deps"] = set(remap[d] for d in I["deps"])
        n = len(ins)
        for i, I in enumerate(ins):
            for d in I["deps"]:
                assert d < i, "schedule violates a dependency"
        has_cons = [False] * n
        for I in ins:
            nd = set()
            for d in I["deps"]:
                Pp = ins[d]
                if (not Pp["dma"]) and Pp["eng"] == I["eng"] and I["eng"] == "pe":
                    continue
                nd.add(d)
            I["deps"] = nd
            for d in nd:
                has_cons[d] = True
        cnt = {e: 0 for e in engines}
        dcnt = {e: 0 for e in engines}
        for i, I in enumerate(ins):
            e = I["eng"]
            if I["dma"]:
                k = dcnt[e]
                dcnt[e] += 1
                I["sig"] = (("d", e, k % K_DMA), 16 * (k // K_DMA + 1))
                I["dma_k"] = k
            elif has_cons[i]:
                cnt[e] += 1
                I["sig"] = (("c", e), cnt[e])
            else:
                I["sig"] = None
        with contextlib.ExitStack() as st:
            sems = {}
            for e in engines:
                sems[("c", e)] = st.enter_context(nc.semaphore("c_" + e))
                if dcnt[e] > 0:
                    for j in range(K_DMA):
                        sems[("d", e, j)] = st.enter_context(nc.semaphore("d_%s_%d" % (e, j)))
            block = st.enter_context(nc.Block())

            def make(e):
                def body(eng):
                    seen = {}
                    for I in ins:
                        if I["eng"] != e:
                            continue
                        waits = {}
                        for d in I["deps"]:
                            s, v = ins[d]["sig"]
                            if waits.get(s, 0) < v:
                                waits[s] = v
                        if I["dma"] and I["dma_k"] >= K_DMA:
                            s, v = I["sig"]
                            if waits.get(s, 0) < v - 16:
                                waits[s] = v - 16
                        for s, v in waits.items():
                            if seen.get(s, 0) >= v:
                                continue
                            seen[s] = v
                            eng.wait_ge(sems[s], v)
                        bi = I["fn"](eng)
                        if I["sig"] is not None:
                            s, v = I["sig"]
                            bi.then_inc(sems[s], 16 if I["dma"] else 1)
                    if dcnt[e] > 0:
                        for j in range(K_DMA):
                            tot = (dcnt[e] - j + K_DMA - 1) // K_DMA
                            if tot > 0 and seen.get(("d", e, j), 0) < 16 * tot:
                                eng.wait_ge(sems[("d", e, j)], 16 * tot)
                return body

            reg = {"pe": block.tensor, "act": block.scalar, "dve": block.vector,
                   "pool": block.gpsimd, "sp": block.sync}
            for e in engines:
                if any(I["eng"] == e for I in ins):
                    reg[e](make(e))


def _slopes():
    return [2.0 ** (-8.0 * (h + 1) / 8.0) for h in range(8)]


def _const_tables():
    sl = np.array(_slopes(), np.float64)
    kk = np.arange(128)[:, None, None]
    dd = np.arange(16)[None, None, :]
    btab_p = -(sl[None, :, None]) * (128.0 * dd + 127.0 - kk)
    k = np.arange(128)[:, None, None]
    q = np.arange(128)[None, None, :]
    vis = (k // 64) <= (q // 64)
    bm = -(sl[None, :, None]) * np.abs(q - k) - sl[None, :, None] * (127.0 - q)
    bmat_p = np.where(vis, bm, -30000.0)
    ii = np.arange(16)[None, None, :]
    btab_s = -(sl[None, :, None]) * (2111.0 - 128.0 * ii - kk)
    k6 = np.arange(64)[:, None, None]
    q6 = np.arange(64)[None, None, :]
    bmat_s = -(sl[None, :, None]) * np.abs(q6 - k6) - sl[None, :, None] * (63.0 - q6)
    bmat_s_full = np.zeros((128, 8, 64))
    bmat_s_full[:64] = bmat_s
    tri = (np.arange(128)[:, None] <= np.arange(128)[None, :]).astype(np.float32)
    scanmask = np.ones((128, 512), np.float32)
    scanmask[:, ::128] = 0.0
    return dict(
        btab_p=btab_p.astype(np.float32), bmat_p=bmat_p.astype(np.float32),
        btab_s=btab_s.astype(np.float32), bmat_s=bmat_s_full.astype(np.float32),
        tri=tri, scanmask=scanmask, ident=np.eye(128, dtype=np.float32),
    )


def build_program(NP, NS, NL=4):
    nc = bass.Bass("TRN2", target_bir_lowering=False)

    def din(name, shape):
        return nc.dram_tensor(name, list(shape), F32, kind="ExternalInput").ap()

    def dout(name, shape):
        return nc.dram_tensor(name, list(shape), F32, kind="ExternalOutput").ap()

    xp_d = din("xp", [max(NP, 1), SEQ, D])
    xs_d = din("xs", [max(NS, 1), DEC_SEQ, D])
    ck_d = din("ck", [2, max(NS, 1), PAST, D])
    cv_d = din("cv", [2, max(NS, 1), PAST, D])
    st_d = din("st", [2, max(NS, 1), 8, 128, 128])
    normwT_d = din("normwT", [128, 4, 8])
    hwin_d = din("hwin", [2, D, 4 * D])
    lbT_d = din("lbT", [128, 2, 8])
    onwT_d = din("onwT", [128, 2, 8])
    hwo_d = din("hwo", [2, D, D])
    awin_d = din("awin", [2, D, 4 * D])
    qkn_d = din("qkn", [128, 2, 2, 64])
    lam_d = din("lam", [128, 2, 4, 64])
    sub_d = din("sub", [128, 2, 128])
    awo_d = din("awo", [2, D, D])
    btab_p_d = din("btab_p", [128, 8, 16])
    bmat_p_d = din("bmat_p", [128, 8, 128])
    btab_s_d = din("btab_s", [128, 8, 16])
    bmat_s_d = din("bmat_s", [128, 8, 64])
    tri_d = din("tri", [128, 128])
    scanmask_d = din("scanmask", [128, 512])
    ident_d = din("ident", [128, 128])

    yp_d = dout("yp", [max(NP, 1), SEQ, D])
    ys_d = dout("ys", [max(NS, 1), DEC_SEQ, D])
    nkp_d = dout("nkp", [2, max(NP, 1), SEQ, D])
    nvp_d = dout("nvp", [2, max(NP, 1), SEQ, D])
    nks_d = dout("nks", [2, max(NS, 1), DEC_SEQ, D])
    nvs_d = dout("nvs", [2, max(NS, 1), DEC_SEQ, D])
    nsp_d = dout("nsp", [2, max(NP, 1), 8, 128, 128])
    nss_d = dout("nss", [2, max(NS, 1), 8, 128, 128])

    S = Sched()
    uid = [0]
    ucnt = [0]

    with contextlib.ExitStack() as stk:
        def sb(name, shape, dt):
            return stk.enter_context(nc.sbuf_tensor("s_" + name, list(shape), dt))

        def psb(name, shape, dt):
            return stk.enter_context(nc.psum_tensor("p_" + name, list(shape), dt))

        xres = sb("xres", [128, 16, D], F32)
        hT = sb("hT", [128, 8, SEQ], BF16)
        uT = sb("uT", [128, 8, SEQ], BF16)
        wbuf_t = sb("wbuf", [128, 8, 1024], BF16)
        wbuf = [wbuf_t[:, :, 0:512], wbuf_t[:, :, 512:1024]]
        hb = [sb("hb0", [128, D], BF16)] * 2
        ssx = sb("ssx", [128, 16], F32)
        rstdx = sb("rstdx", [128, 16], F32)
        ssacc = sb("ssacc", [128, 16], F32)
        rstdo = sb("rstdo", [128, 16], F32)
        bar_d = sb("bar_d", [128, 4], F32)
        identb = sb("identb", [128, 128], BF16)
        tri = sb("tri", [128, 128], F32)
        scanmask = sb("scanmask", [128, 512], F32)
        ones1 = sb("ones1", [128, 1], F32)
        normwT = sb("normwT", [128, 4, 8], F32)
        lbraw = sb("lbraw", [128, 2, 8], F32)
        lbT = sb("lbT", [128, 2, 8], F32)
        omlT = sb("omlT", [128, 2, 8], F32)
        onwT = sb("onwT", [128, 2, 8], F32)
        qkn = sb("qkn", [128, 2, 2, 64], F32)
        lams = sb("lams", [128, 2, 2], F32)
        lame = sb("lame", [128, 2, 2], F32)
        neglam = sb("neglam", [128, 2], F32)
        wsub = sb("wsub", [128, 2, 128], F32)
        btab_p = sb("btab_p", [128, 8, 16], F32)
        bmat_p = sb("bmat_p", [128, 8, 128], F32)
        btab_s = sb("btab_s", [128, 8, 16], F32)
        bmat_s = sb("bmat_s", [128, 8, 64], F32)
        Sall = sb("Sall", [128, 17, 128], BF16)
        S32 = [sb("S32a", [128, 128], F32), sb("S32b", [128, 128], F32)]
        SCR_BYTES = 43008
        scr = sb("scr", [128, SCR_BYTES // 2], BF16)

        class Carver:
            def __init__(self):
                self.off = 0

            def __call__(self, shape, dt):
                n = 1
                for x in shape[1:]:
                    n *= x
                nb = n * (4 if dt == F32 else 2)
                nb_al = (nb + 31) // 32 * 32
                assert self.off + nb_al <= SCR_BYTES, (self.off, nb_al)
                ap = scr[:, self.off // 2:(self.off + nb) // 2]
                self.off += nb_al
                if dt == F32:
                    ap = ap.bitcast(F32)
                if len(shape) == 3:
                    ap = ap.rearrange("p (a b) -> p a b", a=shape[1])
                elif len(shape) == 4:
                    ap = ap.rearrange("p (a b c) -> p a b c", a=shape[1], b=shape[2])
                assert tuple(ap.shape) == tuple(shape), (ap.shape, shape)
                return ap

        cv_ = Carver()
        junk = cv_([128, D], BF16)
        base_off = cv_.off
        lamv = cv_([128, 2, 4, 64], F32)
        lamp = cv_([128, 2, 2, 64], F32)
        cv_.off = base_off
        qf = cv_([128, 512], F32)
        ff = cv_([128, 512], F32)
        gg = cv_([128, 512], F32)
        kf = cv_([128, 512], F32)
        bbuf = cv_([128, 512], F32)
        ee = [cv_([128, 512], F32), cv_([128, 512], F32)]
        ex = [cv_([128, 512], F32), cv_([128, 512], F32)]
        osq = cv_([128, 512], F32)
        qt = cv_([128, 512], BF16)
        qh = cv_([128, 512], BF16)
        khT = cv_([128, 512], BF16)
        szTs = [cv_([128, 512], BF16), cv_([128, 512], BF16)]
        Kb = [cv_([128, 4, 128], BF16) for i in range(4)]
        khtm = cv_([128, 4, 128], BF16)
        vtms = [cv_([128, 4, 128], BF16), cv_([128, 4, 128], BF16)]
        attm = cv_([128, 4, 128], BF16)
        dec = cv_([128, 8], F32)
        hg_end = cv_.off
        cv_.off = base_off
        kT = cv_([128, 2, SEQ + 64], BF16)
        vaug = cv_([128, 17, 2, 129], BF16)
        kctm = cv_([128, 16, 2, 128], BF16)
        sq = cv_([128, 256], F32)
        ssq4 = cv_([128, 8], F32)
        rs4 = cv_([128, 8], F32)
        kn = cv_([128, 256], F32)
        kout = [cv_([128, 256], F32), cv_([128, 256], F32)]
        vout = [cv_([128, 256], F32), cv_([128, 256], F32)]
        qkb = cv_([128, 256], BF16)
        qTt = [cv_([128, 2, 128], BF16), cv_([128, 2, 128], BF16)]
        szt = [cv_([128, 256], BF16), cv_([128, 256], BF16)]
        Pt = [cv_([128, 2, 128], BF16) for i in range(4)]
        dtmp = cv_([128, 2, 128], F32)
        rr = cv_([128, 8], F32)
        rl = cv_([128, 8], F32)
        t1 = cv_([128, 128], F32)
        oo = cv_([128, 128], F32)
        ssn = cv_([128, 8], F32)
        rsn = cv_([128, 8], F32)
        ug = cv_([128, 128], F32)
        ub = cv_([128, 128], BF16)
        sgt = cv_([128, 256], F32)
        at_end = cv_.off
        print("scratch bytes: hgrn", hg_end, "attn", at_end)
        pb = [psb("pb%d" % i, [128, 512], F32) for i in range(6)]
        pt = [psb("pt%d" % i, [128, 1024], BF16) for i in range(2)]

        def fsz(ap):
            n = 1
            for x in tuple(ap.shape)[1:]:
                n *= x
            return float(n)

        def mm(out, lhsT, rhs, start, stop, r, w):
            nn = fsz(out)
            c = max(nn, 64.0) / 2.2 + 12.0
            if lhsT.dtype == F32:
                c *= 4.0
            S.op("pe", lambda e: e.matmul(out, lhsT=lhsT, rhs=rhs, start=start, stop=stop), r, w, cost=c)

        def tr(out, in_, ident, r, w):
            S.op("pe", lambda e: e.transpose(out=out, in_=in_, identity=ident), r, w, cost=max(fsz(out), 64.0) / 2.2 + 40.0)

        def act(out, in_, func, r, w, bias=None, scale=None, accum=None):
            kw = {}
            if bias is not None:
                kw["bias"] = bias
            if scale is not None:
                kw["scale"] = scale
            if accum is not None:
                kw["accum_out"] = accum
            S.op("act", lambda e: e.activation(out=out, in_=in_, func=func, **kw), r, w, cost=(224.0 + fsz(out)) / 1.4)

        def vcost(eng, out):
            if eng == "pool":
                return 120.0 + 2.1 * fsz(out)
            return (70.0 + fsz(out)) / 0.96

        def tt(eng, out, in0, in1, op, r, w):
            S.op(eng, lambda e: e.tensor_tensor(out=out, in0=in0, in1=in1, op=op), r, w, cost=vcost(eng, out))

        def tsc(eng, out, in0, s1, s2, op0, op1, r, w):
            if op1 is None:
                S.op(eng, lambda e: e.tensor_scalar(out=out, in0=in0, scalar1=s1, scalar2=None, op0=op0), r, w,
                     cost=vcost(eng, out))
            else:
                S.op(eng, lambda e: e.tensor_scalar(out=out, in0=in0, scalar1=s1, scalar2=s2, op0=op0, op1=op1), r, w,
                     cost=vcost(eng, out))

        def stt(eng, out, in0, scalar, in1, op0, op1, r, w):
            S.op(eng, lambda e: e.scalar_tensor_tensor(out=out, in0=in0, scalar=scalar, in1=in1, op0=op0, op1=op1), r, w,
                 cost=vcost(eng, out))

        def cp(eng, out, in_, r, w):
            if eng == "act":
                S.op("act", lambda e: e.copy(out=out, in_=in_), r, w, cost=(224.0 + fsz(out)) / 1.4)
            else:
                S.op(eng, lambda e: e.tensor_copy(out=out, in_=in_), r, w, cost=vcost(eng, out))

        def mset(eng, ap, val, w):
            S.op(eng, lambda e: e.memset(ap, val), (), w, cost=vcost(eng, ap) * 0.5)

        def dma(eng, out, in_, r, w):
            nb = fsz(out) * float(out.shape[0]) * 4.0
            S.op(eng, lambda e: e.dma_start(out=out, in_=in_), r, w, dma=True, cost=2000.0 + nb / 120.0)

        def rstd_ops(eng, out, in_, n, r, w):
            tsc(eng, out, in_, 1.0 / n, EPS, ALU.mult, ALU.add, r, w)
            act(out, out, AF.Ln, w, w)
            act(out, out, AF.Exp, w, w, scale=-0.5)

        dma("sp", tri[:], tri_d, (), ["tri"])
        dma("sp", scanmask[:], scanmask_d, (), ["scanmask"])
        dma("pool", identb[:], ident_d, (), ["identb"])
        dma("sp", normwT[:], normwT_d, (), ["normwT"])
        dma("sp", lbraw[:], lbT_d, (), ["lbraw"])
        dma("sp", onwT[:], onwT_d, (), ["onwT"])
        dma("sp", qkn[:], qkn_d, (), ["qkn"])
        dma("sp", lamv[:], lam_d, (), ["lamv"])
        dma("sp", wsub[:], sub_d, (), ["wsub"])
        dma("sp", btab_p[:], btab_p_d, (), ["btab_p"])
        dma("sp", bmat_p[:], bmat_p_d, (), ["bmat_p"])
        dma("sp", btab_s[:], btab_s_d, (), ["btab_s"])
        dma("sp", bmat_s[:], bmat_s_d, (), ["bmat_s"])
        mset("dve", ones1[:], 1.0, ["ones1"])
        mset("dve", lbT[:], 0.0, ["lbT"])
        tt("dve", lbT[:, 1, :], lbraw[:, 1, :], lbraw[:, 0, :], ALU.subtract, ["lbraw"], ["lbT"])
        act(lbT[:, 1, :], lbT[:, 1, :], AF.Exp, ["lbT"], ["lbT"], scale=-1.0)
        tsc("dve", lbT[:, 1, :], lbT[:, 1, :], 1.0, None, ALU.add, None, ["lbT"], ["lbT"])
        S.op("dve", lambda e: e.reciprocal(out=lbT[:, 1, :], in_=lbT[:, 1, :]), ["lbT"], ["lbT"])
        tsc("dve", omlT[:], lbT[:], -1.0, 1.0, ALU.mult, ALU.add, ["lbT"], ["omlT"])
        tt("dve", lamp[:, :, 0, :], lamv[:, :, 0, :], lamv[:, :, 1, :], ALU.mult, ["lamv"], ["lamp"])
        tt("dve", lamp[:, :, 1, :], lamv[:, :, 2, :], lamv[:, :, 3, :], ALU.mult, ["lamv"], ["lamp"])
        S.op("dve", lambda e: e.tensor_reduce(out=lams[:], in_=lamp[:], axis=AX.X, op=ALU.add), ["lamp"], ["lams"])
        act(lame[:], lams[:], AF.Exp, ["lams"], ["lame"])
        for j in range(2):
            lam_init = 0.8 - 0.6 * math.exp(-0.3 * (2 * j + 1))
            tt("dve", neglam[:, j:j + 1], lame[:, j, 1:2], lame[:, j, 0:1], ALU.subtract, ["lame"], ["neglam"])
            tsc("dve", neglam[:, j:j + 1], neglam[:, j:j + 1], -lam_init, None, ALU.add, None, ["neglam"], ["neglam"])
            tsc("dve", wsub[:, j, :], wsub[:, j, :], 1.0 - lam_init, None, ALU.mult, None, ["wsub"], ["wsub"])

        def fence():
            S.barrier({
                "act": lambda e: e.copy(out=bar_d[:, 0:1], in_=ones1[:, 0:1]),
                "dve": lambda e: e.memset(bar_d[:, 1:2], 0.0),
                "pool": lambda e: e.memset(bar_d[:, 2:3], 0.0),
            })

        def phase_norm(l, T):
            TP = min(T, 128)
            NT = T // TP
            mset("dve", ssx[:], 0.0, ["ssx"])
            for t in range(NT):
                act(junk[:TP, :], xres[:TP, t, :], AF.Square, [("x", t)], ["junk", "ssx"], accum=ssx[:TP, t:t + 1])
            rstd_ops("dve", rstdx[:TP, :NT], ssx[:TP, :NT], float(D), ["ssx"], ["rstdx"])
            for t in range(NT):
                hbt = hb[t % 2]
                act(hbt[:TP, :], xres[:TP, t, :], AF.Identity, [("x", t), "rstdx"], [("hb", 0)],
                    scale=rstdx[:TP, t:t + 1])
                ptb = pt[t % 2]
                ptv = ptb[:, :].rearrange("p (a b) -> p a b", a=8)
                for kc in range(8):
                    tr(ptv[:, kc, :TP], hbt[:TP, kc * 128:(kc + 1) * 128], identb[:TP, :TP],
                       [("hb", 0), "identb"], [("pt", t % 2)])
                tt("dve", hT[:, :, t * TP:(t + 1) * TP], ptv[:, :, :TP],
                   normwT[:, l, :].unsqueeze(2).broadcast_to([128, 8, TP]), ALU.mult,
                   [("pt", t % 2), "normwT"], [("hT", t)])

        def phase_outproj(w_d, j, T, use_rstd):
            TP = min(T, 128)
            NT = T // TP
            wv = w_d[j].rearrange("(kc p) n -> p kc n", p=128)
            for half in range(2):
                dma("pool", wbuf[half][:, :, 0:512], wv[:, :, half * 512:(half + 1) * 512], (), [("wbuf", half)])
            for t in range(NT):
                for half in range(2):
                    bank = pb[(2 * t + half) % 2]
                    bk = ("pb", (2 * t + half) % 2)
                    for kc in range(8):
                        mm(bank[:TP, :], uT[:, kc, t * TP:(t + 1) * TP], wbuf[half][:, kc, 0:512],
                           kc == 0, kc == 7, ["uT", ("wbuf", half)], [bk])
                    xo = xres[:TP, t, half * 512:(half + 1) * 512]
                    if use_rstd:
                        stt("dve", xo, bank[:TP, :], rstdo[:TP, t:t + 1], xo, ALU.mult, ALU.add,
                            [bk, "rstdo", ("x", t)], [("x", t)])
                    else:
                        tt("dve", xo, bank[:TP, :], xo, ALU.add, [bk, ("x", t)], [("x", t)])

        def hgrn_layer(l, T, kind, si):
            j = l // 2
            CH = min(T, 128)
            NCHT = T // CH
            SEGT = min(T, 512)
            NSEG = T // SEGT
            NCH = SEGT // CH
            NSB = CH // 32
            fence()
            for i in range(4):
                mset("pool", Kb[i], 0.0, [("Kb", i)])
            phase_norm(l, T)
            wv = hwin_d[j].rearrange("(kc p) n -> p kc n", p=128)
            for h in range(8):
                slot = h % 2
                wb = wbuf[slot]
                wk = ("wbuf", slot)
                for qi in range(4):
                    dma("pool", wb[:, :, qi * 128:(qi + 1) * 128],
                        wv[:, :, qi * 1024 + h * 128: qi * 1024 + (h + 1) * 128], (), [wk])
                if kind == "p":
                    mset("pool", S32[0][:], 0.0, [("S32", 0)])
                    mset("pool", Sall[:, 0, :], 0.0, ["Sall"])
                else:
                    dma("sp", S32[0][:], st_d[j, si, h], (), [("S32", 0)])
                    cp("pool", Sall[:, 0, :], S32[0][:], [("S32", 0)], ["Sall"])
                for seg in range(NSEG):
                    t0 = seg * SEGT
                    for qi, bi in ((0, 0), (1, 1), (3, 2)):
                        for kc in range(8):
                            mm(pb[bi][:, :SEGT], wb[:, kc, qi * 128:(qi + 1) * 128], hT[:, kc, t0:t0 + SEGT],
                               kc == 0, kc == 7, [wk] + [("hT", tt_) for tt_ in range(t0 // CH, (t0 + SEGT) // CH)], [("pb", bi)])
                    act(qf[:, :SEGT], pb[0][:, :SEGT], AF.Identity, [("pb", 0)], ["qf"], scale=128.0 ** -0.5)
                    act(gg[:, :SEGT], pb[1][:, :SEGT], AF.Exp, [("pb", 1)], ["gg"], scale=-1.0)
                    act(gg[:, :SEGT], gg[:, :SEGT], AF.Ln, ["gg"], ["gg"], bias=1.0)
                    act(ff[:, :SEGT], gg[:, :SEGT], AF.Exp, ["gg"], ["ff"], scale=-1.0)
                    act(ex[1][:, :SEGT], pb[2][:, :SEGT], AF.Exp, [("pb", 2)], [("ex", 1)], scale=-1.0)
                    act(ex[1][:, :SEGT], ex[1][:, :SEGT], AF.Ln, [("ex", 1)], [("ex", 1)], bias=1.0)
                    act(ex[1][:, :SEGT], ex[1][:, :SEGT], AF.Exp, [("ex", 1)], [("ex", 1)], scale=-1.0)
                    szT = szTs[ucnt[0] % 2]
                    szk = ("szT", ucnt[0] % 2)
                    vtm = vtms[ucnt[0] % 2]
                    vtk = ("vtm", ucnt[0] % 2)
                    ucnt[0] += 1
                    tt("dve", szT[:, :SEGT], ex[1][:, :SEGT], pb[2][:, :SEGT], ALU.mult, [("ex", 1), ("pb", 2)], [szk])
                    tsc("dve", ff[:, :SEGT], ff[:, :SEGT], omlT[:, j, h:h + 1], lbT[:, j, h:h + 1], ALU.mult, ALU.add,
                        ["ff", "omlT", "lbT"], ["ff"])
                    act(gg[:, :SEGT], ff[:, :SEGT], AF.Ln, ["ff"], ["gg"])
                    tsc("pool", kf[:, :SEGT], ff[:, :SEGT], -1.0, 1.0, ALU.mult, ALU.add, ["ff"], ["kf"])
                    pv = pb[3][:, :].rearrange("p (c v) -> p c v", c=4)
                    for c in range(NCH):
                        for kc in range(8):
                            mm(pv[:CH, c, :], hT[:, kc, t0 + c * CH:t0 + (c + 1) * CH], wb[:, kc, 256:384],
                               kc == 0, kc == 7, [wk, ("hT", t0 // CH + c)], [("pb", 3)])
                    cp("act", vtm[:CH, :NCH, :], pv[:CH, :NCH, :], [("pb", 3)], [vtk])
                    S.op("dve", lambda e, _o=bbuf[:, :SEGT], _m=scanmask[:, :SEGT], _g=gg[:, :SEGT]:
                         e.tensor_tensor_scan(out=_o, data0=_m, data1=_g, initial=0.0, op0=ALU.mult, op1=ALU.add),
                         ["scanmask", "gg"], ["bb"], cost=(70.0 + SEGT) / 0.96)
                    b3 = bbuf[:, :SEGT].rearrange("p (c t) -> p c t", c=NCH)
                    b4 = bbuf[:, :SEGT].rearrange("p (c i t) -> p c i t", c=NCH, i=NSB)
                    e0 = ee[0]
                    e04 = e0[:, :SEGT].rearrange("p (c i t) -> p c i t", c=NCH, i=NSB)
                    if NSB > 1:
                        tt("dve", e04[:, :, 1:NSB, :], b4[:, :, 1:NSB, :],
                           b4[:, :, 0:NSB - 1, 31:32].broadcast_to([128, NCH, NSB - 1, 32]), ALU.subtract,
                           ["bb"], [("ee", 0)])
                    cp("pool", e04[:, :, 0, :], b4[:, :, 0, :], ["bb"], [("ee", 0)])
                    act(ex[0][:, :SEGT], e0[:, :SEGT], AF.Exp, [("ee", 0)], [("ex", 0)])
                    tt("dve", qt[:, :SEGT], qf[:, :SEGT], ex[0][:, :SEGT], ALU.mult, ["qf", ("ex", 0)], ["qt"])
                    act(ex[1][:, :SEGT], bbuf[:, :SEGT], AF.Exp, ["bb"], [("ex", 1)])
                    tt("pool", qh[:, :SEGT], qf[:, :SEGT], ex[1][:, :SEGT], ALU.mult, ["qf", ("ex", 1)], ["qh"])
                    kf3 = kf[:, :SEGT].rearrange("p (c t) -> p c t", c=NCH)
                    for i in range(NSB):
                        wi = 32 * (i + 1)
                        b_ = i % 2
                        ex3 = ex[b_][:, :SEGT].rearrange("p (c t) -> p c t", c=NCH)
                        if i == 0:
                            act(ex3[:, :, 0:wi], b3[:, :, 0:wi], AF.Exp, ["bb"], [("ex", b_)], scale=-1.0)
                        else:
                            ee3 = ee[b_][:, :SEGT].rearrange("p (c t) -> p c t", c=NCH)
                            tt("dve", ee3[:, :, 0:wi], b3[:, :, 32 * i - 1:32 * i].broadcast_to([128, NCH, wi]),
                               b3[:, :, 0:wi], ALU.subtract, ["bb"], [("ee", b_)])
                            act(ex3[:, :, 0:wi], ee3[:, :, 0:wi], AF.Exp, [("ee", b_)], [("ex", b_)])
                        tt("dve" if i % 2 == 0 else "pool", Kb[i][:, :NCH, 0:wi], kf3[:, :, 0:wi], ex3[:, :, 0:wi], ALU.mult,
                           ["kf", ("ex", b_)], [("Kb", i)])
                    ee3 = ee[0][:, :SEGT].rearrange("p (c t) -> p c t", c=NCH)
                    tt("dve", ee3[:, :, :], b3[:, :, CH - 1:CH].broadcast_to([128, NCH, CH]), b3[:, :, :], ALU.subtract,
                       ["bb"], [("ee", 0)])
                    act(ex[0][:, :SEGT], ee[0][:, :SEGT], AF.Exp, [("ee", 0)], [("ex", 0)])
                    tt("pool", khT[:, :SEGT], kf[:, :SEGT], ex[0][:, :SEGT], ALU.mult, ["kf", ("ex", 0)], ["khT"])
                    act(dec[:, :NCH], b3[:, :, CH - 1], AF.Exp, ["bb"], ["dec"])
                    ptv = pt[0][:, 0:512].rearrange("p (c k) -> p c k", c=4)
                    for c in range(NCH):
                        tr(ptv[:CH, c, :], khT[:, c * CH:(c + 1) * CH], identb[:, :], ["khT", "identb"], [("pt", 0)])
                    cp("act", khtm[:CH, :NCH, :], ptv[:CH, :NCH, :], [("pt", 0)], ["khtm"])
                    for c in range(NCH):
                        cg = seg * NCH + c
                        ub_ = 4 if c % 2 == 0 else 2
                        pu = pb[ub_][:, 0:128]
                        mm(pu, khtm[:CH, c, :], vtm[:CH, c, :], True, True, ["khtm", vtk], [("pb", ub_)])
                        s_old, s_new = S32[cg % 2], S32[(cg + 1) % 2]
                        stt("dve", s_new[:], s_old[:], dec[:, c:c + 1], pu, ALU.mult, ALU.add,
                            [("S32", cg % 2), "dec", ("pb", ub_)], [("S32", (cg + 1) % 2)])
                        cp("pool", Sall[:, cg + 1, :], s_new[:], [("S32", (cg + 1) % 2)], ["Sall"])
                    pa = pb[5][:, :].rearrange("p (c t) -> p c t", c=4)
                    for c in range(NCH):
                        for i in range(NSB):
                            mm(pa[:CH, c, 32 * i:32 * (i + 1)], Kb[i][:, c, 0:CH], qt[:, c * CH + 32 * i:c * CH + 32 * (i + 1)],
                               True, True, [("Kb", i), "qt"], [("pb", 5)])
                    tt("dve", attm[:CH, :NCH, :CH], pa[:CH, :NCH, :CH],
                       tri[:CH, :CH].unsqueeze(1).broadcast_to([CH, NCH, CH]), ALU.mult,
                       [("pb", 5), "tri"], ["attm"])
                    po = pt[1][:, :].bitcast(F32)
                    pok_ = ("pt", 1)
                    for c in range(NCH):
                        cg = seg * NCH + c
                        mm(po[:, c * CH:(c + 1) * CH], vtm[:CH, c, :], attm[:CH, c, :CH], True, False,
                           [vtk, "attm"], [pok_])
                        mm(po[:, c * CH:(c + 1) * CH], Sall[:, cg, :], qh[:, c * CH:(c + 1) * CH], False, True,
                           ["Sall", "qh"], [pok_])
                    act(osq[:, :SEGT], po[:, :SEGT], AF.Identity, [pok_], ["osq", "po_rd"])
                    tt("dve", osq[:, :SEGT], osq[:, :SEGT], osq[:, :SEGT], ALU.mult, ["osq"], ["osq"])
                    stt("dve", uT[:, h, t0:t0 + SEGT], po[:, :SEGT], onwT[:, j, h:h + 1], szT[:, :SEGT], ALU.mult, ALU.mult,
                        [pok_, "onwT", szk, "po_rd"], ["uT"])
                    pss = pb[4][:, 256:264]
                    for c in range(NCH):
                        mm(pss[:CH, c:c + 1], osq[:, c * CH:(c + 1) * CH], ones1[:, 0:1], True, True,
                           ["osq", "ones1"], [("pb", 4)])
                    cg0 = seg * NCH
                    if h == 0:
                        cp("dve", ssacc[:CH, cg0:cg0 + NCH], pss[:CH, 0:NCH], [("pb", 4)], ["ssacc"])
                    else:
                        tt("dve", ssacc[:CH, cg0:cg0 + NCH], pss[:CH, 0:NCH], ssacc[:CH, cg0:cg0 + NCH], ALU.add,
                           [("pb", 4), "ssacc"], ["ssacc"])
                od = nsp_d if kind == "p" else nss_d
                dma("sp", od[j, si, h], S32[NCHT % 2][:], [("S32", NCHT % 2)], ())
            rstd_ops("dve", rstdo[:CH, :NCHT], ssacc[:CH, :NCHT], float(D), ["ssacc"], ["rstdo"])
            phase_outproj(hwo_d, j, T, True)

        def attn_layer(l, T, kind, si):
            j = l // 2
            TP = min(T, 128)
            NT = T // TP
            fence()
            mset("pool", vaug, 1.0, ["vaug"])
            phase_norm(l, T)
            wv = awin_d[j].rearrange("(kc p) n -> p kc n", p=128)
            nk_d = nkp_d if kind == "p" else nks_d
            nv_d = nvp_d if kind == "p" else nvs_d
            ktile0 = 16 if kind == "s" else 0
            for hp in range(4):
                wb = wbuf_t
                wk = ("wbuf", 0)
                wk1 = ("wbuf", 1)
                cols = [(0, hp * 128, 128), (128, 512 + hp * 128, 128), (256, 1024 + hp * 128, 128),
                        (384, 1536 + hp * 128, 128), (512, 2048 + hp * 256, 256), (768, 3072 + hp * 256, 256)]
                for (o, c0, n) in cols:
                    dma("pool", wb[:, :, o:o + n], wv[:, :, c0:c0 + n], (), [wk, wk1])
                if kind == "s":
                    ckv = ck_d[j, si].rearrange("(t p) n -> p t n", p=128)
                    cvv = cv_d[j, si].rearrange("(t p) n -> p t n", p=128)
                    for m in range(2):
                        dma("pool", kctm[:, :, m, :], ckv[:, :, m * 512 + hp * 128:m * 512 + (hp + 1) * 128], (), ["kctm"])
                    for hh in range(2):
                        dma("pool", vaug[:, 0:16, hh, 0:128], cvv[:, :, hp * 256 + hh * 128:hp * 256 + (hh + 1) * 128],
                            (), ["vaug"])
                    for t in range(16):
                        ptb = pt[t % 2]
                        ptv = ptb[:, 0:256].rearrange("p (m k) -> p m k", m=2)
                        for m in range(2):
                            tr(ptv[:, m, :], kctm[:, t, m, :], identb[:, :], ["kctm", "identb"], [("pt", t % 2)])
                        cp("act" if t % 2 == 0 else "dve", kT[:, :, t * 128:(t + 1) * 128], ptv[:, :, :],
                           [("pt", t % 2)], ["kT"])
                for t in range(NT):
                    bank = pb[t % 2]
                    bk = ("pb", t % 2)
                    for kc in range(8):
                        mm(bank[:TP, :], hT[:, kc, t * TP:(t + 1) * TP], wb[:, kc, 256:768], kc == 0, kc == 7,
                           [wk, wk1, ("hT", t)], [bk])
                    qk_norm(bank, bk, TP, j, 1)
                    ko = kout[t % 2]
                    tt("pool", ko[:TP, :].rearrange("p (g d) -> p g d", g=4), kn[:TP, :].rearrange("p (g d) -> p g d", g=4),
                       qkn[:TP, j, 1, :].unsqueeze(1).broadcast_to([TP, 4, 64]), ALU.mult, ["kn", "qkn"], [("kout", t % 2)])
                    for m in range(2):
                        dma("sp", nk_d[j, si, t * TP:(t + 1) * TP, m * 512 + hp * 128:m * 512 + (hp + 1) * 128],
                            ko[:TP, m * 128:(m + 1) * 128], [("kout", t % 2)], ())
                    cp("act", qkb[:TP, :], ko[:TP, :], [("kout", t % 2)], ["qkb"])
                    ptv = pt[t % 2][:, 0:256].rearrange("p (m k) -> p m k", m=2)
                    for m in range(2):
                        tr(ptv[:, m, :TP], qkb[:TP, m * 128:(m + 1) * 128], identb[:TP, :TP], ["qkb", "identb"],
                           [("pt", t % 2)])
                    kt0 = ktile0 * 128 + t * TP
                    cp("dve", kT[:, :, kt0:kt0 + TP], ptv[:, :, :TP], [("pt", t % 2)], ["kT"])
                    vo = vout[t % 2]
                    cp("act", vo[:TP, :], bank[:TP, 256:512], [bk], [("vout", t % 2)])
                    dma("sp", nv_d[j, si, t * TP:(t + 1) * TP, hp * 256:(hp + 1) * 256], vo[:TP, :], [("vout", t % 2)], ())
                    cp("pool", vaug[:TP, ktile0 + t, :, 0:128], vo[:TP, :].rearrange("p (h e) -> p h e", h=2),
                       [("vout", t % 2)], ["vaug"])
                for jt in range(NT):
                    bank = pb[jt % 2]
                    bk = ("pb", jt % 2)
                    for kc in range(8):
                        mm(bank[:TP, 0:256], hT[:, kc, jt * TP:(jt + 1) * TP], wb[:, kc, 0:256], kc == 0, kc == 7,
                           [wk, wk1, ("hT", jt)], [bk])
                    for kc in range(8):
                        mm(bank[:TP, 256:512], hT[:, kc, jt * TP:(jt + 1) * TP], wb[:, kc, 768:1024], kc == 0, kc == 7,
                           [wk, wk1, ("hT", jt)], [bk])
                    qk_norm(bank, bk, TP, j, 0)
                    tt("pool", qkb[:TP, :].rearrange("p (g d) -> p g d", g=4), kn[:TP, :].rearrange("p (g d) -> p g d", g=4),
                       qkn[:TP, j, 0, :].unsqueeze(1).broadcast_to([TP, 4, 64]), ALU.mult, ["kn", "qkn"], ["qkb"])
                    ptv = pt[jt % 2][:, 0:256].rearrange("p (m k) -> p m k", m=2)
                    for m in range(2):
                        tr(ptv[:, m, :TP], qkb[:TP, m * 128:(m + 1) * 128], identb[:TP, :TP], ["qkb", "identb"],
                           [("pt", jt % 2)])
                    qT_ = qTt[jt % 2]
                    cp("dve", qT_[:, :, :TP], ptv[:, :, :TP], [("pt", jt % 2)], [("qTt", jt % 2)])
                    sz_ = szt[jt % 2]
                    act(sgt[:TP, :], bank[:TP, 256:512], AF.Exp, [bk], ["sgt"], scale=-1.0)
                    tsc("dve", sgt[:TP, :], sgt[:TP, :], 1.0, None, ALU.add, None, ["sgt"], ["sgt"])
                    S.op("dve", lambda e, _a=sgt[:TP, :]: e.reciprocal(out=_a, in_=_a), ["sgt"], ["sgt"], cost=350.0)
                    tt("dve", sz_[:TP, :], sgt[:TP, :], bank[:TP, 256:512], ALU.mult, ["sgt", bk], [("szt", jt % 2)])
                    for hh in range(2):
                        h = 2 * hp + hh
                        slope_h = _slopes()[h]
                        if kind == "p":
                            ktiles = [(i, 128, "off" if i < jt else "diag") for i in range(jt + 1)
                                      if i == jt or slope_h * ((jt - i - 1) * 128 + 1) <= SKIP_T]
                        else:
                            ktiles = [(i, 128, "off") for i in range(16)
                                      if slope_h * (PAST - (128 * i + 127)) <= SKIP_T] + [(16, 64, "diag")]
                        r0 = 64 * hh
                        pos = [pb[4 + m][:, hh * 129:(hh + 1) * 129] for m in range(2)]
                        poks = [("pb", 4), ("pb", 5)]
                        for idx, (i, nk, typ) in enumerate(ktiles):
                            sslot = uid[0] % 4
                            uid[0] += 1
                            psb_ = pb[2 + sslot % 2]
                            ps = psb_[:, 0:256].rearrange("p (m q) -> p m q", m=2)
                            psk = ("pb", 2 + sslot % 2)
                            for m in range(2):
                                mm(ps[:nk, m, :TP], kT[r0:r0 + 64, m, i * 128:i * 128 + nk], qT_[r0:r0 + 64, m, :TP],
                                   True, True, ["kT", ("qTt", jt % 2)], [psk])
                            P_ = Pt[sslot]
                            pk = ("Pt", sslot)
                            if typ == "off":
                                bias = btab_p[:nk, h, jt - i:jt - i + 1] if kind == "p" else btab_s[:nk, h, i:i + 1]
                                act(P_[:nk, :, :TP], ps[:nk, :, :TP], AF.Exp, [psk, "btab_p", "btab_s"], [pk],
                                    bias=bias, scale=0.125)
                            else:
                                bm = bmat_p[:nk, h, :TP] if kind == "p" else bmat_s[:nk, h, :TP]
                                stt("dve", dtmp[:nk, :, :TP], ps[:nk, :, :TP], 0.125,
                                    bm.unsqueeze(1).broadcast_to([nk, 2, TP]), ALU.mult, ALU.add,
                                    [psk, "bmat_p", "bmat_s"], ["dtmp"])
                                act(P_[:nk, :, :TP], dtmp[:nk, :, :TP], AF.Exp, ["dtmp"], [pk])
                            for m in range(2):
                                mm(pos[m][:TP, :], P_[:nk, m, :TP], vaug[:nk, i, hh, :], idx == 0, idx == len(ktiles) - 1,
                                   [pk, "vaug"], [poks[m]])
                        for m in range(2):
                            S.op("dve", lambda e, _o=rr[:TP, m:m + 1], _i=pos[m][:TP, 128:129]: e.reciprocal(out=_o, in_=_i),
                                 [poks[m]], ["rr"])
                        tt("dve", rl[:TP, 0:1], rr[:TP, 1:2], neglam[:TP, j:j + 1], ALU.mult, ["rr", "neglam"], ["rl"])
                        tsc("dve", t1[:TP, :], pos[1][:TP, 0:128], rl[:TP, 0:1], None, ALU.mult, None, [poks[1], "rl"], ["t1"])
                        stt("dve", oo[:TP, :], pos[0][:TP, 0:128], rr[:TP, 0:1], t1[:TP, :], ALU.mult, ALU.add,
                            [poks[0], "rr", "t1"], ["oo"])
                        mset("pool", ssn[:, 0:1], 0.0, ["ssn"])
                        act(junk[:TP, 0:128], oo[:TP, :], AF.Square, ["oo", "ssn"], ["junk", "ssn"], accum=ssn[:TP, 0:1])
                        rstd_ops("dve", rsn[:TP, 0:1], ssn[:TP, 0:1], 128.0, ["ssn"], ["rsn"])
                        stt("dve", ug[:TP, :], oo[:TP, :], rsn[:TP, 0:1], wsub[:TP, j, :], ALU.mult, ALU.mult,
                            ["oo", "rsn", "wsub"], ["ug"])
                        tt("pool", ub[:TP, :], ug[:TP, :], sz_[:TP, hh * 128:(hh + 1) * 128], ALU.mult,
                           ["ug", ("szt", jt % 2)], ["ub"])
                        us = uid[0] % 2
                        ptu = pt[us][:, 512:640]
                        tr(ptu[:, :TP], ub[:TP, :], identb[:TP, :TP], ["ub", "identb"], [("pt", us)])
                        cp("act", uT[:, h, jt * TP:(jt + 1) * TP], ptu[:, :TP], [("pt", us)], ["uT"])
            phase_outproj(awo_d, j, T, False)

        def qk_norm(bank, bk, TP, j, which):
            act(sq[:TP, :], bank[:TP, 0:256], AF.Identity, [bk], ["sq"])
            tt("dve", sq[:TP, :], sq[:TP, :], sq[:TP, :], ALU.mult, ["sq"], ["sq"])
            S.op("dve", lambda e, _o=ssq4[:TP, 0:4], _i=sq[:TP, :].rearrange("p (g d) -> p g d", g=4):
                 e.tensor_reduce(out=_o, in_=_i, axis=AX.X, op=ALU.add), ["sq"], ["ssq4"])
            rstd_ops("dve", rs4[:TP, 0:4], ssq4[:TP, 0:4], 64.0, ["ssq4"], ["rs4"])
            tt("dve", kn[:TP, :].rearrange("p (g d) -> p g d", g=4), bank[:TP, 0:256].rearrange("p (g d) -> p g d", g=4),
               rs4[:TP, 0:4].unsqueeze(2).broadcast_to([TP, 4, 64]), ALU.mult, [bk, "rs4"], ["kn"])

        seqs = [("p", i) for i in range(NP)] + [("s", i) for i in range(NS)]
        for kind, si in seqs:
            T = SEQ if kind == "p" else DEC_SEQ
            TP = min(T, 128)
            NT = T // TP
            xd = (xp_d if kind == "p" else xs_d)[si]
            yd = (yp_d if kind == "p" else ys_d)[si]
            for t in range(NT):
                dma("sp", xres[:TP, t, :], xd[t * TP:(t + 1) * TP, :], (), [("x", t)])
            for l in range(NL):
                if l % 2 == 0:
                    hgrn_layer(l, T, kind, si)
                else:
                    attn_layer(l, T, kind, si)
            for t in range(NT):
                dma("sp", yd[t * TP:(t + 1) * TP, :], xres[:TP, t, :], [("x", t)], ())
        S.emit(nc)
    return nc, len(S.ins)


_PROG_CACHE = {}


def _get_prog(NP, NS, NL=4):
    key = (NP, NS, NL)
    if key not in _PROG_CACHE:
        _PROG_CACHE[key] = build_program(NP, NS, NL)
    return _PROG_CACHE[key]


def make_in_maps(inputs, NP, NS, ncores):
    f = lambda a: np.ascontiguousarray(np.asarray(a, dtype=np.float32))
    c = _const_tables()
    shared = dict(
        normwT=f(np.asarray(inputs["norm_w"]).reshape(4, 8, 128).transpose(2, 0, 1)),
        hwin=f(inputs["hgrn_w_in"]),
        lbT=f(np.asarray(inputs["hgrn_lb_logits"]).reshape(2, 8, 128).transpose(2, 0, 1)),
        onwT=f(np.asarray(inputs["hgrn_onorm_w"]).reshape(2, 8, 128).transpose(2, 0, 1)),
        hwo=f(inputs["hgrn_w_out"]),
        awin=f(inputs["attn_w_in"]),
        qkn=f(np.broadcast_to(np.stack([np.asarray(inputs["attn_q_norm"]), np.asarray(inputs["attn_k_norm"])], axis=1)[None],
                              (128, 2, 2, 64))),
        lam=f(np.broadcast_to(np.asarray(inputs["attn_lambda"])[None], (128, 2, 4, 64))),
        sub=f(np.broadcast_to(np.asarray(inputs["attn_subln"])[None], (128, 2, 128))),
        awo=f(inputs["attn_w_out"]),
        btab_p=c["btab_p"], bmat_p=c["bmat_p"], btab_s=c["btab_s"], bmat_s=c["bmat_s"],
        tri=c["tri"], scanmask=c["scanmask"], ident=c["ident"],
    )
    xp = np.asarray(inputs["x_prompt"])
    xs = np.asarray(inputs["x_sample"])
    ck = np.asarray(inputs["cache_k"]).reshape(2, -1, PAST, D)
    cv = np.asarray(inputs["cache_v"]).reshape(2, -1, PAST, D)
    st = np.asarray(inputs["state_hgrn"])
    maps = []
    for c_ in range(ncores):
        m = dict(shared)
        m["xp"] = f(xp[c_ * NP:(c_ + 1) * NP]) if NP > 0 else np.zeros((1, SEQ, D), np.float32)
        if NS > 0:
            m["xs"] = f(xs[c_ * NS:(c_ + 1) * NS])
            m["ck"] = f(ck[:, c_ * NS:(c_ + 1) * NS])
            m["cv"] = f(cv[:, c_ * NS:(c_ + 1) * NS])
            m["st"] = f(st[:, c_ * NS:(c_ + 1) * NS])
        else:
            m["xs"] = np.zeros((1, DEC_SEQ, D), np.float32)
            m["ck"] = np.zeros((2, 1, PAST, D), np.float32)
            m["cv"] = np.zeros((2, 1, PAST, D), np.float32)
            m["st"] = np.zeros((2, 1, 8, 128, 128), np.float32)
        maps.append(m)
    return maps


def kernel(x_prompt, x_sample, cache_k, cache_v, state_hgrn, norm_w, hgrn_w_in, hgrn_lb_logits,
           hgrn_onorm_w, hgrn_w_out, attn_w_in, attn_q_norm, attn_k_norm, attn_lambda, attn_subln,
           attn_w_out):
    inputs = dict(x_prompt=x_prompt, x_sample=x_sample, cache_k=cache_k, cache_v=cache_v,
                  state_hgrn=state_hgrn, norm_w=norm_w, hgrn_w_in=hgrn_w_in,
                  hgrn_lb_logits=hgrn_lb_logits, hgrn_onorm_w=hgrn_onorm_w, hgrn_w_out=hgrn_w_out,
                  attn_w_in=attn_w_in, attn_q_norm=attn_q_norm, attn_k_norm=attn_k_norm,
                  attn_lambda=attn_lambda, attn_subln=attn_subln, attn_w_out=attn_w_out)
    B = np.asarray(x_prompt).shape[0]
    Bs = np.asarray(x_sample).shape[0]
    NP = B // NCORES
    NS = Bs // NCORES
    nc, _ = _get_prog(NP, NS)
    maps = make_in_maps(inputs, NP, NS, NCORES)
    res = run_bass_kernel_spmd(nc, maps, core_ids=list(range(NCORES)))
    R = res.results
    cat = lambda name, ax: np.concatenate([np.asarray(r[name]) for r in R], axis=ax)
    yp = cat("yp", 0)
    ys = cat("ys", 0)
    nkp = cat("nkp", 1).reshape(2, B, SEQ, 2, 8, 64)
    nvp = cat("nvp", 1).reshape(2, B, SEQ, 8, 128)
    nks = cat("nks", 1).reshape(2, Bs, DEC_SEQ, 2, 8, 64)
    nvs = cat("nvs", 1).reshape(2, Bs, DEC_SEQ, 8, 128)
    nsp = cat("nsp", 1)
    nss = cat("nss", 1)
    return (yp, ys, nkp, nvp, nks, nvs, nsp, nss)
```

```python
import contextlib
import math
import numpy as np
import concourse.bass as bass
import concourse.mybir as mybir
from concourse.bass_utils import run_bass_kernel_spmd

F32 = mybir.dt.float32
BF16 = mybir.dt.bfloat16
AF = mybir.ActivationFunctionType
ALU = mybir.AluOpType
AX = mybir.AxisListType

NCORES = 8
D = 1024
SEQ = 2048
DEC_SEQ = 64
PAST = 2048
EPS = 1e-6
K_DMA = 8
STOP = 0
SKIP_T = 180.0


class Sched:
    LAT_X = 250.0
    LAT_S = 80.0

    def __init__(self, reorder=True):
        self.ins = []
        self.last_w = {}
        self.readers = {}
        self.cur_bar = None
        self.reorder = reorder
        self.rkeys = set()
        self.wkeys = set()

    def op(self, eng, fn, reads=(), writes=(), dma=False, cost=100.0, bar=False):
        idx = len(self.ins)
        deps = set()
        self.rkeys.update(reads)
        self.wkeys.update(writes)
        for k in reads:
            w = self.last_w.get(k)
            if w is not None:
                deps.add(w)
            if isinstance(k, tuple) and k[0] in ("pb", "pt"):
                for r in self.readers.get(k, ()):
                    if self.ins[r]["eng"] != eng:
                        deps.add(r)
        for k in writes:
            w = self.last_w.get(k)
            if w is not None:
                deps.add(w)
            for r in self.readers.get(k, ()):
                deps.add(r)
        for k in reads:
            self.readers.setdefault(k, []).append(idx)
        for k in writes:
            self.last_w[k] = idx
            self.readers[k] = []
        if self.cur_bar is not None and not bar:
            deps.add(self.cur_bar.get(eng, self.cur_bar["dve"]))
        deps.discard(idx)
        self.ins.append(dict(eng=eng, fn=fn, deps=deps, dma=dma, cost=cost, bar=bar))
        return idx

    def barrier(self, dummies):
        nb = {}
        for e, fn in dummies.items():
            nb[e] = self.op(e, fn, (), [("bar", e)], bar=True)
        self.cur_bar = nb

    def _sched_segment(self, seg):
        import heapq
        ins = self.ins
        if not self.reorder or len(seg) < 3:
            return list(seg)
        segset = set(seg)
        indeg = {}
        succ = {}
        avail = {}
        for i in seg:
            ds = [d for d in ins[i]["deps"] if d in segset]
            indeg[i] = len(ds)
            avail[i] = 0.0
            for d in ds:
                succ.setdefault(d, []).append(i)
        engs = sorted(set(ins[i]["eng"] for i in seg))
        waiting = {e: [] for e in engs}
        ready = {e: [] for e in engs}
        busy = {e: 0.0 for e in engs}
        for i in seg:
            if indeg[i] == 0:
                heapq.heappush(ready[ins[i]["eng"]], i)
        order = []
        t = 0.0
        done = 0
        n = len(seg)
        finish = {}
        while done < n:
            progressed = False
            for e in engs:
                if busy[e] > t:
                    continue
                w = waiting[e]
                while w and w[0][0] <= t:
                    heapq.heappush(ready[e], heapq.heappop(w)[1])
                if not ready[e]:
                    continue
                i = heapq.heappop(ready[e])
                I = ins[i]
                c = I["cost"]
                if I["dma"]:
                    issue = 60.0 if e == "sp" else 700.0
                    busy[e] = t + issue
                    fin = t + c
                else:
                    busy[e] = t + c
                    fin = t + c
                finish[i] = fin
                order.append(i)
                done += 1
                progressed = True
                for sidx in succ.get(i, ()):
                    lat = self.LAT_S if ins[sidx]["eng"] == e and not I["dma"] else self.LAT_X
                    if e == "pe" and ins[sidx]["eng"] == "pe":
                        lat = 0.0
                    a = fin + lat
                    if a > avail[sidx]:
                        avail[sidx] = a
                    indeg[sidx] -= 1
                    if indeg[sidx] == 0:
                        heapq.heappush(waiting[ins[sidx]["eng"]], (avail[sidx], sidx))
            if not progressed:
                cand = []
                for e in engs:
                    if busy[e] > t:
                        cand.append(busy[e])
                    elif waiting[e]:
                        cand.append(max(waiting[e][0][0], busy[e]))
                assert cand, "scheduler stuck"
                t = min(cand)
        return order

    def schedule(self):
        ins = self.ins
        n = len(ins)
        order = []
        seg = []
        last_on = {}
        seg_dmas = []

        def flush():
            nonlocal seg
            o = self._sched_segment(seg)
            for i in o:
                last_on[ins[i]["eng"]] = i
                if ins[i]["dma"]:
                    seg_dmas.append(i)
            order.extend(o)
            seg = []

        i = 0
        while i < n:
            if ins[i]["bar"]:
                flush()
                prev = set(last_on.values()) | set(seg_dmas)
                seg_dmas.clear()
                while i < n and ins[i]["bar"]:
                    ins[i]["deps"] = set(prev)
                    order.append(i)
                    i += 1
                for k in order[-8:]:
                    if ins[k]["bar"]:
                        last_on[ins[k]["eng"]] = k
            else:
                seg.append(i)
                i += 1
        flush()
        assert len(order) == n and len(set(order)) == n
        return order

    def emit(self, nc, engines=("pe", "act", "dve", "pool", "sp")):
        print("keys read but never written:", sorted(map(str, self.rkeys - self.wkeys)))
        order = self.schedule()
        ins = [self.ins[i] for i in order]
        remap = {old: new for new, old in enumerate(order)}
        for I in ins:
            I["deps"] = set(remap[d] for d in I["deps"])
        n = len(ins)
        for i, I in enumerate(ins):
            for d in I["deps"]:
                assert d < i, "schedule violates a dependency"
        has_cons = [False] * n
        for I in ins:
            nd = set()
            for d in I["deps"]:
                Pp = ins[d]
                if (not Pp["dma"]) and Pp["eng"] == I["eng"] and I["eng"] == "pe":
                    continue
                nd.add(d)
            I["deps"] = nd
            for d in nd:
                has_cons[d] = True
        cnt = {e: 0 for e in engines}
        dcnt = {e: 0 for e in engines}
        for i, I in enumerate(ins):
            e = I["eng"]
            if I["dma"]:
                k = dcnt[e]
                dcnt[e] += 1
                I["sig"] = (("d", e, k % K_DMA), 16 * (k // K_DMA + 1))
                I["dma_k"] = k
            elif has_cons[i]:
                cnt[e] += 1
                I["sig"] = (("c", e), cnt[e])
            else:
                I["sig"] = None
        with contextlib.ExitStack() as st:
            sems = {}
            for e in engines:
                sems[("c", e)] = st.enter_context(nc.semaphore("c_" + e))
                if dcnt[e] > 0:
                    for j in range(K_DMA):
                        sems[("d", e, j)] = st.enter_context(nc.semaphore("d_%s_%d" % (e, j)))
            block = st.enter_context(nc.Block())

            def make(e):
                def body(eng):
                    seen = {}
                    for I in ins:
                        if I["eng"] != e:
                            continue
                        waits = {}
                        for d in I["deps"]:
                            s, v = ins[d]["sig"]
                            if waits.get(s, 0) < v:
                                waits[s] = v
                        if I["dma"] and I["dma_k"] >= K_DMA:
                            s, v = I["sig"]
                            if waits.get(s, 0) < v - 16:
                                waits[s] = v - 16
                        for s, v in waits.items():
                            if seen.get(s, 0) >= v:
                                continue
                            seen[s] = v
                            eng.wait_ge(sems[s], v)
                        bi = I["fn"](eng)
                        if I["sig"] is not None:
                            s, v = I["sig"]
                            bi.then_inc(sems[s], 16 if I["dma"] else 1)
                    if dcnt[e] > 0:
                        for j in range(K_DMA):
                            tot = (dcnt[e] - j + K_DMA - 1) // K_DMA
                            if tot > 0 and seen.get(("d", e, j), 0) < 16 * tot:
                                eng.wait_ge(sems[("d", e, j)], 16 * tot)
                return body

            reg = {"pe": block.tensor, "act": block.scalar, "dve": block.vector,
                   "pool": block.gpsimd, "sp": block.sync}
            for e in engines:
                if any(I["eng"] == e for I in ins):
                    reg[e](make(e))


def _slopes():
    return [2.0 ** (-8.0 * (h + 1) / 8.0) for h in range(8)]


def _const_tables():
    sl = np.array(_slopes(), np.float64)
    kk = np.arange(128)[:, None, None]
    dd = np.arange(16)[None, None, :]
    btab_p = -(sl[None, :, None]) * (128.0 * dd + 127.0 - kk)
    k = np.arange(128)[:, None, None]
    q = np.arange(128)[None, None, :]
    vis = (k // 64) <= (q // 64)
    bm = -(sl[None, :, None]) * np.abs(q - k) - sl[None, :, None] * (127.0 - q)
    bmat_p = np.where(vis, bm, -30000.0)
    ii = np.arange(16)[None, None, :]
    btab_s = -(sl[None, :, None]) * (2111.0 - 128.0 * ii - kk)
    k6 = np.arange(64)[:, None, None]
    q6 = np.arange(64)[None, None, :]
    bmat_s = -(sl[None, :, None]) * np.abs(q6 - k6) - sl[None, :, None] * (63.0 - q6)
    bmat_s_full = np.zeros((128, 8, 64))
    bmat_s_full[:64] = bmat_s
    tri = (np.arange(128)[:, None] <= np.arange(128)[None, :]).astype(np.float32)
    scanmask = np.ones((128, 512), np.float32)
    scanmask[:, ::128] = 0.0
    return dict(
        btab_p=btab_p.astype(np.float32), bmat_p=bmat_p.astype(np.float32),
        btab_s=btab_s.astype(np.float32), bmat_s=bmat_s_full.astype(np.float32),
        tri=tri, scanmask=scanmask, ident=np.eye(128, dtype=np.float32),
    )


def build_program(NP, NS, NL=4):
    nc = bass.Bass("TRN2", target_bir_lowering=False)

    def din(name, shape):
        return nc.dram_tensor(name, list(shape), F32, kind="ExternalInput").ap()

    def dout(name, shape):
        return nc.dram_tensor(name, list(shape), F32, kind="ExternalOutput").ap()

    xp_d = din("xp", [max(NP, 1), SEQ, D])
    xs_d = din("xs", [max(NS, 1), DEC_SEQ, D])
    ck_d = din("ck", [2, max(NS, 1), PAST, D])
    cv_d = din("cv", [2, max(NS, 1), PAST, D])
    st_d = din("st", [2, max(NS, 1), 8, 128, 128])
    normwT_d = din("normwT", [128, 4, 8])
    hwin_d = din("hwin", [2, D, 4 * D])
    lbT_d = din("lbT", [128, 2, 8])
    onwT_d = din("onwT", [128, 2, 8])
    hwo_d = din("hwo", [2, D, D])
    awin_d = din("awin", [2, D, 4 * D])
    qkn_d = din("qkn", [128, 2, 2, 64])
    lam_d = din("lam", [128, 2, 4, 64])
    sub_d = din("sub", [128, 2, 128])
    awo_d = din("awo", [2, D, D])
    btab_p_d = din("btab_p", [128, 8, 16])
    bmat_p_d = din("bmat_p", [128, 8, 128])
    btab_s_d = din("btab_s", [128, 8, 16])
    bmat_s_d = din("bmat_s", [128, 8, 64])
    tri_d = din("tri", [128, 128])
    scanmask_d = din("scanmask", [128, 512])
    ident_d = din("ident", [128, 128])

    yp_d = dout("yp", [max(NP, 1), SEQ, D])
    ys_d = dout("ys", [max(NS, 1), DEC_SEQ, D])
    nkp_d = dout("nkp", [2, max(NP, 1), SEQ, D])
    nvp_d = dout("nvp", [2, max(NP, 1), SEQ, D])
    nks_d = dout("nks", [2, max(NS, 1), DEC_SEQ, D])
    nvs_d = dout("nvs", [2, max(NS, 1), DEC_SEQ, D])
    nsp_d = dout("nsp", [2, max(NP, 1), 8, 128, 128])
    nss_d = dout("nss", [2, max(NS, 1), 8, 128, 128])

    S = Sched()
    uid = [0]
    ucnt = [0]

    with contextlib.ExitStack() as stk:
        def sb(name, shape, dt):
            return stk.enter_context(nc.sbuf_tensor("s_" + name, list(shape), dt))

        def psb(name, shape, dt):
            return stk.enter_context(nc.psum_tensor("p_" + name, list(shape), dt))

        xres = sb("xres", [128, 16, D], F32)
        hT = sb("hT", [128, 8, SEQ], BF16)
        uT = sb("uT", [128, 8, SEQ], BF16)
        wbuf_t = sb("wbuf", [128, 8, 1024], BF16)
        wbuf = [wbuf_t[:, :, 0:512], wbuf_t[:, :, 512:1024]]
        hb = [sb("hb0", [128, D], BF16)] * 2
        ssx = sb("ssx", [128, 16], F32)
        rstdx = sb("rstdx", [128, 16], F32)
        ssacc = sb("ssacc", [128, 16], F32)
        rstdo = sb("rstdo", [128, 16], F32)
        bar_d = sb("bar_d", [128, 4], F32)
        identb = sb("identb", [128, 128], BF16)
        tri = sb("tri", [128, 128], F32)
        scanmask = sb("scanmask", [128, 512], F32)
        ones1 = sb("ones1", [128, 1], F32)
        normwT = sb("normwT", [128, 4, 8], F32)
        lbraw = sb("lbraw", [128, 2, 8], F32)
        lbT = sb("lbT", [128, 2, 8], F32)
        omlT = sb("omlT", [128, 2, 8], F32)
        onwT = sb("onwT", [128, 2, 8], F32)
        qkn = sb("qkn", [128, 2, 2, 64], F32)
        lams = sb("lams", [128, 2, 2], F32)
        lame = sb("lame", [128, 2, 2], F32)
        neglam = sb("neglam", [128, 2], F32)
        wsub = sb("wsub", [128, 2, 128], F32)
        btab_p = sb("btab_p", [128, 8, 16], F32)
        bmat_p = sb("bmat_p", [128, 8, 128], F32)
        btab_s = sb("btab_s", [128, 8, 16], F32)
        bmat_s = sb("bmat_s", [128, 8, 64], F32)
        Sall = sb("Sall", [128, 17, 128], BF16)
        S32 = [sb("S32a", [128, 128], F32), sb("S32b", [128, 128], F32)]
        SCR_BYTES = 43008
        scr = sb("scr", [128, SCR_BYTES // 2], BF16)

        class Carver:
            def __init__(self):
                self.off = 0

            def __call__(self, shape, dt):
                n = 1
                for x in shape[1:]:
                    n *= x
                nb = n * (4 if dt == F32 else 2)
                nb_al = (nb + 31) // 32 * 32
                assert self.off + nb_al <= SCR_BYTES, (self.off, nb_al)
                ap = scr[:, self.off // 2:(self.off + nb) // 2]
                self.off += nb_al
                if dt == F32:
                    ap = ap.bitcast(F32)
                if len(shape) == 3:
                    ap = ap.rearrange("p (a b) -> p a b", a=shape[1])
                elif len(shape) == 4:
                    ap = ap.rearrange("p (a b c) -> p a b c", a=shape[1], b=shape[2])
                assert tuple(ap.shape) == tuple(shape), (ap.shape, shape)
                return ap

        cv_ = Carver()
        junk = cv_([128, D], BF16)
        base_off = cv_.off
        lamv = cv_([128, 2, 4, 64], F32)
        lamp = cv_([128, 2, 2, 64], F32)
        cv_.off = base_off
        qf = cv_([128, 512], F32)
        ff = cv_([128, 512], F32)
        gg = cv_([128, 512], F32)
        kf = cv_([128, 512], F32)
        bbuf = cv_([128, 512], F32)
        ee = [cv_([128, 512], F32), cv_([128, 512], F32)]
        ex = [cv_([128, 512], F32), cv_([128, 512], F32)]
        osq = cv_([128, 512], F32)
        qt = cv_([128, 512], BF16)
        qh = cv_([128, 512], BF16)
        khT = cv_([128, 512], BF16)
        szTs = [cv_([128, 512], BF16), cv_([128, 512], BF16)]
        Kb = [cv_([128, 4, 128], BF16) for i in range(4)]
        khtm = cv_([128, 4, 128], BF16)
        vtms = [cv_([128, 4, 128], BF16), cv_([128, 4, 128], BF16)]
        attm = cv_([128, 4, 128], BF16)
        dec = cv_([128, 8], F32)
        hg_end = cv_.off
        cv_.off = base_off
        kT = cv_([128, 2, SEQ + 64], BF16)
        vaug = cv_([128, 17, 2, 129], BF16)
        kctm = cv_([128, 16, 2, 128], BF16)
        sq = cv_([128, 256], F32)
        ssq4 = cv_([128, 8], F32)
        rs4 = cv_([128, 8], F32)
        kn = cv_([128, 256], F32)
        kout = [cv_([128, 256], F32), cv_([128, 256], F32)]
        vout = [cv_([128, 256], F32), cv_([128, 256], F32)]
        qkb = cv_([128, 256], BF16)
        qTt = [cv_([128, 2, 128], BF16), cv_([128, 2, 128], BF16)]
        szt = [cv_([128, 256], BF16), cv_([128, 256], BF16)]
        Pt = [cv_([128, 2, 128], BF16) for i in range(4)]
        dtmp = cv_([128, 2, 128], F32)
        rr = cv_([128, 8], F32)
        rl = cv_([128, 8], F32)
        t1 = cv_([128, 128], F32)
        oo = cv_([128, 128], F32)
        ssn = cv_([128, 8], F32)
        rsn = cv_([128, 8], F32)
        ug = cv_([128, 128], F32)
        ub = cv_([128, 128], BF16)
        sgt = cv_([128, 256], F32)
        at_end = cv_.off
        print("scratch bytes: hgrn", hg_end, "attn", at_end)
        pb = [psb("pb%d" % i, [128, 512], F32) for i in range(6)]
        pt = [psb("pt%d" % i, [128, 1024], BF16) for i in range(2)]

        def fsz(ap):
            n = 1
            for x in tuple(ap.shape)[1:]:
                n *= x
            return float(n)

        def mm(out, lhsT, rhs, start, stop, r, w):
            nn = fsz(out)
            c = max(nn, 64.0) / 2.2 + 12.0
            if lhsT.dtype == F32:
                c *= 4.0
            S.op("pe", lambda e: e.matmul(out, lhsT=lhsT, rhs=rhs, start=start, stop=stop), r, w, cost=c)

        def tr(out, in_, ident, r, w):
            S.op("pe", lambda e: e.transpose(out=out, in_=in_, identity=ident), r, w, cost=max(fsz(out), 64.0) / 2.2 + 40.0)

        def act(out, in_, func, r, w, bias=None, scale=None, accum=None):
            kw = {}
            if bias is not None:
                kw["bias"] = bias
            if scale is not None:
                kw["scale"] = scale
            if accum is not None:
                kw["accum_out"] = accum
            S.op("act", lambda e: e.activation(out=out, in_=in_, func=func, **kw), r, w, cost=(224.0 + fsz(out)) / 1.4)

        def vcost(eng, out):
            if eng == "pool":
                return 120.0 + 2.1 * fsz(out)
            return (70.0 + fsz(out)) / 0.96

        def tt(eng, out, in0, in1, op, r, w):
            S.op(eng, lambda e: e.tensor_tensor(out=out, in0=in0, in1=in1, op=op), r, w, cost=vcost(eng, out))

        def tsc(eng, out, in0, s1, s2, op0, op1, r, w):
            if op1 is None:
                S.op(eng, lambda e: e.tensor_scalar(out=out, in0=in0, scalar1=s1, scalar2=None, op0=op0), r, w,
                     cost=vcost(eng, out))
            else:
                S.op(eng, lambda e: e.tensor_scalar(out=out, in0=in0, scalar1=s1, scalar2=s2, op0=op0, op1=op1), r, w,
                     cost=vcost(eng, out))

        def stt(eng, out, in0, scalar, in1, op0, op1, r, w):
            S.op(eng, lambda e: e.scalar_tensor_tensor(out=out, in0=in0, scalar=scalar, in1=in1, op0=op0, op1=op1), r, w,
                 cost=vcost(eng, out))

        def cp(eng, out, in_, r, w):
            if eng == "act":
                S.op("act", lambda e: e.copy(out=out, in_=in_), r, w, cost=(224.0 + fsz(out)) / 1.4)
            else:
                S.op(eng, lambda e: e.tensor_copy(out=out, in_=in_), r, w, cost=vcost(eng, out))

        def mset(eng, ap, val, w):
            S.op(eng, lambda e: e.memset(ap, val), (), w, cost=vcost(eng, ap) * 0.5)

        def dma(eng, out, in_, r, w):
            nb = fsz(out) * float(out.shape[0]) * 4.0
            S.op(eng, lambda e: e.dma_start(out=out, in_=in_), r, w, dma=True, cost=2000.0 + nb / 120.0)

        def rstd_ops(eng, out, in_, n, r, w):
            tsc(eng, out, in_, 1.0 / n, EPS, ALU.mult, ALU.add, r, w)
            act(out, out, AF.Ln, w, w)
            act(out, out, AF.Exp, w, w, scale=-0.5)

        dma("sp", tri[:], tri_d, (), ["tri"])
        dma("sp", scanmask[:], scanmask_d, (), ["scanmask"])
        dma("pool", identb[:], ident_d, (), ["identb"])
        dma("sp", normwT[:], normwT_d, (), ["normwT"])
        dma("sp", lbraw[:], lbT_d, (), ["lbraw"])
        dma("sp", onwT[:], onwT_d, (), ["onwT"])
        dma("sp", qkn[:], qkn_d, (), ["qkn"])
        dma("sp", lamv[:], lam_d, (), ["lamv"])
        dma("sp", wsub[:], sub_d, (), ["wsub"])
        dma("sp", btab_p[:], btab_p_d, (), ["btab_p"])
        dma("sp", bmat_p[:], bmat_p_d, (), ["bmat_p"])
        dma("sp", btab_s[:], btab_s_d, (), ["btab_s"])
        dma("sp", bmat_s[:], bmat_s_d, (), ["bmat_s"])
        mset("dve", ones1[:], 1.0, ["ones1"])
        mset("dve", lbT[:], 0.0, ["lbT"])
        tt("dve", lbT[:, 1, :], lbraw[:, 1, :], lbraw[:, 0, :], ALU.subtract, ["lbraw"], ["lbT"])
        act(lbT[:, 1, :], lbT[:, 1, :], AF.Exp, ["lbT"], ["lbT"], scale=-1.0)
        tsc("dve", lbT[:, 1, :], lbT[:, 1, :], 1.0, None, ALU.add, None, ["lbT"], ["lbT"])
        S.op("dve", lambda e: e.reciprocal(out=lbT[:, 1, :], in_=lbT[:, 1, :]), ["lbT"], ["lbT"])
        tsc("dve", omlT[:], lbT[:], -1.0, 1.0, ALU.mult, ALU.add, ["lbT"], ["omlT"])
        tt("dve", lamp[:, :, 0, :], lamv[:, :, 0, :], lamv[:, :, 1, :], ALU.mult, ["lamv"], ["lamp"])
        tt("dve", lamp[:, :, 1, :], lamv[:, :, 2, :], lamv[:, :, 3, :], ALU.mult, ["lamv"], ["lamp"])
        S.op("dve", lambda e: e.tensor_reduce(out=lams[:], in_=lamp[:], axis=AX.X, op=ALU.add), ["lamp"], ["lams"])
        act(lame[:], lams[:], AF.Exp, ["lams"], ["lame"])
        for j in range(2):
            lam_init = 0.8 - 0.6 * math.exp(-0.3 * (2 * j + 1))
            tt("dve", neglam[:, j:j + 1], lame[:, j, 1:2], lame[:, j, 0:1], ALU.subtract, ["lame"], ["neglam"])
            tsc("dve", neglam[:, j:j + 1], neglam[:, j:j + 1], -lam_init, None, ALU.add, None, ["neglam"], ["neglam"])
            tsc("dve", wsub[:, j, :], wsub[:, j, :], 1.0 - lam_init, None, ALU.mult, None, ["wsub"], ["wsub"])

        def fence():
            S.barrier({
                "act": lambda e: e.copy(out=bar_d[:, 0:1], in_=ones1[:, 0:1]),
                "dve": lambda e: e.memset(bar_d[:, 1:2], 0.0),
                "pool": lambda e: e.memset(bar_d[:, 2:3], 0.0),
            })

        def phase_norm(l, T):
            TP = min(T, 128)
            NT = T // TP
            mset("dve", ssx[:], 0.0, ["ssx"])
            for t in range(NT):
                act(junk[:TP, :], xres[:TP, t, :], AF.Square, [("x", t)], ["junk", "ssx"], accum=ssx[:TP, t:t + 1])
            rstd_ops("dve", rstdx[:TP, :NT], ssx[:TP, :NT], float(D), ["ssx"], ["rstdx"])
            for t in range(NT):
                hbt = hb[t % 2]
                act(hbt[:TP, :], xres[:TP, t, :], AF.Identity, [("x", t), "rstdx"], [("hb", 0)],
                    scale=rstdx[:TP, t:t + 1])
                ptb = pt[t % 2]
                ptv = ptb[:, :].rearrange("p (a b) -> p a b", a=8)
                for kc in range(8):
                    tr(ptv[:, kc, :TP], hbt[:TP, kc * 128:(kc + 1) * 128], identb[:TP, :TP],
                       [("hb", 0), "identb"], [("pt", t % 2)])
                tt("dve", hT[:, :, t * TP:(t + 1) * TP], ptv[:, :, :TP],
                   normwT[:, l, :].unsqueeze(2).broadcast_to([128, 8, TP]), ALU.mult,
                   [("pt", t % 2), "normwT"], [("hT", t)])

        def phase_outproj(w_d, j, T, use_rstd):
            TP = min(T, 128)
            NT = T // TP
            wv = w_d[j].rearrange("(kc p) n -> p kc n", p=128)
            for half in range(2):
                dma("pool", wbuf[half][:, :, 0:512], wv[:, :, half * 512:(half + 1) * 512], (), [("wbuf", half)])
            for t in range(NT):
                for half in range(2):
                    bank = pb[(2 * t + half) % 2]
                    bk = ("pb", (2 * t + half) % 2)
                    for kc in range(8):
                        mm(bank[:TP, :], uT[:, kc, t * TP:(t + 1) * TP], wbuf[half][:, kc, 0:512],
                           kc == 0, kc == 7, ["uT", ("wbuf", half)], [bk])
                    xo = xres[:TP, t, half * 512:(half + 1) * 512]
                    if use_rstd:
                        stt("dve", xo, bank[:TP, :], rstdo[:TP, t:t + 1], xo, ALU.mult, ALU.add,
                            [bk, "rstdo", ("x", t)], [("x", t)])
                    else:
                        tt("dve", xo, bank[:TP, :], xo, ALU.add, [bk, ("x", t)], [("x", t)])

        def hgrn_layer(l, T, kind, si):
            j = l // 2
            CH = min(T, 128)
            NCHT = T // CH
            SEGT = min(T, 512)
            NSEG = T // SEGT
            NCH = SEGT // CH
            NSB = CH // 32
            fence()
            for i in range(4):
                mset("pool", Kb[i], 0.0, [("Kb", i)])
            phase_norm(l, T)
            wv = hwin_d[j].rearrange("(kc p) n -> p kc n", p=128)
            for h in range(8):
                slot = h % 2
                wb = wbuf[slot]
                wk = ("wbuf", slot)
                for qi in range(4):
                    dma("pool", wb[:, :, qi * 128:(qi + 1) * 128],
                        wv[:, :, qi * 1024 + h * 128: qi * 1024 + (h + 1) * 128], (), [wk])
                if kind == "p":
                    mset("pool", S32[0][:], 0.0, [("S32", 0)])
                    mset("pool", Sall[:, 0, :], 0.0, ["Sall"])
                else:
                    dma("sp", S32[0][:], st_d[j, si, h], (), [("S32", 0)])
                    cp("pool", Sall[:, 0, :], S32[0][:], [("S32", 0)], ["Sall"])
                for seg in range(NSEG):
                    t0 = seg * SEGT
                    for qi, bi in ((0, 0), (1, 1), (3, 2)):
                        for kc in range(8):
                            mm(pb[bi][:, :SEGT], wb[:, kc, qi * 128:(qi + 1) * 128], hT[:, kc, t0:t0 + SEGT],
                               kc == 0, kc == 7, [wk] + [("hT", tt_) for tt_ in range(t0 // CH, (t0 + SEGT) // CH)], [("pb", bi)])
                    act(qf[:, :SEGT], pb[0][:, :SEGT], AF.Identity, [("pb", 0)], ["qf"], scale=128.0 ** -0.5)
                    act(gg[:, :SEGT], pb[1][:, :SEGT], AF.Exp, [("pb", 1)], ["gg"], scale=-1.0)
                    act(gg[:, :SEGT], gg[:, :SEGT], AF.Ln, ["gg"], ["gg"], bias=1.0)
                    act(ff[:, :SEGT], gg[:, :SEGT], AF.Exp, ["gg"], ["ff"], scale=-1.0)
                    act(ex[1][:, :SEGT], pb[2][:, :SEGT], AF.Exp, [("pb", 2)], [("ex", 1)], scale=-1.0)
                    act(ex[1][:, :SEGT], ex[1][:, :SEGT], AF.Ln, [("ex", 1)], [("ex", 1)], bias=1.0)
                    act(ex[1][:, :SEGT], ex[1][:, :SEGT], AF.Exp, [("ex", 1)], [("ex", 1)], scale=-1.0)
                    szT = szTs[ucnt[0] % 2]
                    szk = ("szT", ucnt[0] % 2)
                    vtm = vtms[ucnt[0] % 2]
                    vtk = ("vtm", ucnt[0] % 2)
                    ucnt[0] += 1
                    tt("dve", szT[:, :SEGT], ex[1][:, :SEGT], pb[2][:, :SEGT], ALU.mult, [("ex", 1), ("pb", 2)], [szk])
                    tsc("dve", ff[:, :SEGT], ff[:, :SEGT], omlT[:, j, h:h + 1], lbT[:, j, h:h + 1], ALU.mult, ALU.add,
                        ["ff", "omlT", "lbT"], ["ff"])
                    act(gg[:, :SEGT], ff[:, :SEGT], AF.Ln, ["ff"], ["gg"])
                    tsc("pool", kf[:, :SEGT], ff[:, :SEGT], -1.0, 1.0, ALU.mult, ALU.add, ["ff"], ["kf"])
                    pv = pb[3][:, :].rearrange("p (c v) -> p c v", c=4)
                    for c in range(NCH):
                        for kc in range(8):
                            mm(pv[:CH, c, :], hT[:, kc, t0 + c * CH:t0 + (c + 1) * CH], wb[:, kc, 256:384],
                               kc == 0, kc == 7, [wk, ("hT", t0 // CH + c)], [("pb", 3)])
                    cp("act", vtm[:CH, :NCH, :], pv[:CH, :NCH, :], [("pb", 3)], [vtk])
                    S.op("dve", lambda e, _o=bbuf[:, :SEGT], _m=scanmask[:, :SEGT], _g=gg[:, :SEGT]:
                         e.tensor_tensor_scan(out=_o, data0=_m, data1=_g, initial=0.0, op0=ALU.mult, op1=ALU.add),
                         ["scanmask", "gg"], ["bb"], cost=(70.0 + SEGT) / 0.96)
                    b3 = bbuf[:, :SEGT].rearrange("p (c t) -> p c t", c=NCH)
                    b4 = bbuf[:, :SEGT].rearrange("p (c i t) -> p c i t", c=NCH, i=NSB)
                    e0 = ee[0]
                    e04 = e0[:, :SEGT].rearrange("p (c i t) -> p c i t", c=NCH, i=NSB)
                    if NSB > 1:
                        tt("dve", e04[:, :, 1:NSB, :], b4[:, :, 1:NSB, :],
                           b4[:, :, 0:NSB - 1, 31:32].broadcast_to([128, NCH, NSB - 1, 32]), ALU.subtract,
                           ["bb"], [("ee", 0)])
                    cp("pool", e04[:, :, 0, :], b4[:, :, 0, :], ["bb"], [("ee", 0)])
                    act(ex[0][:, :SEGT], e0[:, :SEGT], AF.Exp, [("ee", 0)], [("ex", 0)])
                    tt("dve", qt[:, :SEGT], qf[:, :SEGT], ex[0][:, :SEGT], ALU.mult, ["qf", ("ex", 0)], ["qt"])
                    act(ex[1][:, :SEGT], bbuf[:, :SEGT], AF.Exp, ["bb"], [("ex", 1)])
                    tt("pool", qh[:, :SEGT], qf[:, :SEGT], ex[1][:, :SEGT], ALU.mult, ["qf", ("ex", 1)], ["qh"])
                    kf3 = kf[:, :SEGT].rearrange("p (c t) -> p c t", c=NCH)
                    for i in range(NSB):
                        wi = 32 * (i + 1)
                        b_ = i % 2
                        ex3 = ex[b_][:, :SEGT].rearrange("p (c t) -> p c t", c=NCH)
                        if i == 0:
                            act(ex3[:, :, 0:wi], b3[:, :, 0:wi], AF.Exp, ["bb"], [("ex", b_)], scale=-1.0)
                        else:
                            ee3 = ee[b_][:, :SEGT].rearrange("p (c t) -> p c t", c=NCH)
                            tt("dve", ee3[:, :, 0:wi], b3[:, :, 32 * i - 1:32 * i].broadcast_to([128, NCH, wi]),
                               b3[:, :, 0:wi], ALU.subtract, ["bb"], [("ee", b_)])
                            act(ex3[:, :, 0:wi], ee3[:, :, 0:wi], AF.Exp, [("ee", b_)], [("ex", b_)])
                        tt("dve" if i % 2 == 0 else "pool", Kb[i][:, :NCH, 0:wi], kf3[:, :, 0:wi], ex3[:, :, 0:wi], ALU.mult,
                           ["kf", ("ex", b_)], [("Kb", i)])
                    ee3 = ee[0][:, :SEGT].rearrange("p (c t) -> p c t", c=NCH)
                    tt("dve", ee3[:, :, :], b3[:, :, CH - 1:CH].broadcast_to([128, NCH, CH]), b3[:, :, :], ALU.subtract,
                       ["bb"], [("ee", 0)])
                    act(ex[0][:, :SEGT], ee[0][:, :SEGT], AF.Exp, [("ee", 0)], [("ex", 0)])
                    tt("pool", khT[:, :SEGT], kf[:, :SEGT], ex[0][:, :SEGT], ALU.mult, ["kf", ("ex", 0)], ["khT"])
                    act(dec[:, :NCH], b3[:, :, CH - 1], AF.Exp, ["bb"], ["dec"])
                    ptv = pt[0][:, 0:512].rearrange("p (c k) -> p c k", c=4)
                    for c in range(NCH):
                        tr(ptv[:CH, c, :], khT[:, c * CH:(c + 1) * CH], identb[:, :], ["khT", "identb"], [("pt", 0)])
                    cp("act", khtm[:CH, :NCH, :], ptv[:CH, :NCH, :], [("pt", 0)], ["khtm"])
                    for c in range(NCH):
                        cg = seg * NCH + c
                        ub_ = 4 if c % 2 == 0 else 2
                        pu = pb[ub_][:, 0:128]
                        mm(pu, khtm[:CH, c, :], vtm[:CH, c, :], True, True, ["khtm", vtk], [("pb", ub_)])
                        s_old, s_new = S32[cg % 2], S32[(cg + 1) % 2]
                        stt("dve", s_new[:], s_old[:], dec[:, c:c + 1], pu, ALU.mult, ALU.add,
                            [("S32", cg % 2), "dec", ("pb", ub_)], [("S32", (cg + 1) % 2)])
                        cp("pool", Sall[:, cg + 1, :], s_new[:], [("S32", (cg + 1) % 2)], ["Sall"])
                    pa = pb[5][:, :].rearrange("p (c t) -> p c t", c=4)
                    for c in range(NCH):
                        for i in range(NSB):
                            mm(pa[:CH, c, 32 * i:32 * (i + 1)], Kb[i][:, c, 0:CH], qt[:, c * CH + 32 * i:c * CH + 32 * (i + 1)],
                               True, True, [("Kb", i), "qt"], [("pb", 5)])
                    tt("dve", attm[:CH, :NCH, :CH], pa[:CH, :NCH, :CH],
                       tri[:CH, :CH].unsqueeze(1).broadcast_to([CH, NCH, CH]), ALU.mult,
                       [("pb", 5), "tri"], ["attm"])
                    po = pt[1][:, :].bitcast(F32)
                    pok_ = ("pt", 1)
                    for c in range(NCH):
                        cg = seg * NCH + c
                        mm(po[:, c * CH:(c + 1) * CH], vtm[:CH, c, :], attm[:CH, c, :CH], True, False,
                           [vtk, "attm"], [pok_])
                        mm(po[:, c * CH:(c + 1) * CH], Sall[:, cg, :], qh[:, c * CH:(c + 1) * CH], False, True,
                           ["Sall", "qh"], [pok_])
                    act(osq[:, :SEGT], po[:, :SEGT], AF.Identity, [pok_], ["osq", "po_rd"])
                    tt("dve", osq[:, :SEGT], osq[:, :SEGT], osq[:, :SEGT], ALU.mult, ["osq"], ["osq"])
                    stt("dve", uT[:, h, t0:t0 + SEGT], po[:, :SEGT], onwT[:, j, h:h + 1], szT[:, :SEGT], ALU.mult, ALU.mult,
                        [pok_, "onwT", szk, "po_rd"], ["uT"])
                    pss = pb[4][:, 256:264]
                    for c in range(NCH):
                        mm(pss[:CH, c:c + 1], osq[:, c * CH:(c + 1) * CH], ones1[:, 0:1], True, True,
                           ["osq", "ones1"], [("pb", 4)])
                    cg0 = seg * NCH
                    if h == 0:
                        cp("dve", ssacc[:CH, cg0:cg0 + NCH], pss[:CH, 0:NCH], [("pb", 4)], ["ssacc"])
                    else:
                        tt("dve", ssacc[:CH, cg0:cg0 + NCH], pss[:CH, 0:NCH], ssacc[:CH, cg0:cg0 + NCH], ALU.add,
                           [("pb", 4), "ssacc"], ["ssacc"])
                od = nsp_d if kind == "p" else nss_d
                dma("sp", od[j, si, h], S32[NCHT % 2][:], [("S32", NCHT % 2)], ())
            rstd_ops("dve", rstdo[:CH, :NCHT], ssacc[:CH, :NCHT], float(D), ["ssacc"], ["rstdo"])
            phase_outproj(hwo_d, j, T, True)

        def attn_layer(l, T, kind, si):
            j = l // 2
            TP = min(T, 128)
            NT = T // TP
            fence()
            mset("pool", vaug, 1.0, ["vaug"])
            phase_norm(l, T)
            wv = awin_d[j].rearrange("(kc p) n -> p kc n", p=128)
            nk_d = nkp_d if kind == "p" else nks_d
            nv_d = nvp_d if kind == "p" else nvs_d
            ktile0 = 16 if kind == "s" else 0
            for hp in range(4):
                wb = wbuf_t
                wk = ("wbuf", 0)
                wk1 = ("wbuf", 1)
                cols = [(0, hp * 128, 128), (128, 512 + hp * 128, 128), (256, 1024 + hp * 128, 128),
                        (384, 1536 + hp * 128, 128), (512, 2048 + hp * 256, 256), (768, 3072 + hp * 256, 256)]
                for (o, c0, n) in cols:
                    dma("pool", wb[:, :, o:o + n], wv[:, :, c0:c0 + n], (), [wk, wk1])
                if kind == "s":
                    ckv = ck_d[j, si].rearrange("(t p) n -> p t n", p=128)
                    cvv = cv_d[j, si].rearrange("(t p) n -> p t n", p=128)
                    for m in range(2):
                        dma("pool", kctm[:, :, m, :], ckv[:, :, m * 512 + hp * 128:m * 512 + (hp + 1) * 128], (), ["kctm"])
                    for hh in range(2):
                        dma("pool", vaug[:, 0:16, hh, 0:128], cvv[:, :, hp * 256 + hh * 128:hp * 256 + (hh + 1) * 128],
                            (), ["vaug"])
                    for t in range(16):
                        ptb = pt[t % 2]
                        ptv = ptb[:, 0:256].rearrange("p (m k) -> p m k", m=2)
                        for m in range(2):
                            tr(ptv[:, m, :], kctm[:, t, m, :], identb[:, :], ["kctm", "identb"], [("pt", t % 2)])
                        cp("act" if t % 2 == 0 else "dve", kT[:, :, t * 128:(t + 1) * 128], ptv[:, :, :],
                           [("pt", t % 2)], ["kT"])
                for t in range(NT):
                    bank = pb[t % 2]
                    bk = ("pb", t % 2)
                    for kc in range(8):
                        mm(bank[:TP, :], hT[:, kc, t * TP:(t + 1) * TP], wb[:, kc, 256:768], kc == 0, kc == 7,
                           [wk, wk1, ("hT", t)], [bk])
                    qk_norm(bank, bk, TP, j, 1)
                    ko = kout[t % 2]
                    tt("pool", ko[:TP, :].rearrange("p (g d) -> p g d", g=4), kn[:TP, :].rearrange("p (g d) -> p g d", g=4),
                       qkn[:TP, j, 1, :].unsqueeze(1).broadcast_to([TP, 4, 64]), ALU.mult, ["kn", "qkn"], [("kout", t % 2)])
                    for m in range(2):
                        dma("sp", nk_d[j, si, t * TP:(t + 1) * TP, m * 512 + hp * 128:m * 512 + (hp + 1) * 128],
                            ko[:TP, m * 128:(m + 1) * 128], [("kout", t % 2)], ())
                    cp("act", qkb[:TP, :], ko[:TP, :], [("kout", t % 2)], ["qkb"])
                    ptv = pt[t % 2][:, 0:256].rearrange("p (m k) -> p m k", m=2)
                    for m in range(2):
                        tr(ptv[:, m, :TP], qkb[:TP, m * 128:(m + 1) * 128], identb[:TP, :TP], ["qkb", "identb"],
                           [("pt", t % 2)])
                    kt0 = ktile0 * 128 + t * TP
                    cp("dve", kT[:, :, kt0:kt0 + TP], ptv[:, :, :TP], [("pt", t % 2)], ["kT"])
                    vo = vout[t % 2]
                    cp("act", vo[:TP, :], bank[:TP, 256:512], [bk], [("vout", t % 2)])
                    dma("sp", nv_d[j, si, t * TP:(t + 1) * TP, hp * 256:(hp + 1) * 256], vo[:TP, :], [("vout", t % 2)], ())
                    cp("pool", vaug[:TP, ktile0 + t, :, 0:128], vo[:TP, :].rearrange("p (h e) -> p h e", h=2),
                       [("vout", t % 2)], ["vaug"])
                for jt in range(NT):
                    bank = pb[jt % 2]
                    bk = ("pb", jt % 2)
                    for kc in range(8):
                        mm(bank[:TP, 0:256], hT[:, kc, jt * TP:(jt + 1) * TP], wb[:, kc, 0:256], kc == 0, kc == 7,
                           [wk, wk1, ("hT", jt)], [bk])
                    for kc in range(8):
                        mm(bank[:TP, 256:512], hT[:, kc, jt * TP:(jt + 1) * TP], wb[:, kc, 768:1024], kc == 0, kc == 7,
                           [wk, wk1, ("hT", jt)], [bk])
                    qk_norm(bank, bk, TP, j, 0)
                    tt("pool", qkb[:TP, :].rearrange("p (g d) -> p g d", g=4), kn[:TP, :].rearrange("p (g d) -> p g d", g=4),
                       qkn[:TP, j, 0, :].unsqueeze(1).broadcast_to([TP, 4, 64]), ALU.mult, ["kn", "qkn"], ["qkb"])
                    ptv = pt[jt % 2][:, 0:256].rearrange("p (m k) -> p m k", m=2)
                    for m in range(2):
                        tr(ptv[:, m, :TP], qkb[:TP, m * 128:(m + 1) * 128], identb[:TP, :TP], ["qkb", "identb"],
                           [("pt", jt % 2)])
                    qT_ = qTt[jt % 2]
                    cp("dve", qT_[:, :, :TP], ptv[:, :, :TP], [("pt", jt % 2)], [("qTt", jt % 2)])
                    sz_ = szt[jt % 2]
                    act(sgt[:TP, :], bank[:TP, 256:512], AF.Exp, [bk], ["sgt"], scale=-1.0)
                    act(sgt[:TP, :], sgt[:TP, :], AF.Ln, ["sgt"], ["sgt"], bias=1.0)
                    act(sgt[:TP, :], sgt[:TP, :], AF.Exp, ["sgt"], ["sgt"], scale=-1.0)
                    tt("dve", sz_[:TP, :], sgt[:TP, :], bank[:TP, 256:512], ALU.mult, ["sgt", bk], [("szt", jt % 2)])
                    for hh in range(2):
                        h = 2 * hp + hh
                        slope_h = _slopes()[h]
                        if kind == "p":
                            ktiles = [(i, 128, "off" if i < jt else "diag") for i in range(jt + 1)
                                      if i == jt or slope_h * ((jt - i - 1) * 128 + 1) <= SKIP_T]
                        else:
                            ktiles = [(i, 128, "off") for i in range(16)
                                      if slope_h * (PAST - (128 * i + 127)) <= SKIP_T] + [(16, 64, "diag")]
                        r0 = 64 * hh
                        pos = [pb[4 + m][:, hh * 129:(hh + 1) * 129] for m in range(2)]
                        poks = [("pb", 4), ("pb", 5)]
                        for idx, (i, nk, typ) in enumerate(ktiles):
                            sslot = uid[0] % 4
                            uid[0] += 1
                            psb_ = pb[2 + sslot % 2]
                            ps = psb_[:, 0:256].rearrange("p (m q) -> p m q", m=2)
                            psk = ("pb", 2 + sslot % 2)
                            for m in range(2):
                                mm(ps[:nk, m, :TP], kT[r0:r0 + 64, m, i * 128:i * 128 + nk], qT_[r0:r0 + 64, m, :TP],
                                   True, True, ["kT", ("qTt", jt % 2)], [psk])
                            P_ = Pt[sslot]
                            pk = ("Pt", sslot)
                            if typ == "off":
                                bias = btab_p[:nk, h, jt - i:jt - i + 1] if kind == "p" else btab_s[:nk, h, i:i + 1]
                                act(P_[:nk, :, :TP], ps[:nk, :, :TP], AF.Exp, [psk, "btab_p", "btab_s"], [pk],
                                    bias=bias, scale=0.125)
                            else:
                                bm = bmat_p[:nk, h, :TP] if kind == "p" else bmat_s[:nk, h, :TP]
                                stt("dve", dtmp[:nk, :, :TP], ps[:nk, :, :TP], 0.125,
                                    bm.unsqueeze(1).broadcast_to([nk, 2, TP]), ALU.mult, ALU.add,
                                    [psk, "bmat_p", "bmat_s"], ["dtmp"])
                                act(P_[:nk, :, :TP], dtmp[:nk, :, :TP], AF.Exp, ["dtmp"], [pk])
                            for m in range(2):
                                mm(pos[m][:TP, :], P_[:nk, m, :TP], vaug[:nk, i, hh, :], idx == 0, idx == len(ktiles) - 1,
                                   [pk, "vaug"], [poks[m]])
                        for m in range(2):
                            S.op("dve", lambda e, _o=rr[:TP, m:m + 1], _i=pos[m][:TP, 128:129]: e.reciprocal(out=_o, in_=_i),
                                 [poks[m]], ["rr"])
                        tt("dve", rl[:TP, 0:1], rr[:TP, 1:2], neglam[:TP, j:j + 1], ALU.mult, ["rr", "neglam"], ["rl"])
                        tsc("dve", t1[:TP, :], pos[1][:TP, 0:128], rl[:TP, 0:1], None, ALU.mult, None, [poks[1], "rl"], ["t1"])
                        stt("dve", oo[:TP, :], pos[0][:TP, 0:128], rr[:TP, 0:1], t1[:TP, :], ALU.mult, ALU.add,
                            [poks[0], "rr", "t1"], ["oo"])
                        mset("pool", ssn[:, 0:1], 0.0, ["ssn"])
                        act(junk[:TP, 0:128], oo[:TP, :], AF.Square, ["oo", "ssn"], ["junk", "ssn"], accum=ssn[:TP, 0:1])
                        rstd_ops("dve", rsn[:TP, 0:1], ssn[:TP, 0:1], 128.0, ["ssn"], ["rsn"])
                        stt("dve", ug[:TP, :], oo[:TP, :], rsn[:TP, 0:1], wsub[:TP, j, :], ALU.mult, ALU.mult,
                            ["oo", "rsn", "wsub"], ["ug"])
                        tt("pool", ub[:TP, :], ug[:TP, :], sz_[:TP, hh * 128:(hh + 1) * 128], ALU.mult,
                           ["ug", ("szt", jt % 2)], ["ub"])
                        us = uid[0] % 2
                        ptu = pt[us][:, 512:640]
                        tr(ptu[:, :TP], ub[:TP, :], identb[:TP, :TP], ["ub", "identb"], [("pt", us)])
                        cp("act", uT[:, h, jt * TP:(jt + 1) * TP], ptu[:, :TP], [("pt", us)], ["uT"])
            phase_outproj(awo_d, j, T, False)

        def qk_norm(bank, bk, TP, j, which):
            act(sq[:TP, :], bank[:TP, 0:256], AF.Identity, [bk], ["sq"])
            tt("dve", sq[:TP, :], sq[:TP, :], sq[:TP, :], ALU.mult, ["sq"], ["sq"])
            S.op("dve", lambda e, _o=ssq4[:TP, 0:4], _i=sq[:TP, :].rearrange("p (g d) -> p g d", g=4):
                 e.tensor_reduce(out=_o, in_=_i, axis=AX.X, op=ALU.add), ["sq"], ["ssq4"])
            rstd_ops("dve", rs4[:TP, 0:4], ssq4[:TP, 0:4], 64.0, ["ssq4"], ["rs4"])
            tt("dve", kn[:TP, :].rearrange("p (g d) -> p g d", g=4), bank[:TP, 0:256].rearrange("p (g d) -> p g d", g=4),
               rs4[:TP, 0:4].unsqueeze(2).broadcast_to([TP, 4, 64]), ALU.mult, [bk, "rs4"], ["kn"])

        seqs = [("p", i) for i in range(NP)] + [("s", i) for i in range(NS)]
        for kind, si in seqs:
            T = SEQ if kind == "p" else DEC_SEQ
            TP = min(T, 128)
            NT = T // TP
            xd = (xp_d if kind == "p" else xs_d)[si]
            yd = (yp_d if kind == "p" else ys_d)[si]
            for t in range(NT):
                dma("sp", xres[:TP, t, :], xd[t * TP:(t + 1) * TP, :], (), [("x", t)])
            for l in range(NL):
                if l % 2 == 0:
                    hgrn_layer(l, T, kind, si)
                else:
                    attn_layer(l, T, kind, si)
            for t in range(NT):
                dma("sp", yd[t * TP:(t + 1) * TP, :], xres[:TP, t, :], [("x", t)], ())
        S.emit(nc)
    return nc, len(S.ins)


_PROG_CACHE = {}


def _get_prog(NP, NS, NL=4):
    key = (NP, NS, NL)
    if key not in _PROG_CACHE:
        _PROG_CACHE[key] = build_program(NP, NS, NL)
    return _PROG_CACHE[key]


def make_in_maps(inputs, NP, NS, ncores):
    f = lambda a: np.ascontiguousarray(np.asarray(a, dtype=np.float32))
    c = _const_tables()
    shared = dict(
        normwT=f(np.asarray(inputs["norm_w"]).reshape(4, 8, 128).transpose(2, 0, 1)),
        hwin=f(inputs["hgrn_w_in"]),
        lbT=f(np.asarray(inputs["hgrn_lb_logits"]).reshape(2, 8, 128).transpose(2, 0, 1)),
        onwT=f(np.asarray(inputs["hgrn_onorm_w"]).reshape(2, 8, 128).transpose(2, 0, 1)),
        hwo=f(inputs["hgrn_w_out"]),
        awin=f(inputs["attn_w_in"]),
        qkn=f(np.broadcast_to(np.stack([np.asarray(inputs["attn_q_norm"]), np.asarray(inputs["attn_k_norm"])], axis=1)[None],
                              (128, 2, 2, 64))),
        lam=f(np.broadcast_to(np.asarray(inputs["attn_lambda"])[None], (128, 2, 4, 64))),
        sub=f(np.broadcast_to(np.asarray(inputs["attn_subln"])[None], (128, 2, 128))),
        awo=f(inputs["attn_w_out"]),
        btab_p=c["btab_p"], bmat_p=c["bmat_p"], btab_s=c["btab_s"], bmat_s=c["bmat_s"],
        tri=c["tri"], scanmask=c["scanmask"], ident=c["ident"],
    )
    xp = np.asarray(inputs["x_prompt"])
    xs = np.asarray(inputs["x_sample"])
    ck = np.asarray(inputs["cache_k"]).reshape(2, -1, PAST, D)
    cv = np.asarray(inputs["cache_v"]).reshape(2, -1, PAST, D)
    st = np.asarray(inputs["state_hgrn"])
    maps = []
    for c_ in range(ncores):
        m = dict(shared)
        m["xp"] = f(xp[c_ * NP:(c_ + 1) * NP]) if NP > 0 else np.zeros((1, SEQ, D), np.float32)
        if NS > 0:
            m["xs"] = f(xs[c_ * NS:(c_ + 1) * NS])
            m["ck"] = f(ck[:, c_ * NS:(c_ + 1) * NS])
            m["cv"] = f(cv[:, c_ * NS:(c_ + 1) * NS])
            m["st"] = f(st[:, c_ * NS:(c_ + 1) * NS])
        else:
            m["xs"] = np.zeros((1, DEC_SEQ, D), np.float32)
            m["ck"] = np.zeros((2, 1, PAST, D), np.float32)
            m["cv"] = np.zeros((2, 1, PAST, D), np.float32)
            m["st"] = np.zeros((2, 1, 8, 128, 128), np.float32)
        maps.append(m)
    return maps


def kernel(x_prompt, x_sample, cache_k, cache_v, state_hgrn, norm_w, hgrn_w_in, hgrn_lb_logits,
           hgrn_onorm_w, hgrn_w_out, attn_w_in, attn_q_norm, attn_k_norm, attn_lambda, attn_subln,
           attn_w_out):
    inputs = dict(x_prompt=x_prompt, x_sample=x_sample, cache_k=cache_k, cache_v=cache_v,
                  state_hgrn=state_hgrn, norm_w=norm_w, hgrn_w_in=hgrn_w_in,
                  hgrn_lb_logits=hgrn_lb_logits, hgrn_onorm_w=hgrn_onorm_w, hgrn_w_out=hgrn_w_out,
                  attn_w_in=attn_w_in, attn_q_norm=attn_q_norm, attn_k_norm=attn_k_norm,
                  attn_lambda=attn_lambda, attn_subln=attn_subln, attn_w_out=attn_w_out)
    B = np.asarray(x_prompt).shape[0]
    Bs = np.asarray(x_sample).shape[0]
    NP = B // NCORES
    NS = Bs // NCORES
    nc, _ = _get_prog(NP, NS)
    maps = make_in_maps(inputs, NP, NS, NCORES)
    res = run_bass_kernel_spmd(nc, maps, core_ids=list(range(NCORES)))
    R = res.results
    cat = lambda name, ax: np.concatenate([np.asarray(r[name]) for r in R], axis=ax)
    yp = cat("yp", 0)
    ys = cat("ys", 0)
    nkp = cat("nkp", 1).reshape(2, B, SEQ, 2, 8, 64)
    nvp = cat("nvp", 1).reshape(2, B, SEQ, 8, 128)
    nks = cat("nks", 1).reshape(2, Bs, DEC_SEQ, 2, 8, 64)
    nvs = cat("nvs", 1).reshape(2, Bs, DEC_SEQ, 8, 128)
    nsp = cat("nsp", 1)
    nss = cat("nss", 1)
    return (yp, ys, nkp, nvp, nks, nvs, nsp, nss)
```

```python
import contextlib
import math
import numpy as np
import concourse.bass as bass
import concourse.mybir as mybir
from concourse.bass_utils import run_bass_kernel_spmd

F32 = mybir.dt.float32
BF16 = mybir.dt.bfloat16
AF = mybir.ActivationFunctionType
ALU = mybir.AluOpType
AX = mybir.AxisListType

NCORES = 8
D = 1024
SEQ = 2048
DEC_SEQ = 64
PAST = 2048
EPS = 1e-6
K_DMA = 8
STOP = 0
SKIP_T = 180.0


class Sched:
    LAT_X = 250.0
    LAT_S = 80.0

    def __init__(self, reorder=True):
        self.ins = []
        self.last_w = {}
        self.readers = {}
        self.cur_bar = None
        self.reorder = reorder
        self.rkeys = set()
        self.wkeys = set()

    def op(self, eng, fn, reads=(), writes=(), dma=False, cost=100.0, bar=False):
        idx = len(self.ins)
        deps = set()
        self.rkeys.update(reads)
        self.wkeys.update(writes)
        for k in reads:
            w = self.last_w.get(k)
            if w is not None:
                deps.add(w)
            if isinstance(k, tuple) and k[0] in ("pb", "pt"):
                for r in self.readers.get(k, ()):
                    if self.ins[r]["eng"] != eng:
                        deps.add(r)
        for k in writes:
            w = self.last_w.get(k)
            if w is not None:
                deps.add(w)
            for r in self.readers.get(k, ()):
                deps.add(r)
        for k in reads:
            self.readers.setdefault(k, []).append(idx)
        for k in writes:
            self.last_w[k] = idx
            self.readers[k] = []
        if self.cur_bar is not None and not bar:
            deps.add(self.cur_bar.get(eng, self.cur_bar["dve"]))
        deps.discard(idx)
        self.ins.append(dict(eng=eng, fn=fn, deps=deps, dma=dma, cost=cost, bar=bar))
        return idx

    def barrier(self, dummies):
        nb = {}
        for e, fn in dummies.items():
            nb[e] = self.op(e, fn, (), [("bar", e)], bar=True)
        self.cur_bar = nb

    def _sched_segment(self, seg):
        import heapq
        ins = self.ins
        if not self.reorder or len(seg) < 3:
            return list(seg)
        segset = set(seg)
        indeg = {}
        succ = {}
        avail = {}
        for i in seg:
            ds = [d for d in ins[i]["deps"] if d in segset]
            indeg[i] = len(ds)
            avail[i] = 0.0
            for d in ds:
                succ.setdefault(d, []).append(i)
        engs = sorted(set(ins[i]["eng"] for i in seg))
        waiting = {e: [] for e in engs}
        ready = {e: [] for e in engs}
        busy = {e: 0.0 for e in engs}
        for i in seg:
            if indeg[i] == 0:
                heapq.heappush(ready[ins[i]["eng"]], i)
        order = []
        t = 0.0
        done = 0
        n = len(seg)
        finish = {}
        while done < n:
            progressed = False
            for e in engs:
                if busy[e] > t:
                    continue
                w = waiting[e]
                while w and w[0][0] <= t:
                    heapq.heappush(ready[e], heapq.heappop(w)[1])
                if not ready[e]:
                    continue
                i = heapq.heappop(ready[e])
                I = ins[i]
                c = I["cost"]
                if I["dma"]:
                    issue = 60.0 if e == "sp" else 700.0
                    busy[e] = t + issue
                    fin = t + c
                else:
                    busy[e] = t + c
                    fin = t + c
                finish[i] = fin
                order.append(i)
                done += 1
                progressed = True
                for sidx in succ.get(i, ()):
                    lat = self.LAT_S if ins[sidx]["eng"] == e and not I["dma"] else self.LAT_X
                    if e == "pe" and ins[sidx]["eng"] == "pe":
                        lat = 0.0
                    a = fin + lat
                    if a > avail[sidx]:
                        avail[sidx] = a
                    indeg[sidx] -= 1
                    if indeg[sidx] == 0:
                        heapq.heappush(waiting[ins[sidx]["eng"]], (avail[sidx], sidx))
            if not progressed:
                cand = []
                for e in engs:
                    if busy[e] > t:
                        cand.append(busy[e])
                    elif waiting[e]:
                        cand.append(max(waiting[e][0][0], busy[e]))
                assert cand, "scheduler stuck"
                t = min(cand)
        return order

    def schedule(self):
        ins = self.ins
        n = len(ins)
        order = []
        seg = []
        last_on = {}
        seg_dmas = []

        def flush():
            nonlocal seg
            o = self._sched_segment(seg)
            for i in o:
                last_on[ins[i]["eng"]] = i
                if ins[i]["dma"]:
                    seg_dmas.append(i)
            order.extend(o)
            seg = []

        i = 0
        while i < n:
            if ins[i]["bar"]:
                flush()
                prev = set(last_on.values()) | set(seg_dmas)
                seg_dmas.clear()
                while i < n and ins[i]["bar"]:
                    ins[i]["deps"] = set(prev)
                    order.append(i)
                    i += 1
                for k in order[-8:]:
                    if ins[k]["bar"]:
                        last_on[ins[k]["eng"]] = k
            else:
                seg.append(i)
                i += 1
        flush()
        assert len(order) == n and len(set(order)) == n
        return order

    def emit(self, nc, engines=("pe", "act", "dve", "pool", "sp")):
        print("keys read but never written:", sorted(map(str, self.rkeys - self.wkeys)))
        order = self.schedule()
        ins = [self.ins[i] for i in order]
        remap = {old: new for new, old in enumerate(order)}
        for I in ins:
            I["deps"] = set(remap[d] for d in I["deps"])
        n = len(ins)
        for i, I in enumerate(ins):
            for d in I["deps"]:
                assert d < i, "schedule violates a dependency"
        has_cons = [False] * n
        for I in ins:
            nd = set()
            for d in I["deps"]:
                Pp = ins[d]
                if (not Pp["dma"]) and Pp["eng"] == I["eng"] and I["eng"] == "pe":
                    continue
                nd.add(d)
            I["deps"] = nd
            for d in nd:
                has_cons[d] = True
        cnt = {e: 0 for e in engines}
        dcnt = {e: 0 for e in engines}
        for i, I in enumerate(ins):
            e = I["eng"]
            if I["dma"]:
                k = dcnt[e]
                dcnt[e] += 1
                I["sig"] = (("d", e, k % K_DMA), 16 * (k // K_DMA + 1))
                I["dma_k"] = k
            elif has_cons[i]:
                cnt[e] += 1
                I["sig"] = (("c", e), cnt[e])
            else:
                I["sig"] = None
        with contextlib.ExitStack() as st:
            sems = {}
            for e in engines:
                sems[("c", e)] = st.enter_context(nc.semaphore("c_" + e))
                if dcnt[e] > 0:
                    for j in range(K_DMA):
                        sems[("d", e, j)] = st.enter_context(nc.semaphore("d_%s_%d" % (e, j)))
            block = st.enter_context(nc.Block())

            def make(e):
                def body(eng):
                    seen = {}
                    for I in ins:
                        if I["eng"] != e:
                            continue
                        waits = {}
                        for d in I["deps"]:
                            s, v = ins[d]["sig"]
                            if waits.get(s, 0) < v:
                                waits[s] = v
                        if I["dma"] and I["dma_k"] >= K_DMA:
                            s, v = I["sig"]
                            if waits.get(s, 0) < v - 16:
                                waits[s] = v - 16
                        for s, v in waits.items():
                            if seen.get(s, 0) >= v:
                                continue
                            seen[s] = v
                            eng.wait_ge(sems[s], v)
                        bi = I["fn"](eng)
                        if I["sig"] is not None:
                            s, v = I["sig"]
                            bi.then_inc(sems[s], 16 if I["dma"] else 1)
                    if dcnt[e] > 0:
                        for j in range(K_DMA):
                            tot = (dcnt[e] - j + K_DMA - 1) // K_DMA
                            if tot > 0 and seen.get(("d", e, j), 0) < 16 * tot:
                                eng.wait_ge(sems[("d", e, j)], 16 * tot)
                return body

            reg = {"pe": block.tensor, "act": block.scalar, "dve": block.vector,
                   "pool": block.gpsimd, "sp": block.sync}
            for e in engines:
                if any(I["eng"] == e for I in ins):
                    reg[e](make(e))


def _slopes():
    return [2.0 ** (-8.0 * (h + 1) / 8.0) for h in range(8)]


def _const_tables():
    sl = np.array(_slopes(), np.float64)
    kk = np.arange(128)[:, None, None]
    dd = np.arange(16)[None, None, :]
    btab_p = -(sl[None, :, None]) * (128.0 * dd + 127.0 - kk)
    k = np.arange(128)[:, None, None]
    q = np.arange(128)[None, None, :]
    vis = (k // 64) <= (q // 64)
    bm = -(sl[None, :, None]) * np.abs(q - k) - sl[None, :, None] * (127.0 - q)
    bmat_p = np.where(vis, bm, -30000.0)
    ii = np.arange(16)[None, None, :]
    btab_s = -(sl[None, :, None]) * (2111.0 - 128.0 * ii - kk)
    k6 = np.arange(64)[:, None, None]
    q6 = np.arange(64)[None, None, :]
    bmat_s = -(sl[None, :, None]) * np.abs(q6 - k6) - sl[None, :, None] * (63.0 - q6)
    bmat_s_full = np.zeros((128, 8, 64))
    bmat_s_full[:64] = bmat_s
    tri = (np.arange(128)[:, None] <= np.arange(128)[None, :]).astype(np.float32)
    scanmask = np.ones((128, 512), np.float32)
    scanmask[:, ::128] = 0.0
    return dict(
        btab_p=btab_p.astype(np.float32), bmat_p=bmat_p.astype(np.float32),
        btab_s=btab_s.astype(np.float32), bmat_s=bmat_s_full.astype(np.float32),
        tri=tri, scanmask=scanmask, ident=np.eye(128, dtype=np.float32),
    )


def build_program(NP, NS, NL=4):
    nc = bass.Bass("TRN2", target_bir_lowering=False)

    def din(name, shape):
        return nc.dram_tensor(name, list(shape), F32, kind="ExternalInput").ap()

    def dout(name, shape):
        return nc.dram_tensor(name, list(shape), F32, kind="ExternalOutput").ap()

    xp_d = din("xp", [max(NP, 1), SEQ, D])
    xs_d = din("xs", [max(NS, 1), DEC_SEQ, D])
    ck_d = din("ck", [2, max(NS, 1), PAST, D])
    cv_d = din("cv", [2, max(NS, 1), PAST, D])
    st_d = din("st", [2, max(NS, 1), 8, 128, 128])
    normwT_d = din("normwT", [128, 4, 8])
    hwin_d = din("hwin", [2, D, 4 * D])
    lbT_d = din("lbT", [128, 2, 8])
    onwT_d = din("onwT", [128, 2, 8])
    hwo_d = din("hwo", [2, D, D])
    awin_d = din("awin", [2, D, 4 * D])
    qkn_d = din("qkn", [128, 2, 2, 64])
    lam_d = din("lam", [128, 2, 4, 64])
    sub_d = din("sub", [128, 2, 128])
    awo_d = din("awo", [2, D, D])
    btab_p_d = din("btab_p", [128, 8, 16])
    bmat_p_d = din("bmat_p", [128, 8, 128])
    btab_s_d = din("btab_s", [128, 8, 16])
    bmat_s_d = din("bmat_s", [128, 8, 64])
    tri_d = din("tri", [128, 128])
    scanmask_d = din("scanmask", [128, 512])
    ident_d = din("ident", [128, 128])

    yp_d = dout("yp", [max(NP, 1), SEQ, D])
    ys_d = dout("ys", [max(NS, 1), DEC_SEQ, D])
    nkp_d = dout("nkp", [2, max(NP, 1), SEQ, D])
    nvp_d = dout("nvp", [2, max(NP, 1), SEQ, D])
    nks_d = dout("nks", [2, max(NS, 1), DEC_SEQ, D])
    nvs_d = dout("nvs", [2, max(NS, 1), DEC_SEQ, D])
    nsp_d = dout("nsp", [2, max(NP, 1), 8, 128, 128])
    nss_d = dout("nss", [2, max(NS, 1), 8, 128, 128])

    S = Sched()
    uid = [0]
    ucnt = [0]

    with contextlib.ExitStack() as stk:
        def sb(name, shape, dt):
            return stk.enter_context(nc.sbuf_tensor("s_" + name, list(shape), dt))

        def psb(name, shape, dt):
            return stk.enter_context(nc.psum_tensor("p_" + name, list(shape), dt))

        xres = sb("xres", [128, 16, D], F32)
        hT = sb("hT", [128, 8, SEQ], BF16)
        uT = sb("uT", [128, 8, SEQ], BF16)
        wbuf_t = sb("wbuf", [128, 8, 1024], BF16)
        wbuf = [wbuf_t[:, :, 0:512], wbuf_t[:, :, 512:1024]]
        hb = [sb("hb0", [128, D], BF16)] * 2
        ssx = sb("ssx", [128, 16], F32)
        rstdx = sb("rstdx", [128, 16], F32)
        ssacc = sb("ssacc", [128, 16], F32)
        rstdo = sb("rstdo", [128, 16], F32)
        bar_d = sb("bar_d", [128, 4], F32)
        identb = sb("identb", [128, 128], BF16)
        tri = sb("tri", [128, 128], F32)
        scanmask = sb("scanmask", [128, 512], F32)
        ones1 = sb("ones1", [128, 1], F32)
        normwT = sb("normwT", [128, 4, 8], F32)
        lbraw = sb("lbraw", [128, 2, 8], F32)
        lbT = sb("lbT", [128, 2, 8], F32)
        omlT = sb("omlT", [128, 2, 8], F32)
        onwT = sb("onwT", [128, 2, 8], F32)
        qkn = sb("qkn", [128, 2, 2, 64], F32)
        lams = sb("lams", [128, 2, 2], F32)
        lame = sb("lame", [128, 2, 2], F32)
        neglam = sb("neglam", [128, 2], F32)
        wsub = sb("wsub", [128, 2, 128], F32)
        btab_p = sb("btab_p", [128, 8, 16], F32)
        bmat_p = sb("bmat_p", [128, 8, 128], F32)
        btab_s = sb("btab_s", [128, 8, 16], F32)
        bmat_s = sb("bmat_s", [128, 8, 64], F32)
        Sall = sb("Sall", [128, 17, 128], BF16)
        S32 = [sb("S32a", [128, 128], F32), sb("S32b", [128, 128], F32)]
        SCR_BYTES = 44032
        scr = sb("scr", [128, SCR_BYTES // 2], BF16)

        class Carver:
            def __init__(self):
                self.off = 0

            def __call__(self, shape, dt):
                n = 1
                for x in shape[1:]:
                    n *= x
                nb = n * (4 if dt == F32 else 2)
                nb_al = (nb + 31) // 32 * 32
                assert self.off + nb_al <= SCR_BYTES, (self.off, nb_al)
                ap = scr[:, self.off // 2:(self.off + nb) // 2]
                self.off += nb_al
                if dt == F32:
                    ap = ap.bitcast(F32)
                if len(shape) == 3:
                    ap = ap.rearrange("p (a b) -> p a b", a=shape[1])
                elif len(shape) == 4:
                    ap = ap.rearrange("p (a b c) -> p a b c", a=shape[1], b=shape[2])
                assert tuple(ap.shape) == tuple(shape), (ap.shape, shape)
                return ap

        cv_ = Carver()
        junk = cv_([128, D], BF16)
        base_off = cv_.off
        lamv = cv_([128, 2, 4, 64], F32)
        lamp = cv_([128, 2, 2, 64], F32)
        cv_.off = base_off
        qf = cv_([128, 512], F32)
        ff = cv_([128, 512], F32)
        gg = cv_([128, 512], F32)
        kf = cv_([128, 512], F32)
        bbuf = cv_([128, 512], F32)
        ee = [cv_([128, 512], F32), cv_([128, 512], F32)]
        ex = [cv_([128, 512], F32), cv_([128, 512], F32)]
        osq = cv_([128, 512], F32)
        qt = cv_([128, 512], BF16)
        qh = cv_([128, 512], BF16)
        khT = cv_([128, 512], BF16)
        szTs = [cv_([128, 512], BF16), cv_([128, 512], BF16)]
        Kb = [cv_([128, 4, 128], BF16) for i in range(4)]
        khtm = cv_([128, 4, 128], BF16)
        vtms = [cv_([128, 4, 128], BF16), cv_([128, 4, 128], BF16)]
        attm = cv_([128, 4, 128], BF16)
        dec = cv_([128, 8], F32)
        hg_end = cv_.off
        cv_.off = base_off
        kT = cv_([128, 2, SEQ + 64], BF16)
        vaug = cv_([128, 17, 2, 129], BF16)
        kctm = cv_([128, 16, 2, 128], BF16)
        sq = cv_([128, 256], F32)
        ssq4 = cv_([128, 8], F32)
        rs4 = cv_([128, 8], F32)
        kn = cv_([128, 256], F32)
        kout = [cv_([128, 256], F32), cv_([128, 256], F32)]
        vout = [cv_([128, 256], F32), cv_([128, 256], F32)]
        qkb = cv_([128, 256], BF16)
        qTt = [cv_([128, 2, 256], BF16), cv_([128, 2, 256], BF16)]
        szt = [cv_([128, 256], BF16), cv_([128, 256], BF16)]
        Pt = [cv_([128, 2, 128], BF16) for i in range(4)]
        dtmp = cv_([128, 2, 128], F32)
        rr = cv_([128, 8], F32)
        rl = cv_([128, 8], F32)
        t1 = cv_([128, 128], F32)
        oo = cv_([128, 128], F32)
        ssn = cv_([128, 8], F32)
        rsn = cv_([128, 8], F32)
        ug = cv_([128, 128], F32)
        ub = cv_([128, 128], BF16)
        sgt = cv_([128, 256], F32)
        at_end = cv_.off
        print("scratch bytes: hgrn", hg_end, "attn", at_end)
        pb = [psb("pb%d" % i, [128, 512], F32) for i in range(6)]
        pt = [psb("pt%d" % i, [128, 1024], BF16) for i in range(2)]

        def fsz(ap):
            n = 1
            for x in tuple(ap.shape)[1:]:
                n *= x
            return float(n)

        def mm(out, lhsT, rhs, start, stop, r, w):
            nn = fsz(out)
            c = max(nn, 64.0) / 2.2 + 12.0
            if lhsT.dtype == F32:
                c *= 4.0
            S.op("pe", lambda e: e.matmul(out, lhsT=lhsT, rhs=rhs, start=start, stop=stop), r, w, cost=c)

        def tr(out, in_, ident, r, w):
            S.op("pe", lambda e: e.transpose(out=out, in_=in_, identity=ident), r, w, cost=max(fsz(out), 64.0) / 2.2 + 40.0)

        def act(out, in_, func, r, w, bias=None, scale=None, accum=None):
            kw = {}
            if bias is not None:
                kw["bias"] = bias
            if scale is not None:
                kw["scale"] = scale
            if accum is not None:
                kw["accum_out"] = accum
            S.op("act", lambda e: e.activation(out=out, in_=in_, func=func, **kw), r, w, cost=(224.0 + fsz(out)) / 1.4)

        def vcost(eng, out):
            if eng == "pool":
                return 120.0 + 2.1 * fsz(out)
            return (70.0 + fsz(out)) / 0.96

        def tt(eng, out, in0, in1, op, r, w):
            S.op(eng, lambda e: e.tensor_tensor(out=out, in0=in0, in1=in1, op=op), r, w, cost=vcost(eng, out))

        def tsc(eng, out, in0, s1, s2, op0, op1, r, w):
            if op1 is None:
                S.op(eng, lambda e: e.tensor_scalar(out=out, in0=in0, scalar1=s1, scalar2=None, op0=op0), r, w,
                     cost=vcost(eng, out))
            else:
                S.op(eng, lambda e: e.tensor_scalar(out=out, in0=in0, scalar1=s1, scalar2=s2, op0=op0, op1=op1), r, w,
                     cost=vcost(eng, out))

        def stt(eng, out, in0, scalar, in1, op0, op1, r, w):
            S.op(eng, lambda e: e.scalar_tensor_tensor(out=out, in0=in0, scalar=scalar, in1=in1, op0=op0, op1=op1), r, w,
                 cost=vcost(eng, out))

        def cp(eng, out, in_, r, w):
            if eng == "act":
                S.op("act", lambda e: e.copy(out=out, in_=in_), r, w, cost=(224.0 + fsz(out)) / 1.4)
            else:
                S.op(eng, lambda e: e.tensor_copy(out=out, in_=in_), r, w, cost=vcost(eng, out))

        def mset(eng, ap, val, w):
            S.op(eng, lambda e: e.memset(ap, val), (), w, cost=vcost(eng, ap) * 0.5)

        def dma(eng, out, in_, r, w):
            nb = fsz(out) * float(out.shape[0]) * 4.0
            S.op(eng, lambda e: e.dma_start(out=out, in_=in_), r, w, dma=True, cost=2000.0 + nb / 120.0)

        def rstd_ops(eng, out, in_, n, r, w):
            tsc(eng, out, in_, 1.0 / n, EPS, ALU.mult, ALU.add, r, w)
            act(out, out, AF.Ln, w, w)
            act(out, out, AF.Exp, w, w, scale=-0.5)

        dma("sp", tri[:], tri_d, (), ["tri"])
        dma("sp", scanmask[:], scanmask_d, (), ["scanmask"])
        dma("pool", identb[:], ident_d, (), ["identb"])
        dma("sp", normwT[:], normwT_d, (), ["normwT"])
        dma("sp", lbraw[:], lbT_d, (), ["lbraw"])
        dma("sp", onwT[:], onwT_d, (), ["onwT"])
        dma("sp", qkn[:], qkn_d, (), ["qkn"])
        dma("sp", lamv[:], lam_d, (), ["lamv"])
        dma("sp", wsub[:], sub_d, (), ["wsub"])
        dma("sp", btab_p[:], btab_p_d, (), ["btab_p"])
        dma("sp", bmat_p[:], bmat_p_d, (), ["bmat_p"])
        dma("sp", btab_s[:], btab_s_d, (), ["btab_s"])
        dma("sp", bmat_s[:], bmat_s_d, (), ["bmat_s"])
        mset("dve", ones1[:], 1.0, ["ones1"])
        mset("dve", lbT[:], 0.0, ["lbT"])
        tt("dve", lbT[:, 1, :], lbraw[:, 1, :], lbraw[:, 0, :], ALU.subtract, ["lbraw"], ["lbT"])
        act(lbT[:, 1, :], lbT[:, 1, :], AF.Exp, ["lbT"], ["lbT"], scale=-1.0)
        tsc("dve", lbT[:, 1, :], lbT[:, 1, :], 1.0, None, ALU.add, None, ["lbT"], ["lbT"])
        S.op("dve", lambda e: e.reciprocal(out=lbT[:, 1, :], in_=lbT[:, 1, :]), ["lbT"], ["lbT"])
        tsc("dve", omlT[:], lbT[:], -1.0, 1.0, ALU.mult, ALU.add, ["lbT"], ["omlT"])
        tt("dve", lamp[:, :, 0, :], lamv[:, :, 0, :], lamv[:, :, 1, :], ALU.mult, ["lamv"], ["lamp"])
        tt("dve", lamp[:, :, 1, :], lamv[:, :, 2, :], lamv[:, :, 3, :], ALU.mult, ["lamv"], ["lamp"])
        S.op("dve", lambda e: e.tensor_reduce(out=lams[:], in_=lamp[:], axis=AX.X, op=ALU.add), ["lamp"], ["lams"])
        act(lame[:], lams[:], AF.Exp, ["lams"], ["lame"])
        for j in range(2):
            lam_init = 0.8 - 0.6 * math.exp(-0.3 * (2 * j + 1))
            tt("dve", neglam[:, j:j + 1], lame[:, j, 1:2], lame[:, j, 0:1], ALU.subtract, ["lame"], ["neglam"])
            tsc("dve", neglam[:, j:j + 1], neglam[:, j:j + 1], -lam_init, None, ALU.add, None, ["neglam"], ["neglam"])
            tsc("dve", wsub[:, j, :], wsub[:, j, :], 1.0 - lam_init, None, ALU.mult, None, ["wsub"], ["wsub"])

        def fence():
            S.barrier({
                "act": lambda e: e.copy(out=bar_d[:, 0:1], in_=ones1[:, 0:1]),
                "dve": lambda e: e.memset(bar_d[:, 1:2], 0.0),
                "pool": lambda e: e.memset(bar_d[:, 2:3], 0.0),
            })

        def phase_norm(l, T):
            TP = min(T, 128)
            NT = T // TP
            mset("dve", ssx[:], 0.0, ["ssx"])
            for t in range(NT):
                act(junk[:TP, :], xres[:TP, t, :], AF.Square, [("x", t)], ["junk", "ssx"], accum=ssx[:TP, t:t + 1])
            rstd_ops("dve", rstdx[:TP, :NT], ssx[:TP, :NT], float(D), ["ssx"], ["rstdx"])
            for t in range(NT):
                hbt = hb[t % 2]
                act(hbt[:TP, :], xres[:TP, t, :], AF.Identity, [("x", t), "rstdx"], [("hb", 0)],
                    scale=rstdx[:TP, t:t + 1])
                ptb = pt[t % 2]
                ptv = ptb[:, :].rearrange("p (a b) -> p a b", a=8)
                for kc in range(8):
                    tr(ptv[:, kc, :TP], hbt[:TP, kc * 128:(kc + 1) * 128], identb[:TP, :TP],
                       [("hb", 0), "identb"], [("pt", t % 2)])
                tt("dve", hT[:, :, t * TP:(t + 1) * TP], ptv[:, :, :TP],
                   normwT[:, l, :].unsqueeze(2).broadcast_to([128, 8, TP]), ALU.mult,
                   [("pt", t % 2), "normwT"], [("hT", t)])

        def phase_outproj(w_d, j, T, use_rstd):
            TP = min(T, 128)
            NT = T // TP
            wv = w_d[j].rearrange("(kc p) n -> p kc n", p=128)
            for half in range(2):
                dma("pool", wbuf[half][:, :, 0:512], wv[:, :, half * 512:(half + 1) * 512], (), [("wbuf", half)])
            for t in range(NT):
                for half in range(2):
                    bank = pb[(2 * t + half) % 2]
                    bk = ("pb", (2 * t + half) % 2)
                    for kc in range(8):
                        mm(bank[:TP, :], uT[:, kc, t * TP:(t + 1) * TP], wbuf[half][:, kc, 0:512],
                           kc == 0, kc == 7, ["uT", ("wbuf", half)], [bk])
                    xo = xres[:TP, t, half * 512:(half + 1) * 512]
                    if use_rstd:
                        stt("dve", xo, bank[:TP, :], rstdo[:TP, t:t + 1], xo, ALU.mult, ALU.add,
                            [bk, "rstdo", ("x", t)], [("x", t)])
                    else:
                        tt("dve", xo, bank[:TP, :], xo, ALU.add, [bk, ("x", t)], [("x", t)])

        def hgrn_layer(l, T, kind, si):
            j = l // 2
            CH = min(T, 128)
            NCHT = T // CH
            SEGT = min(T, 512)
            NSEG = T // SEGT
            NCH = SEGT // CH
            NSB = CH // 32
            fence()
            for i in range(4):
                mset("pool", Kb[i], 0.0, [("Kb", i)])
            phase_norm(l, T)
            wv = hwin_d[j].rearrange("(kc p) n -> p kc n", p=128)
            for h in range(8):
                slot = h % 2
                wb = wbuf[slot]
                wk = ("wbuf", slot)
                for qi in range(4):
                    dma("pool", wb[:, :, qi * 128:(qi + 1) * 128],
                        wv[:, :, qi * 1024 + h * 128: qi * 1024 + (h + 1) * 128], (), [wk])
                if kind == "p":
                    mset("pool", S32[0][:], 0.0, [("S32", 0)])
                    mset("pool", Sall[:, 0, :], 0.0, ["Sall"])
                else:
                    dma("sp", S32[0][:], st_d[j, si, h], (), [("S32", 0)])
                    cp("pool", Sall[:, 0, :], S32[0][:], [("S32", 0)], ["Sall"])
                for seg in range(NSEG):
                    t0 = seg * SEGT
                    for qi, bi in ((0, 0), (1, 1), (3, 2)):
                        for kc in range(8):
                            mm(pb[bi][:, :SEGT], wb[:, kc, qi * 128:(qi + 1) * 128], hT[:, kc, t0:t0 + SEGT],
                               kc == 0, kc == 7, [wk] + [("hT", tt_) for tt_ in range(t0 // CH, (t0 + SEGT) // CH)], [("pb", bi)])
                    act(qf[:, :SEGT], pb[0][:, :SEGT], AF.Identity, [("pb", 0)], ["qf"], scale=128.0 ** -0.5)
                    act(gg[:, :SEGT], pb[1][:, :SEGT], AF.Exp, [("pb", 1)], ["gg"], scale=-1.0)
                    act(gg[:, :SEGT], gg[:, :SEGT], AF.Ln, ["gg"], ["gg"], bias=1.0)
                    act(ff[:, :SEGT], gg[:, :SEGT], AF.Exp, ["gg"], ["ff"], scale=-1.0)
                    act(ex[1][:, :SEGT], pb[2][:, :SEGT], AF.Exp, [("pb", 2)], [("ex", 1)], scale=-1.0)
                    act(ex[1][:, :SEGT], ex[1][:, :SEGT], AF.Ln, [("ex", 1)], [("ex", 1)], bias=1.0)
                    act(ex[1][:, :SEGT], ex[1][:, :SEGT], AF.Exp, [("ex", 1)], [("ex", 1)], scale=-1.0)
                    szT = szTs[ucnt[0] % 2]
                    szk = ("szT", ucnt[0] % 2)
                    vtm = vtms[ucnt[0] % 2]
                    vtk = ("vtm", ucnt[0] % 2)
                    ucnt[0] += 1
                    tt("dve", szT[:, :SEGT], ex[1][:, :SEGT], pb[2][:, :SEGT], ALU.mult, [("ex", 1), ("pb", 2)], [szk])
                    tsc("dve", ff[:, :SEGT], ff[:, :SEGT], omlT[:, j, h:h + 1], lbT[:, j, h:h + 1], ALU.mult, ALU.add,
                        ["ff", "omlT", "lbT"], ["ff"])
                    act(gg[:, :SEGT], ff[:, :SEGT], AF.Ln, ["ff"], ["gg"])
                    tsc("pool", kf[:, :SEGT], ff[:, :SEGT], -1.0, 1.0, ALU.mult, ALU.add, ["ff"], ["kf"])
                    pv = pb[3][:, :].rearrange("p (c v) -> p c v", c=4)
                    for c in range(NCH):
                        for kc in range(8):
                            mm(pv[:CH, c, :], hT[:, kc, t0 + c * CH:t0 + (c + 1) * CH], wb[:, kc, 256:384],
                               kc == 0, kc == 7, [wk, ("hT", t0 // CH + c)], [("pb", 3)])
                    cp("act", vtm[:CH, :NCH, :], pv[:CH, :NCH, :], [("pb", 3)], [vtk])
                    S.op("dve", lambda e, _o=bbuf[:, :SEGT], _m=scanmask[:, :SEGT], _g=gg[:, :SEGT]:
                         e.tensor_tensor_scan(out=_o, data0=_m, data1=_g, initial=0.0, op0=ALU.mult, op1=ALU.add),
                         ["scanmask", "gg"], ["bb"], cost=(70.0 + SEGT) / 0.96)
                    b3 = bbuf[:, :SEGT].rearrange("p (c t) -> p c t", c=NCH)
                    b4 = bbuf[:, :SEGT].rearrange("p (c i t) -> p c i t", c=NCH, i=NSB)
                    e0 = ee[0]
                    e04 = e0[:, :SEGT].rearrange("p (c i t) -> p c i t", c=NCH, i=NSB)
                    if NSB > 1:
                        tt("dve", e04[:, :, 1:NSB, :], b4[:, :, 1:NSB, :],
                           b4[:, :, 0:NSB - 1, 31:32].broadcast_to([128, NCH, NSB - 1, 32]), ALU.subtract,
                           ["bb"], [("ee", 0)])
                    cp("pool", e04[:, :, 0, :], b4[:, :, 0, :], ["bb"], [("ee", 0)])
                    act(ex[0][:, :SEGT], e0[:, :SEGT], AF.Exp, [("ee", 0)], [("ex", 0)])
                    tt("dve", qt[:, :SEGT], qf[:, :SEGT], ex[0][:, :SEGT], ALU.mult, ["qf", ("ex", 0)], ["qt"])
                    act(ex[1][:, :SEGT], bbuf[:, :SEGT], AF.Exp, ["bb"], [("ex", 1)])
                    tt("pool", qh[:, :SEGT], qf[:, :SEGT], ex[1][:, :SEGT], ALU.mult, ["qf", ("ex", 1)], ["qh"])
                    kf3 = kf[:, :SEGT].rearrange("p (c t) -> p c t", c=NCH)
                    for i in range(NSB):
                        wi = 32 * (i + 1)
                        b_ = i % 2
                        ex3 = ex[b_][:, :SEGT].rearrange("p (c t) -> p c t", c=NCH)
                        if i == 0:
                            act(ex3[:, :, 0:wi], b3[:, :, 0:wi], AF.Exp, ["bb"], [("ex", b_)], scale=-1.0)
                        else:
                            ee3 = ee[b_][:, :SEGT].rearrange("p (c t) -> p c t", c=NCH)
                            tt("dve", ee3[:, :, 0:wi], b3[:, :, 32 * i - 1:32 * i].broadcast_to([128, NCH, wi]),
                               b3[:, :, 0:wi], ALU.subtract, ["bb"], [("ee", b_)])
                            act(ex3[:, :, 0:wi], ee3[:, :, 0:wi], AF.Exp, [("ee", b_)], [("ex", b_)])
                        tt("dve" if i % 2 == 0 else "pool", Kb[i][:, :NCH, 0:wi], kf3[:, :, 0:wi], ex3[:, :, 0:wi], ALU.mult,
                           ["kf", ("ex", b_)], [("Kb", i)])
                    ee3 = ee[0][:, :SEGT].rearrange("p (c t) -> p c t", c=NCH)
                    tt("dve", ee3[:, :, :], b3[:, :, CH - 1:CH].broadcast_to([128, NCH, CH]), b3[:, :, :], ALU.subtract,
                       ["bb"], [("ee", 0)])
                    act(ex[0][:, :SEGT], ee[0][:, :SEGT], AF.Exp, [("ee", 0)], [("ex", 0)])
                    tt("pool", khT[:, :SEGT], kf[:, :SEGT], ex[0][:, :SEGT], ALU.mult, ["kf", ("ex", 0)], ["khT"])
                    act(dec[:, :NCH], b3[:, :, CH - 1], AF.Exp, ["bb"], ["dec"])
                    ptv = pt[0][:, 0:512].rearrange("p (c k) -> p c k", c=4)
                    for c in range(NCH):
                        tr(ptv[:CH, c, :], khT[:, c * CH:(c + 1) * CH], identb[:, :], ["khT", "identb"], [("pt", 0)])
                    cp("act", khtm[:CH, :NCH, :], ptv[:CH, :NCH, :], [("pt", 0)], ["khtm"])
                    for c in range(NCH):
                        cg = seg * NCH + c
                        ub_ = 4 if c % 2 == 0 else 2
                        pu = pb[ub_][:, 0:128]
                        mm(pu, khtm[:CH, c, :], vtm[:CH, c, :], True, True, ["khtm", vtk], [("pb", ub_)])
                        s_old, s_new = S32[cg % 2], S32[(cg + 1) % 2]
                        stt("dve", s_new[:], s_old[:], dec[:, c:c + 1], pu, ALU.mult, ALU.add,
                            [("S32", cg % 2), "dec", ("pb", ub_)], [("S32", (cg + 1) % 2)])
                        cp("pool", Sall[:, cg + 1, :], s_new[:], [("S32", (cg + 1) % 2)], ["Sall"])
                    pa = pb[5][:, :].rearrange("p (c t) -> p c t", c=4)
                    for c in range(NCH):
                        for i in range(NSB):
                            mm(pa[:CH, c, 32 * i:32 * (i + 1)], Kb[i][:, c, 0:CH], qt[:, c * CH + 32 * i:c * CH + 32 * (i + 1)],
                               True, True, [("Kb", i), "qt"], [("pb", 5)])
                    tt("dve", attm[:CH, :NCH, :CH], pa[:CH, :NCH, :CH],
                       tri[:CH, :CH].unsqueeze(1).broadcast_to([CH, NCH, CH]), ALU.mult,
                       [("pb", 5), "tri"], ["attm"])
                    po = pt[1][:, :].bitcast(F32)
                    pok_ = ("pt", 1)
                    for c in range(NCH):
                        cg = seg * NCH + c
                        mm(po[:, c * CH:(c + 1) * CH], vtm[:CH, c, :], attm[:CH, c, :CH], True, False,
                           [vtk, "attm"], [pok_])
                        mm(po[:, c * CH:(c + 1) * CH], Sall[:, cg, :], qh[:, c * CH:(c + 1) * CH], False, True,
                           ["Sall", "qh"], [pok_])
                    act(osq[:, :SEGT], po[:, :SEGT], AF.Identity, [pok_], ["osq", "po_rd"])
                    tt("dve", osq[:, :SEGT], osq[:, :SEGT], osq[:, :SEGT], ALU.mult, ["osq"], ["osq"])
                    stt("dve", uT[:, h, t0:t0 + SEGT], po[:, :SEGT], onwT[:, j, h:h + 1], szT[:, :SEGT], ALU.mult, ALU.mult,
                        [pok_, "onwT", szk, "po_rd"], ["uT"])
                    pss = pb[4][:, 256:264]
                    for c in range(NCH):
                        mm(pss[:CH, c:c + 1], osq[:, c * CH:(c + 1) * CH], ones1[:, 0:1], True, True,
                           ["osq", "ones1"], [("pb", 4)])
                    cg0 = seg * NCH
                    if h == 0:
                        cp("dve", ssacc[:CH, cg0:cg0 + NCH], pss[:CH, 0:NCH], [("pb", 4)], ["ssacc"])
                    else:
                        tt("dve", ssacc[:CH, cg0:cg0 + NCH], pss[:CH, 0:NCH], ssacc[:CH, cg0:cg0 + NCH], ALU.add,
                           [("pb", 4), "ssacc"], ["ssacc"])
                od = nsp_d if kind == "p" else nss_d
                dma("sp", od[j, si, h], S32[NCHT % 2][:], [("S32", NCHT % 2)], ())
            rstd_ops("dve", rstdo[:CH, :NCHT], ssacc[:CH, :NCHT], float(D), ["ssacc"], ["rstdo"])
            phase_outproj(hwo_d, j, T, True)

        def attn_layer(l, T, kind, si):
            j = l // 2
            TP = min(T, 128)
            NT = T // TP
            fence()
            mset("pool", vaug, 1.0, ["vaug"])
            for b_ in range(2):
                mset("pool", qTt[b_], 0.0, [("qTt", b_)])
            phase_norm(l, T)
            wv = awin_d[j].rearrange("(kc p) n -> p kc n", p=128)
            nk_d = nkp_d if kind == "p" else nks_d
            nv_d = nvp_d if kind == "p" else nvs_d
            ktile0 = 16 if kind == "s" else 0
            for hp in range(4):
                wb = wbuf_t
                wk = ("wbuf", 0)
                wk1 = ("wbuf", 1)
                cols = []
                for sec in range(2):
                    for hh_ in range(2):
                        for m_ in range(2):
                            cols.append((sec * 256 + hh_ * 128 + m_ * 64, sec * 1024 + m_ * 512 + (2 * hp + hh_) * 64, 64))
                cols += [(512, 2048 + hp * 256, 256), (768, 3072 + hp * 256, 256)]
                for (o, c0, n) in cols:
                    dma("pool", wb[:, :, o:o + n], wv[:, :, c0:c0 + n], (), [wk, wk1])
                if kind == "s":
                    ckv = ck_d[j, si].rearrange("(t p) n -> p t n", p=128)
                    cvv = cv_d[j, si].rearrange("(t p) n -> p t n", p=128)
                    for hh_ in range(2):
                        for m in range(2):
                            c0_ = m * 512 + (2 * hp + hh_) * 64
                            dma("pool", kctm[:, :, hh_, m * 64:(m + 1) * 64], ckv[:, :, c0_:c0_ + 64], (), ["kctm"])
                    for hh in range(2):
                        dma("pool", vaug[:, 0:16, hh, 0:128], cvv[:, :, hp * 256 + hh * 128:hp * 256 + (hh + 1) * 128],
                            (), ["vaug"])
                    for t in range(16):
                        ptb = pt[t % 2]
                        ptv = ptb[:, 0:256].rearrange("p (m k) -> p m k", m=2)
                        for m in range(2):
                            tr(ptv[:, m, :], kctm[:, t, m, :], identb[:, :], ["kctm", "identb"], [("pt", t % 2)])
                        cp("act" if t % 2 == 0 else "dve", kT[:, :, t * 128:(t + 1) * 128], ptv[:, :, :],
                           [("pt", t % 2)], ["kT"])
                for t in range(NT):
                    bank = pb[t % 2]
                    bk = ("pb", t % 2)
                    for kc in range(8):
                        mm(bank[:TP, :], hT[:, kc, t * TP:(t + 1) * TP], wb[:, kc, 256:768], kc == 0, kc == 7,
                           [wk, wk1, ("hT", t)], [bk])
                    qk_norm(bank, bk, TP, j, 1)
                    ko = kout[t % 2]
                    tt("pool", ko[:TP, :].rearrange("p (g d) -> p g d", g=4), kn[:TP, :].rearrange("p (g d) -> p g d", g=4),
                       qkn[:TP, j, 1, :].unsqueeze(1).broadcast_to([TP, 4, 64]), ALU.mult, ["kn", "qkn"], [("kout", t % 2)])
                    for hh_ in range(2):
                        for m in range(2):
                            c0_ = m * 512 + (2 * hp + hh_) * 64
                            dma("sp", nk_d[j, si, t * TP:(t + 1) * TP, c0_:c0_ + 64],
                                ko[:TP, hh_ * 128 + m * 64:hh_ * 128 + (m + 1) * 64], [("kout", t % 2)], ())
                    cp("act", qkb[:TP, :], ko[:TP, :], [("kout", t % 2)], ["qkb"])
                    ptv = pt[t % 2][:, 0:256].rearrange("p (m k) -> p m k", m=2)
                    for m in range(2):
                        tr(ptv[:, m, :TP], qkb[:TP, m * 128:(m + 1) * 128], identb[:TP, :TP], ["qkb", "identb"],
                           [("pt", t % 2)])
                    kt0 = ktile0 * 128 + t * TP
                    cp("dve", kT[:, :, kt0:kt0 + TP], ptv[:, :, :TP], [("pt", t % 2)], ["kT"])
                    vo = vout[t % 2]
                    cp("act", vo[:TP, :], bank[:TP, 256:512], [bk], [("vout", t % 2)])
                    dma("sp", nv_d[j, si, t * TP:(t + 1) * TP, hp * 256:(hp + 1) * 256], vo[:TP, :], [("vout", t % 2)], ())
                    cp("pool", vaug[:TP, ktile0 + t, :, 0:128], vo[:TP, :].rearrange("p (h e) -> p h e", h=2),
                       [("vout", t % 2)], ["vaug"])
                for jt in range(NT):
                    bank = pb[jt % 2]
                    bk = ("pb", jt % 2)
                    for kc in range(8):
                        mm(bank[:TP, 0:256], hT[:, kc, jt * TP:(jt + 1) * TP], wb[:, kc, 0:256], kc == 0, kc == 7,
                           [wk, wk1, ("hT", jt)], [bk])
                    for kc in range(8):
                        mm(bank[:TP, 256:512], hT[:, kc, jt * TP:(jt + 1) * TP], wb[:, kc, 768:1024], kc == 0, kc == 7,
                           [wk, wk1, ("hT", jt)], [bk])
                    qk_norm(bank, bk, TP, j, 0)
                    tt("pool", qkb[:TP, :].rearrange("p (g d) -> p g d", g=4), kn[:TP, :].rearrange("p (g d) -> p g d", g=4),
                       qkn[:TP, j, 0, :].unsqueeze(1).broadcast_to([TP, 4, 64]), ALU.mult, ["kn", "qkn"], ["qkb"])
                    ptv = pt[jt % 2][:, 0:256].rearrange("p (m k) -> p m k", m=2)
                    for m in range(2):
                        tr(ptv[:, m, :TP], qkb[:TP, m * 128:(m + 1) * 128], identb[:TP, :TP], ["qkb", "identb"],
                           [("pt", jt % 2)])
                    qT_ = qTt[jt % 2]
                    cp("dve", qT_[0:64, :, 0:TP], ptv[0:64, :, :TP], [("pt", jt % 2)], [("qTt", jt % 2)])
                    cp("dve", qT_[64:128, :, 128:128 + TP], ptv[64:128, :, :TP], [("pt", jt % 2)], [("qTt", jt % 2)])
                    sz_ = szt[jt % 2]
                    act(sgt[:TP, :], bank[:TP, 256:512], AF.Exp, [bk], ["sgt"], scale=-1.0)
                    act(sgt[:TP, :], sgt[:TP, :], AF.Ln, ["sgt"], ["sgt"], bias=1.0)
                    act(sgt[:TP, :], sgt[:TP, :], AF.Exp, ["sgt"], ["sgt"], scale=-1.0)
                    tt("dve", sz_[:TP, :], sgt[:TP, :], bank[:TP, 256:512], ALU.mult, ["sgt", bk], [("szt", jt % 2)])
                    for hh in range(2):
                        h = 2 * hp + hh
                        slope_h = _slopes()[h]
                        if kind == "p":
                            ktiles = [(i, 128, "off" if i < jt else "diag") for i in range(jt + 1)
                                      if i == jt or slope_h * ((jt - i - 1) * 128 + 1) <= SKIP_T]
                        else:
                            ktiles = [(i, 128, "off") for i in range(16)
                                      if slope_h * (PAST - (128 * i + 127)) <= SKIP_T] + [(16, 64, "diag")]
                        r0 = 64 * hh
                        pos = [pb[4 + m][:, hh * 129:(hh + 1) * 129] for m in range(2)]
                        poks = [("pb", 4), ("pb", 5)]
                        for idx, (i, nk, typ) in enumerate(ktiles):
                            sslot = uid[0] % 4
                            uid[0] += 1
                            psb_ = pb[2 + sslot % 2]
                            ps = psb_[:, 0:256].rearrange("p (m q) -> p m q", m=2)
                            psk = ("pb", 2 + sslot % 2)
                            mm(psb_[:nk, 0:256], kT[:, hh, i * 128:i * 128 + nk], qT_[:, hh, :],
                               True, True, ["kT", ("qTt", jt % 2)], [psk])
                            P_ = Pt[sslot]
                            pk = ("Pt", sslot)
                            if typ == "off":
                                bias = btab_p[:nk, h, jt - i:jt - i + 1] if kind == "p" else btab_s[:nk, h, i:i + 1]
                                act(P_[:nk, :, :TP], ps[:nk, :, :TP], AF.Exp, [psk, "btab_p", "btab_s"], [pk],
                                    bias=bias, scale=0.125)
                            else:
                                bm = bmat_p[:nk, h, :TP] if kind == "p" else bmat_s[:nk, h, :TP]
                                stt("dve", dtmp[:nk, :, :TP], ps[:nk, :, :TP], 0.125,
                                    bm.unsqueeze(1).broadcast_to([nk, 2, TP]), ALU.mult, ALU.add,
                                    [psk, "bmat_p", "bmat_s"], ["dtmp"])
                                act(P_[:nk, :, :TP], dtmp[:nk, :, :TP], AF.Exp, ["dtmp"], [pk])
                            for m in range(2):
                                mm(pos[m][:TP, :], P_[:nk, m, :TP], vaug[:nk, i, hh, :], idx == 0, idx == len(ktiles) - 1,
                                   [pk, "vaug"], [poks[m]])
                        for m in range(2):
                            S.op("dve", lambda e, _o=rr[:TP, m:m + 1], _i=pos[m][:TP, 128:129]: e.reciprocal(out=_o, in_=_i),
                                 [poks[m]], ["rr"])
                        tt("dve", rl[:TP, 0:1], rr[:TP, 1:2], neglam[:TP, j:j + 1], ALU.mult, ["rr", "neglam"], ["rl"])
                        tsc("dve", t1[:TP, :], pos[1][:TP, 0:128], rl[:TP, 0:1], None, ALU.mult, None, [poks[1], "rl"], ["t1"])
                        stt("dve", oo[:TP, :], pos[0][:TP, 0:128], rr[:TP, 0:1], t1[:TP, :], ALU.mult, ALU.add,
                            [poks[0], "rr", "t1"], ["oo"])
                        mset("pool", ssn[:, 0:1], 0.0, ["ssn"])
                        act(junk[:TP, 0:128], oo[:TP, :], AF.Square, ["oo", "ssn"], ["junk", "ssn"], accum=ssn[:TP, 0:1])
                        rstd_ops("dve", rsn[:TP, 0:1], ssn[:TP, 0:1], 128.0, ["ssn"], ["rsn"])
                        stt("dve", ug[:TP, :], oo[:TP, :], rsn[:TP, 0:1], wsub[:TP, j, :], ALU.mult, ALU.mult,
                            ["oo", "rsn", "wsub"], ["ug"])
                        tt("pool", ub[:TP, :], ug[:TP, :], sz_[:TP, hh * 128:(hh + 1) * 128], ALU.mult,
                           ["ug", ("szt", jt % 2)], ["ub"])
                        us = uid[0] % 2
                        ptu = pt[us][:, 512:640]
                        tr(ptu[:, :TP], ub[:TP, :], identb[:TP, :TP], ["ub", "identb"], [("pt", us)])
                        cp("act", uT[:, h, jt * TP:(jt + 1) * TP], ptu[:, :TP], [("pt", us)], ["uT"])
            phase_outproj(awo_d, j, T, False)

        def qk_norm(bank, bk, TP, j, which):
            act(sq[:TP, :], bank[:TP, 0:256], AF.Identity, [bk], ["sq"])
            tt("dve", sq[:TP, :], sq[:TP, :], sq[:TP, :], ALU.mult, ["sq"], ["sq"])
            S.op("dve", lambda e, _o=ssq4[:TP, 0:4], _i=sq[:TP, :].rearrange("p (g d) -> p g d", g=4):
                 e.tensor_reduce(out=_o, in_=_i, axis=AX.X, op=ALU.add), ["sq"], ["ssq4"])
            rstd_ops("dve", rs4[:TP, 0:4], ssq4[:TP, 0:4], 64.0, ["ssq4"], ["rs4"])
            tt("dve", kn[:TP, :].rearrange("p (g d) -> p g d", g=4), bank[:TP, 0:256].rearrange("p (g d) -> p g d", g=4),
               rs4[:TP, 0:4].unsqueeze(2).broadcast_to([TP, 4, 64]), ALU.mult, [bk, "rs4"], ["kn"])

        seqs = [("p", i) for i in range(NP)] + [("s", i) for i in range(NS)]
        for kind, si in seqs:
            T = SEQ if kind == "p" else DEC_SEQ
            TP = min(T, 128)
            NT = T // TP
            xd = (xp_d if kind == "p" else xs_d)[si]
            yd = (yp_d if kind == "p" else ys_d)[si]
            for t in range(NT):
                dma("sp", xres[:TP, t, :], xd[t * TP:(t + 1) * TP, :], (), [("x", t)])
            for l in range(NL):
                if l % 2 == 0:
                    hgrn_layer(l, T, kind, si)
                else:
                    attn_layer(l, T, kind, si)
            for t in range(NT):
                dma("sp", yd[t * TP:(t + 1) * TP, :], xres[:TP, t, :], [("x", t)], ())
        S.emit(nc)
    return nc, len(S.ins)


_PROG_CACHE = {}


def _get_prog(NP, NS, NL=4):
    key = (NP, NS, NL)
    if key not in _PROG_CACHE:
        _PROG_CACHE[key] = build_program(NP, NS, NL)
    return _PROG_CACHE[key]


def make_in_maps(inputs, NP, NS, ncores):
    f = lambda a: np.ascontiguousarray(np.asarray(a, dtype=np.float32))
    c = _const_tables()
    shared = dict(
        normwT=f(np.asarray(inputs["norm_w"]).reshape(4, 8, 128).transpose(2, 0, 1)),
        hwin=f(inputs["hgrn_w_in"]),
        lbT=f(np.asarray(inputs["hgrn_lb_logits"]).reshape(2, 8, 128).transpose(2, 0, 1)),
        onwT=f(np.asarray(inputs["hgrn_onorm_w"]).reshape(2, 8, 128).transpose(2, 0, 1)),
        hwo=f(inputs["hgrn_w_out"]),
        awin=f(inputs["attn_w_in"]),
        qkn=f(np.broadcast_to(np.stack([np.asarray(inputs["attn_q_norm"]), np.asarray(inputs["attn_k_norm"])], axis=1)[None],
                              (128, 2, 2, 64))),
        lam=f(np.broadcast_to(np.asarray(inputs["attn_lambda"])[None], (128, 2, 4, 64))),
        sub=f(np.broadcast_to(np.asarray(inputs["attn_subln"])[None], (128, 2, 128))),
        awo=f(inputs["attn_w_out"]),
        btab_p=c["btab_p"], bmat_p=c["bmat_p"], btab_s=c["btab_s"], bmat_s=c["bmat_s"],
        tri=c["tri"], scanmask=c["scanmask"], ident=c["ident"],
    )
    xp = np.asarray(inputs["x_prompt"])
    xs = np.asarray(inputs["x_sample"])
    ck = np.asarray(inputs["cache_k"]).reshape(2, -1, PAST, D)
    cv = np.asarray(inputs["cache_v"]).reshape(2, -1, PAST, D)
    st = np.asarray(inputs["state_hgrn"])
    maps = []
    for c_ in range(ncores):
        m = dict(shared)
        m["xp"] = f(xp[c_ * NP:(c_ + 1) * NP]) if NP > 0 else np.zeros((1, SEQ, D), np.float32)
        if NS > 0:
            m["xs"] = f(xs[c_ * NS:(c_ + 1) * NS])
            m["ck"] = f(ck[:, c_ * NS:(c_ + 1) * NS])
            m["cv"] = f(cv[:, c_ * NS:(c_ + 1) * NS])
            m["st"] = f(st[:, c_ * NS:(c_ + 1) * NS])
        else:
            m["xs"] = np.zeros((1, DEC_SEQ, D), np.float32)
            m["ck"] = np.zeros((2, 1, PAST, D), np.float32)
            m["cv"] = np.zeros((2, 1, PAST, D), np.float32)
            m["st"] = np.zeros((2, 1, 8, 128, 128), np.float32)
        maps.append(m)
    return maps


def kernel(x_prompt, x_sample, cache_k, cache_v, state_hgrn, norm_w, hgrn_w_in, hgrn_lb_logits,
           hgrn_onorm_w, hgrn_w_out, attn_w_in, attn_q_norm, attn_k_norm, attn_lambda, attn_subln,
           attn_w_out):
    inputs = dict(x_prompt=x_prompt, x_sample=x_sample, cache_k=cache_k, cache_v=cache_v,
                  state_hgrn=state_hgrn, norm_w=norm_w, hgrn_w_in=hgrn_w_in,
                  hgrn_lb_logits=hgrn_lb_logits, hgrn_onorm_w=hgrn_onorm_w, hgrn_w_out=hgrn_w_out,
                  attn_w_in=attn_w_in, attn_q_norm=attn_q_norm, attn_k_norm=attn_k_norm,
                  attn_lambda=attn_lambda, attn_subln=attn_subln, attn_w_out=attn_w_out)
    B = np.asarray(x_prompt).shape[0]
    Bs = np.asarray(x_sample).shape[0]
    NP = B // NCORES
    NS = Bs // NCORES
    nc, _ = _get_prog(NP, NS)
    maps = make_in_maps(inputs, NP, NS, NCORES)
    res = run_bass_kernel_spmd(nc, maps, core_ids=list(range(NCORES)))
    R = res.results
    cat = lambda name, ax: np.concatenate([np.asarray(r[name]) for r in R], axis=ax)
    yp = cat("yp", 0)
    ys = cat("ys", 0)
    nkp = cat("nkp", 1).reshape(2, B, SEQ, 2, 8, 64)
    nvp = cat("nvp", 1).reshape(2, B, SEQ, 8, 128)
    nks = cat("nks", 1).reshape(2, Bs, DEC_SEQ, 2, 8, 64)
    nvs = cat("nvs", 1).reshape(2, Bs, DEC_SEQ, 8, 128)
    nsp = cat("nsp", 1)
    nss = cat("nss", 1)
    return (yp, ys, nkp, nvp, nks, nvs, nsp, nss)
```
